# Optimizing a Trainium2 kernel written in Bass

```python
import math, functools
import jax, jax.numpy as jnp
from jax import lax
import numpy as np

D_MODEL = 1024
BATCH = 4
SEQ = 4096
DEPTH = 1
DEC_BATCH = 128
DEC_SEQ = 4
PAST_LEN = 8192
PAGE_SIZE = 128

DN_DK = 128
DN_DV = 128
DN_HEADS = D_MODEL // DN_DV
CONV_W = 4
DN_CHUNK = 64
SW_HEAD_DIM = 64
SW_HEADS = D_MODEL // SW_HEAD_DIM
SW_KV_HEADS = SW_HEADS // 4
SW_GROUP = SW_HEADS // SW_KV_HEADS
WINDOW = 128
ROT_DIM = SW_HEAD_DIM // 4
ROPE_THETA = 500000.0
D_FF = ((8 * D_MODEL // 3 + 63) // 64) * 64
N_SUB = 3
EPS = 1e-6
NEG_INF = -1e30

DN_QK = DN_HEADS * DN_DK
DN_V = DN_HEADS * DN_DV
CONV_CH = 2 * DN_QK + DN_V
SW_Q = SW_HEADS * SW_HEAD_DIM
SW_KV = SW_KV_HEADS * SW_HEAD_DIM
IN_SIZES = (CONV_CH, DN_V, DN_HEADS, DN_HEADS, SW_Q, SW_KV, SW_KV, D_MODEL, D_MODEL)
D_IN = sum(IN_SIZES)

kernel_name = 'hybrid_deltanet_swa_macaron_step'


def rmsnorm(x, gain):
    xf = x.astype(jnp.float32)
    y = xf * lax.rsqrt(jnp.mean(xf * xf, axis=-1, keepdims=True) + EPS)
    return (y * gain.astype(jnp.float32)).astype(x.dtype)


def l2norm(x):
    xf = x.astype(jnp.float32)
    return xf * lax.rsqrt(jnp.sum(xf * xf, axis=-1, keepdims=True) + EPS)


def swiglu(h, w_gate, w_up, w_down):
    return (jax.nn.silu(h @ w_gate) * (h @ w_up)) @ w_down


def short_conv(u, buf, w):
    T = u.shape[1]
    full = jnp.concatenate([buf.astype(u.dtype), u], axis=1)
    out = sum(full[:, i:i + T] * w[i] for i in range(CONV_W))
    return jax.nn.silu(out), full[:, T:]


def rope_partial(x, pos):
    half = ROT_DIM // 2
    inv_freq = ROPE_THETA ** (-jnp.arange(half, dtype=jnp.float32) * (2.0 / ROT_DIM))
    ang = pos.astype(jnp.float32)[:, None] * inv_freq[None, :]
    cos = jnp.cos(ang)[None, :, None, :]
    sin = jnp.sin(ang)[None, :, None, :]
    xr = x[..., :ROT_DIM].astype(jnp.float32)
    x1, x2 = xr[..., :half], xr[..., half:]
    rot = jnp.concatenate([x1 * cos - x2 * sin, x2 * cos + x1 * sin], axis=-1)
    return jnp.concatenate([rot.astype(x.dtype), x[..., ROT_DIM:]], axis=-1)


def gated_delta_rule(q, k, v, g, beta, S0):
    B, T, H, DK = q.shape
    DV = v.shape[-1]
    C = math.gcd(T, DN_CHUNK)
    N = T // C
    f32 = jnp.float32

    def blocks(a):
        return a.astype(f32).reshape(B, N, C, H, -1).transpose(0, 1, 3, 2, 4)

    qc, kc, vc = blocks(q), blocks(k), blocks(v)
    gc = jnp.cumsum(g.astype(f32).reshape(B, N, C, H).transpose(0, 1, 3, 2), axis=-1)
    bc = beta.astype(f32).reshape(B, N, C, H).transpose(0, 1, 3, 2)[..., None]
    incl = jnp.tril(jnp.ones((C, C), dtype=bool))
    strict = jnp.tril(jnp.ones((C, C), dtype=bool), -1)
    diff = gc[..., :, None] - gc[..., None, :]
    decay = jnp.where(incl, jnp.exp(jnp.where(incl, diff, 0.0)), 0.0)
    kb = kc * bc
    lower = jnp.where(strict, jnp.einsum('bnhid,bnhjd->bnhij', kb, kc) * decay, 0.0)
    rhs = jnp.concatenate([vc * bc, kb * jnp.exp(gc)[..., None]], axis=-1)
    sol = lax.linalg.triangular_solve(jnp.eye(C, dtype=f32) + lower, rhs,
                                      left_side=True, lower=True, unit_diagonal=True)
    u, w = sol[..., :DV], sol[..., DV:]
    a_intra = jnp.where(incl, jnp.einsum('bnhid,bnhjd->bnhij', qc, kc) * decay, 0.0)

    def step(S, xs):
        qn, kn, un, wn, gn, an = xs
        v_new = un - jnp.einsum('bhcd,bhde->bhce', wn, S)
        o = (jnp.einsum('bhcd,bhde->bhce', qn * jnp.exp(gn)[..., None], S)
             + jnp.einsum('bhij,bhje->bhie', an, v_new))
        g_last = gn[..., -1:]
        k_dec = kn * jnp.exp(g_last - gn)[..., None]
        S = S * jnp.exp(g_last)[..., None] + jnp.einsum('bhcd,bhce->bhde', k_dec, v_new)
        return S, o

    xs = tuple(jnp.moveaxis(a, 1, 0) for a in (qc, kc, u, w, gc, a_intra))
    S, o = lax.scan(step, S0.astype(f32), xs)
    o = o.transpose(1, 0, 3, 2, 4).reshape(B, T, H, DV)
    return o.astype(v.dtype), S.astype(S0.dtype)


def sink_softmax(scores, mask, sinks):
    s = jnp.where(mask, scores, NEG_INF)
    sink = jnp.broadcast_to(sinks.astype(jnp.float32)[:, :, None, None], s.shape[:-1] + (1,))
    p = jax.nn.softmax(jnp.concatenate([s, sink], axis=-1), axis=-1)
    return p[..., :-1]


def swa_banded(q, k, v, sinks):
    B, T = q.shape[:2]
    W = WINDOW
    NB = T // W
    qb = q.reshape(B, NB, W, SW_KV_HEADS, SW_GROUP, SW_HEAD_DIM)
    kb = k.reshape(B, NB, W, SW_KV_HEADS, SW_HEAD_DIM)
    vb = v.reshape(B, NB, W, SW_KV_HEADS, SW_HEAD_DIM)
    pad = ((0, 0), (1, 0), (0, 0), (0, 0), (0, 0))
    kk = jnp.concatenate([jnp.pad(kb, pad)[:, :NB], kb], axis=2)
    vv = jnp.concatenate([jnp.pad(vb, pad)[:, :NB], vb], axis=2)
    qi = jnp.arange(W)[:, None] + W
    sj = jnp.arange(2 * W)[None, :]
    d = qi - sj
    band = (d >= 0) & (d <= WINDOW)
    valid = (jnp.arange(NB) > 0)[:, None, None] | (sj >= W)[None]
    mask = (band[None] & valid)[None, :, None, None]
    scores = jnp.einsum('bnqkgd,bnskd->bnkgqs', qb, kk).astype(jnp.float32) * (SW_HEAD_DIM ** -0.5)
    p = sink_softmax(scores, mask, sinks.reshape(SW_KV_HEADS, SW_GROUP))
    o = jnp.einsum('bnkgqs,bnskd->bnqkgd', p.astype(vv.dtype), vv).reshape(B, T, SW_Q)
    nb = min(WINDOW, T)
    return o, k[:, -nb:], v[:, -nb:]


def swa_buffered(kbuf, vbuf, q, k, v, sinks):
    B, T = q.shape[:2]
    Wb = kbuf.shape[1]
    kk = jnp.concatenate([kbuf.astype(k.dtype), k], axis=1)
    vv = jnp.concatenate([vbuf.astype(v.dtype), v], axis=1)
    qg = q.reshape(B, T, SW_KV_HEADS, SW_GROUP, SW_HEAD_DIM)
    d = (jnp.arange(T)[:, None] + Wb) - jnp.arange(Wb + T)[None, :]
    mask = ((d >= 0) & (d <= WINDOW))[None, None, None]
    scores = jnp.einsum('bqkgd,bskd->bkgqs', qg, kk).astype(jnp.float32) * (SW_HEAD_DIM ** -0.5)
    p = sink_softmax(scores, mask, sinks.reshape(SW_KV_HEADS, SW_GROUP))
    o = jnp.einsum('bkgqs,bskd->bqkgd', p.astype(vv.dtype), vv).reshape(B, T, SW_Q)
    return o, kk[:, -Wb:], vv[:, -Wb:]


def token_mixer(h, pos, conv_buf, S0, attend, p):
    B, T = h.shape[:2]
    proj = h @ p['w_in']
    split_idx = np.cumsum(IN_SIZES)[:-1].tolist()
    u, z, b_raw, a_raw, q_sw, k_sw, v_sw, gate_a, gate_b = jnp.split(proj, split_idx, axis=-1)
    u, conv_new = short_conv(u, conv_buf, p['conv_w'])
    q_dn, k_dn, v_dn = jnp.split(u, [DN_QK, 2 * DN_QK], axis=-1)
    q_dn = l2norm(q_dn.reshape(B, T, DN_HEADS, DN_DK)) * (DN_DK ** -0.5)
    k_dn = l2norm(k_dn.reshape(B, T, DN_HEADS, DN_DK))
    v_dn = v_dn.reshape(B, T, DN_HEADS, DN_DV)
    beta = jax.nn.sigmoid(b_raw.astype(jnp.float32))
    g = -jnp.exp(p['a_log'].astype(jnp.float32)) * jax.nn.softplus(
        a_raw.astype(jnp.float32) + p['dt_bias'].astype(jnp.float32))
    o_dn, S_new = gated_delta_rule(q_dn, k_dn, v_dn, g, beta, S0)
    o_dn = (rmsnorm(o_dn, p['dn_norm']) * jax.nn.silu(z.reshape(B, T, DN_HEADS, DN_DV))).reshape(B, T, DN_V)
    q = rope_partial(q_sw.reshape(B, T, SW_HEADS, SW_HEAD_DIM), pos)
    k = rope_partial(k_sw.reshape(B, T, SW_KV_HEADS, SW_HEAD_DIM), pos)
    v = v_sw.reshape(B, T, SW_KV_HEADS, SW_HEAD_DIM)
    o_sw, kbuf_new, vbuf_new = attend(q, k, v, p['sinks'])
    y = jax.nn.sigmoid(gate_a) * o_dn + jax.nn.sigmoid(gate_b) * o_sw
    return y @ p['w_out'], (kbuf_new, vbuf_new, conv_new, S_new)


def decoder_layer(x, c, pos, conv_buf, S0, attend, p):
    mod = jax.nn.silu(c) @ p['w_ada'] + p['b_ada']
    sh1, sc1, g1, sh2, sc2, g2, sh3, sc3, g3 = [m[:, None, :] for m in jnp.split(mod, 3 * N_SUB, axis=-1)]
    h = rmsnorm(x, p['ffn1_norm_pre']) * (1 + sc1) + sh1
    x = x + 0.5 * g1 * rmsnorm(swiglu(h, p['ffn1_w_gate'], p['ffn1_w_up'], p['ffn1_w_down']), p['ffn1_norm_post'])
    h = rmsnorm(x, p['mix_norm_pre']) * (1 + sc2) + sh2
    y, state = token_mixer(h, pos, conv_buf, S0, attend, p)
    x = x + g2 * rmsnorm(y, p['mix_norm_post'])
    h = rmsnorm(x, p['ffn2_norm_pre']) * (1 + sc3) + sh3
    x = x + 0.5 * g3 * rmsnorm(swiglu(h, p['ffn2_w_gate'], p['ffn2_w_up'], p['ffn2_w_down']), p['ffn2_norm_post'])
    return x, state


def setup_inputs(seed: int = 0) -> dict:
    key = jax.random.key(seed)
    ks = jax.random.split(key, 40)
    f32 = jnp.float32

    def nrm(k, shape, s=1.0):
        return s * jax.random.normal(k, shape, f32)

    def gain(k, n=D_MODEL):
        return 1.0 + nrm(k, (DEPTH, n), 0.05)

    wb = min(WINDOW, PAST_LEN)
    dt = jnp.exp(jax.random.uniform(ks[30], (DEPTH, DN_HEADS), f32, math.log(1e-3), math.log(1e-1)))
    return {
        'x_prompt': nrm(ks[0], (BATCH, SEQ, D_MODEL)),
        'x_sample': nrm(ks[1], (DEC_BATCH, DEC_SEQ, D_MODEL)),
        'cache_swa_k': nrm(ks[2], (DEPTH, DEC_BATCH, wb, SW_KV_HEADS, SW_HEAD_DIM)),
        'cache_swa_v': nrm(ks[3], (DEPTH, DEC_BATCH, wb, SW_KV_HEADS, SW_HEAD_DIM)),
        'state_conv': nrm(ks[4], (DEPTH, DEC_BATCH, CONV_W - 1, CONV_CH)),
        'state_delta': nrm(ks[5], (DEPTH, DEC_BATCH, DN_HEADS, DN_DK, DN_DV), DN_DK ** -0.5),
        'c_prompt': nrm(ks[6], (BATCH, D_MODEL)),
        'c_sample': nrm(ks[7], (DEC_BATCH, D_MODEL)),
        'w_ada': nrm(ks[8], (DEPTH, D_MODEL, 3 * N_SUB * D_MODEL), 0.5 * D_MODEL ** -0.5),
        'b_ada': nrm(ks[9], (DEPTH, 3 * N_SUB * D_MODEL), 0.02),
        'ffn1_norm_pre': gain(ks[10]),
        'ffn1_norm_post': gain(ks[11]),
        'ffn1_w_gate': nrm(ks[12], (DEPTH, D_MODEL, D_FF), D_MODEL ** -0.5),
        'ffn1_w_up': nrm(ks[13], (DEPTH, D_MODEL, D_FF), D_MODEL ** -0.5),
        'ffn1_w_down': nrm(ks[14], (DEPTH, D_FF, D_MODEL), D_FF ** -0.5),
        'mix_norm_pre': gain(ks[15]),
        'mix_norm_post': gain(ks[16]),
        'w_in': nrm(ks[17], (DEPTH, D_MODEL, D_IN), D_MODEL ** -0.5),
        'conv_w': nrm(ks[18], (DEPTH, CONV_W, CONV_CH), CONV_W ** -0.5),
        'a_log': jnp.log(jax.random.uniform(ks[19], (DEPTH, DN_HEADS), f32, 1.0, 16.0)),
        'dt_bias': dt + jnp.log(-jnp.expm1(-dt)),
        'dn_norm': gain(ks[20], DN_DV),
        'sinks': nrm(ks[21], (DEPTH, SW_HEADS), 0.5),
        'w_out': nrm(ks[22], (DEPTH, D_MODEL, D_MODEL), D_MODEL ** -0.5),
        'ffn2_norm_pre': gain(ks[23]),
        'ffn2_norm_post': gain(ks[24]),
        'ffn2_w_gate': nrm(ks[25], (DEPTH, D_MODEL, D_FF), D_MODEL ** -0.5),
        'ffn2_w_up': nrm(ks[26], (DEPTH, D_MODEL, D_FF), D_MODEL ** -0.5),
        'ffn2_w_down': nrm(ks[27], (DEPTH, D_FF, D_MODEL), D_FF ** -0.5),
    }


def reference(x_prompt, x_sample, cache_swa_k, cache_swa_v, state_conv, state_delta,
              c_prompt, c_sample, w_ada, b_ada, ffn1_norm_pre, ffn1_norm_post,
              ffn1_w_gate, ffn1_w_up, ffn1_w_down, mix_norm_pre, mix_norm_post,
              w_in, conv_w, a_log, dt_bias, dn_norm, sinks, w_out,
              ffn2_norm_pre, ffn2_norm_post, ffn2_w_gate, ffn2_w_up, ffn2_w_down):
    B = x_prompt.shape[0]
    pos_p = jnp.arange(x_prompt.shape[1])
    pos_s = PAST_LEN + jnp.arange(x_sample.shape[1])
    y_p, y_s = x_prompt, x_sample
    kp, vp, cp, sp = [], [], [], []
    ksm, vsm, csm, ssm = [], [], [], []
    for l in range(DEPTH):
        p = dict(w_ada=w_ada[l], b_ada=b_ada[l],
                 ffn1_norm_pre=ffn1_norm_pre[l], ffn1_norm_post=ffn1_norm_post[l],
                 ffn1_w_gate=ffn1_w_gate[l], ffn1_w_up=ffn1_w_up[l], ffn1_w_down=ffn1_w_down[l],
                 mix_norm_pre=mix_norm_pre[l], mix_norm_post=mix_norm_post[l],
                 w_in=w_in[l], conv_w=conv_w[l], a_log=a_log[l], dt_bias=dt_bias[l],
                 dn_norm=dn_norm[l], sinks=sinks[l], w_out=w_out[l],
                 ffn2_norm_pre=ffn2_norm_pre[l], ffn2_norm_post=ffn2_norm_post[l],
                 ffn2_w_gate=ffn2_w_gate[l], ffn2_w_up=ffn2_w_up[l], ffn2_w_down=ffn2_w_down[l])
        conv0 = jnp.zeros((B, CONV_W - 1, CONV_CH), x_prompt.dtype)
        S0 = jnp.zeros((B, DN_HEADS, DN_DK, DN_DV), state_delta.dtype)
        y_p, (k1, v1, c1, s1) = decoder_layer(y_p, c_prompt, pos_p, conv0, S0, swa_banded, p)
        attend_s = functools.partial(swa_buffered, cache_swa_k[l], cache_swa_v[l])
        y_s, (k2, v2, c2, s2) = decoder_layer(y_s, c_sample, pos_s, state_conv[l], state_delta[l], attend_s, p)
        kp.append(k1); vp.append(v1); cp.append(c1); sp.append(s1)
        ksm.append(k2); vsm.append(v2); csm.append(c2); ssm.append(s2)
    swa_k_prompt, swa_v_prompt = jnp.stack(kp), jnp.stack(vp)
    conv_prompt, delta_prompt = jnp.stack(cp), jnp.stack(sp)
    swa_k_sample, swa_v_sample = jnp.stack(ksm), jnp.stack(vsm)
    conv_sample, delta_sample = jnp.stack(csm), jnp.stack(ssm)
    return (y_p, y_s, swa_k_prompt, swa_v_prompt, conv_prompt, delta_prompt,
            swa_k_sample, swa_v_sample, conv_sample, delta_sample)
```

```python
import numpy as np
from contextlib import ExitStack
import concourse.bass as bass
import concourse.mybir as mybir
from concourse.bass_utils import run_bass_kernel_spmd

F32 = mybir.dt.float32
BF16 = mybir.dt.bfloat16
I32 = mybir.dt.int32
AF = mybir.ActivationFunctionType
ALU = mybir.AluOpType
AX = mybir.AxisListType

ENGS = ["pe", "act", "dve", "pool", "sp"]
FUSE_WAIT = True
SMALL_KEYS = {"gsc", "b_tok", "nb_tok", "g_tok", "ba_sb", "Gcol", "Glast", "eG", "beG", "kdc", "eGlast", "mx", "nm", "esk", "den", "rden",
              "dacc", "rope", "invf", "posf", "padm", "rt", "kf32", "Xg", "hist", "hist4", "convout", "convout4", "nexpA", "k_pad"}
BF16_SCRATCH = True


class Ev:
    __slots__ = ("op", "sem", "val")

    def __init__(self, op=None, sem=None, val=None):
        self.op = op
        self.sem = sem
        self.val = val


class Op:
    __slots__ = ("fn", "eng", "deps", "inc", "cnt", "dma", "ev")


class DmaSem:
    def __init__(self, handle):
        self.h = handle
        self.count = 0


class Prog:
    def __init__(self, nc):
        self.nc = nc
        self.streams = {e: [] for e in ENGS}
        self.res = {}
        self.final_deps = []
        self.relaxed = False

    def _state(self, k):
        st = self.res.get(k)
        if st is None:
            st = [None, {}]
            self.res[k] = st
        return st

    def op(self, eng, fn, reads=(), writes=(), dma=None, output=False, fast=False):
        pbr = [k for k in reads if isinstance(k, str) and k.startswith("pb")]
        if pbr:
            reads = [k for k in reads if k not in pbr]
            writes = list(writes) + pbr
        o = Op()
        o.fn = fn
        o.eng = eng
        o.inc = False
        o.cnt = None
        o.dma = dma
        if dma is not None:
            dma.count += 16
            o.ev = Ev(op=None, sem=dma.h, val=dma.count)
        else:
            o.ev = Ev(op=o)
        deps = []
        for k in reads:
            st = self._state(k)
            if st[0] is not None:
                deps.append((st[0], True))
        for k in writes:
            st = self._state(k)
            if st[0] is not None:
                deps.append((st[0], False))
            deps.extend((x, False) for x in st[1].values())
        fd = []
        seen = set()
        for d, raw in deps:
            if id(d) in seen:
                continue
            if d.op is not None and d.op.eng == eng:
                if not (raw and not fast and eng in ("act", "dve", "pool")):
                    continue
                if self.relaxed and not any(k in SMALL_KEYS for k in reads if self.res.get(k) and self.res[k][0] is d):
                    continue
            seen.add(id(d))
            if dma is not None and d.op is None and d.sem is dma.h:
                continue
            fd.append(d)
            if d.op is not None:
                d.op.inc = True
        o.deps = fd
        rk = ("d", id(dma.h)) if dma is not None else ("e", eng)
        for k in reads:
            self._state(k)[1][rk] = o.ev
        for k in writes:
            st = self._state(k)
            st[0] = o.ev
            st[1] = {}
        self.streams[eng].append(o)
        if output:
            self.final_deps.append(o.ev)
        return o

    def finalize(self, block, eng_sems):
        fin = Op()
        fin.fn = None
        fin.eng = "sp"
        fin.inc = False
        fin.cnt = None
        fin.dma = None
        fin.ev = Ev(op=fin)
        fin.deps = list(self.final_deps)
        for d in fin.deps:
            if d.op is not None:
                d.op.inc = True
        self.streams["sp"].append(fin)
        for e in ENGS:
            c = 0
            for o in self.streams[e]:
                if o.inc and o.dma is None:
                    c += 1
                    o.cnt = c
        self.n_waits = 0
        self.n_ops = sum(len(s) for s in self.streams.values())
        self.max_cnt = {e: max([o.cnt or 0 for o in self.streams[e]] + [0]) for e in ENGS}
        print("max sem counts", self.max_cnt, flush=True)

        def replay(ename, eng):
            waited = {}
            mysem = eng_sems[ename]
            for o in self.streams[ename]:
                pend = []
                for d in o.deps:
                    if d.op is not None:
                        sem = eng_sems[d.op.eng]
                        val = d.op.cnt
                    else:
                        sem = d.sem
                        val = d.val
                    key = id(sem)
                    if waited.get(key, 0) >= val:
                        continue
                    waited[key] = val
                    pend.append((sem, val))
                    self.n_waits += 1
                best = {}
                for sem, val in pend:
                    if id(sem) not in best or best[id(sem)][1] < val:
                        best[id(sem)] = (sem, val)
                pend = list(best.values())
                fuse = None
                if o.fn is not None and pend and FUSE_WAIT:
                    fuse = pend.pop()
                for sem, val in pend:
                    eng.wait_ge(sem, val)
                if o.fn is None:
                    continue
                ins = o.fn(eng)
                if fuse is not None:
                    ins._wait_ge(fuse[0], fuse[1])
                if o.dma is not None:
                    ins.then_inc(o.dma.h, 16)
                elif o.inc:
                    ins.then_inc(mysem, 1)

        @block.tensor
        def _(eng):
            replay("pe", eng)

        @block.scalar
        def _(eng):
            replay("act", eng)

        @block.vector
        def _(eng):
            replay("dve", eng)

        @block.gpsimd
        def _(eng):
            replay("pool", eng)

        @block.sync
        def _(eng):
            replay("sp", eng)


D = 1024
DFF = 2752
NFF = 22
DIN = 7696
NT = 512
NS = 16
TS = 4
EPS = 1e-6
OFF_Z = 3072
OFF_B = 4096
OFF_QSW = 4112
OFF_KV = 5136
OFF_GA = 5648
OFF_GB = 6672
PAST = 8192
INV_FREQ = (np.float32(500000.0) ** (-np.arange(8, dtype=np.float32) * np.float32(2.0 / 16))).astype(np.float32)
TWO_PI = 2.0 * np.pi
CW1 = 6.28125
CW2 = TWO_PI - CW1

WEIGHT_NAMES = [
    ("w_ada", [D, 9 * D]), ("b_ada", [9 * D]),
    ("n1pre", [D]), ("n1post", [D]), ("wg1", [D, DFF]), ("wu1", [D, DFF]), ("wd1", [DFF, D]),
    ("n2pre", [D]), ("n2post", [D]), ("w_in", [D, DIN]), ("conv_w", [4, 3072]),
    ("a_log", [8]), ("dt_bias", [8]), ("dn_norm", [128]), ("sinks", [16]), ("w_out", [D, D]),
    ("n3pre", [D]), ("n3post", [D]), ("wg2", [D, DFF]), ("wu2", [D, DFF]), ("wd2", [DFF, D]),
]


def bc_last(ap, n):
    a = [list(x) for x in ap.ap]
    return bass.AP(ap.tensor, ap.offset, a + [[0, n]])


def bc_mid(ap, n):
    a = [list(x) for x in ap.ap]
    return bass.AP(ap.tensor, ap.offset, [a[0], [0, n]] + a[1:])


class KB:
    def __init__(self, NPT=8, sample=True, dbg=(), stage=99):
        self.stage_lim = stage
        self.NPT = NPT
        self.sample = sample
        self.dbg = set(dbg)
        self.nc = bass.Bass("TRN2", target_bir_lowering=False)
        self.es = ExitStack()
        self.P = Prog(self.nc)
        self.dmasems = {}
        self.bank_rr = 0
        self.outs = {}

    def sb(self, name, shape, dt):
        return self.es.enter_context(self.nc.sbuf_tensor(name, shape, dt))

    def sem(self, name):
        return self.es.enter_context(self.nc.semaphore(name))

    def dsem(self, name):
        if name not in self.dmasems:
            self.dmasems[name] = DmaSem(self.sem("d_" + name))
        return self.dmasems[name]

    def din(self, name, shape, dt=F32):
        return self.nc.dram_tensor(name, list(shape), dt, kind="ExternalInput").ap()

    def dout(self, name, shape, dt=F32):
        ap = self.nc.dram_tensor(name, list(shape), dt, kind="ExternalOutput").ap()
        self.outs[name] = ap
        return ap

    def dump(self, name, ap, keys, dt=F32):
        if name not in DEBUG:
            return
        shape = list(ap.shape)
        o = self.nc.dram_tensor("dbg_" + name, shape, dt, kind="ExternalOutput").ap()
        self.dma("sp", o, ap, r=list(keys), sem="dbg_" + name, output=True)

    def pe(self, fn, r=(), w=()):
        return self.P.op("pe", fn, r, w)

    def act(self, fn, r=(), w=(), fast=False):
        return self.P.op("act", fn, r, w, fast=fast)

    def dve(self, fn, r=(), w=(), fast=False):
        return self.P.op("dve", fn, r, w, fast=fast)

    def pool(self, fn, r=(), w=()):
        return self.P.op("pool", fn, r, w)

    def dma(self, q, out, in_, r=(), w=(), sem=None, output=False, nc_ok=False):
        ds = self.dsem(sem)
        if nc_ok:
            fn = lambda e: e.dma_start(out=out, in_=in_, allow_slow_non_contiguous=True)
        else:
            fn = lambda e: e.dma_start(out=out, in_=in_)
        return self.P.op(q, fn, r, w, dma=ds, output=output)

    def bank(self):
        i = self.bank_rr
        self.bank_rr = (self.bank_rr + 1) % 6
        return i

    def bk(self, i):
        t = self.pbig[i // 2]
        return t[:, (i % 2) * 512:(i % 2) * 512 + 512]

    def bkb(self, i):
        t = self.pbigb[i // 2]
        return t[:, (i % 2) * 1024:(i % 2) * 1024 + 1024]

    @staticmethod
    def bkey(i):
        return "pb%d" % i

    def build(self):
        nc = self.nc
        NPT = self.NPT
        T = NPT * NT
        NB = NPT * 4
        self.NB = NB
        self.xp = self.din("xp", [T, D])
        self.cp = self.din("cp", [1, D])
        self.W = {n: self.din(n, s) for n, s in WEIGHT_NAMES}
        self.WB = {}
        if BF16_SCRATCH:
            for n, shp in WEIGHT_NAMES:
                if n in ("wg1", "wu1", "wd1", "w_in", "w_out", "wg2", "wu2", "wd2"):
                    self.WB[n] = self.nc.dram_tensor("wb_" + n, list(shp), BF16).ap()
        self.yp = self.dout("yp", [T, D])
        self.kp_o = self.dout("kp", [128, 256])
        self.vp_o = self.dout("vp", [128, 256])
        self.convp_o = self.dout("convp", [3, 3072])
        self.deltap_o = self.dout("deltap", [8, 128, 128])
        if self.sample:
            self.xs = self.din("xs", [NS * TS, D])
            self.cs = self.din("cs", [NS, D])
            self.ck = self.din("ck", [NS, 128, 256])
            self.cv = self.din("cv", [NS, 128, 256])
            self.sconv = self.din("sconv", [NS, 3, 3072])
            self.sdelta = self.din("sdelta", [NS, 8, 128, 128])
            self.ys = self.dout("ys", [NS * TS, D])
            self.ks_o = self.dout("ks", [NS, 128, 256])
            self.vs_o = self.dout("vs", [NS, 128, 256])
            self.convs_o = self.dout("convs", [NS, 3, 3072])
            self.deltas_o = self.dout("deltas", [NS, 8, 128, 128])
        self.dbg_o = {}

        es = self.es
        with es:
            self.pbig = [es.enter_context(nc.psum_tensor("pbig%d" % i, [128, 1024], F32)) for i in range(4)]
            self.pbigb = [t.bitcast(BF16) for t in self.pbig]
            self.alloc_sbuf()
            self.esems = {e: self.sem("s_" + e) for e in ENGS}
            block = es.enter_context(nc.Block())
            self.prologue()
            self.P.relaxed = True
            for it in range(NPT):
                self.prompt_tile(it)
            if self.sample:
                self.sample_tile()
            self.P.finalize(block, self.esems)
            print("program ops", self.P.n_ops, "waits", self.P.n_waits,
                  {e: len(s) for e, s in self.P.streams.items()}, flush=True)
        return nc

    def alloc_sbuf(self):
        sb = self.sb
        NB = self.NB
        self.iot = sb("iot", [128, 128], F32)
        self.ident_f = sb("ident_f", [128, 128], F32)
        self.identb = sb("identb", [128, 128], BF16)
        self.onesb = sb("onesb", [128, 128], BF16)
        self.ones_f = sb("ones_f", [128, 128], F32)
        self.triT_f = sb("triT_f", [128, 128], F32)
        self.maskbig_f = sb("maskbig_f", [128, 128], F32)
        self.strict_f = sb("strict_f", [128, 128], F32)
        self.mrow = sb("mrow", [128, 256], BF16)
        self.mrow0 = sb("mrow0", [128, 256], BF16)
        self.eps6 = sb("eps6", [128, 1], F32)
        self.lvl_masks = [sb("lvlm%d" % i, [128, 128], F32) for i in range(4)]
        self.cos_t = sb("cos_t", [128, NB, 8], F32)
        self.sin_t = sb("sin_t", [128, NB, 8], F32)
        self.posf = sb("posf", [128, NB], F32)
        self.invf = sb("invf", [128, 8], F32)
        self.sinks_bc = sb("sinks_bc", [128, 16], F32)
        self.alog_bc = sb("alog_bc", [128, 8], F32)
        self.dtb_bc = sb("dtb_bc", [128, 8], F32)
        self.nexpA = sb("nexpA", [128, 8], F32)
        self.dnw = sb("dnw", [128, 1], F32)
        NSQ = 17 if self.sample else 1
        self.modT = sb("modT", [128, 72, NSQ], F32)
        self.mA = [sb("mA%d" % i, [128, 8, NSQ], F32) for i in range(3)]
        self.mG = [sb("mG%d" % i, [128, 8, NSQ], F32) for i in range(3)]
        self.gT = sb("gT", [128, 48], F32)
        self.cwT = sb("cwT", [128, 96], F32)
        self.badaT = sb("badaT", [128, 72], F32)
        self.wba = sb("wba", [128, 8, 16], BF16)
        self.scT = sb("scT", [128, 8, NSQ], BF16)
        self.xtok = sb("xtok", [128, 4, 1024], F32)
        self.stage = self.xtok[:, 0, :]
        self.xT = sb("xT", [128, 8, NT], F32)
        self.yF = sb("yF", [128, 8, NT], F32)
        self.hT = sb("hT", [128, 8, NT], BF16)
        self.tmpc = [sb("tmpc%d" % i, [128, NT], F32) for i in range(2)]
        self.rstd = sb("rstd", [128, NT], F32)
        self.abuf = sb("abuf", [128, NFF, NT], BF16)
        self.wide = [sb("wide%d" % i, [128, 8, 512], BF16) for i in range(3)]
        self.wdn = [sb("wdn%d" % i, [128, NFF, 128], BF16) for i in range(2)]
        ab = self.abuf
        self.QT = ab[:, 0:8, :]
        self.KT = ab[:, 8:16, :]
        xt_b = self.xtok.bitcast(BF16)
        self.vT = None
        self.xtok_b = xt_b
        self.vT = xt_b[:, 0:2, :].rearrange("p a (b n) -> p (a b) n", n=NT)
        self.sz = xt_b[:, 2:4, :].rearrange("p a (b n) -> p (a b) n", n=NT)
        self.sgb = ab[:, 0:8, :]
        self.oT = self.yF
        self.yT = self.hT
        self.ysw = sb("ysw", [128, 8, NT], BF16)
        self.ubuf = [sb("ubuf%d" % i, [128, 4 * 131], BF16) for i in range(2)]
        self.hist = sb("hist", [128, 24, 3], BF16)
        _c0 = sb("cdiag0", [128, 4, 128], BF16)
        self.cdiag = [_c0, _c0]
        self.convout = sb("convout", [128, 24, 3], F32)
        _q0 = sb("sqh0", [128, NT], BF16)
        self.sqh = [_q0, _q0]
        _r0 = sb("rinv0", [128, NT], F32)
        self.rinv = [_r0, _r0]
        if self.sample:
            self.hist4 = sb("hist4", [128, 24, 12], BF16)
            self.convout4 = sb("convout4", [128, 24, 12], F32)
            self.cos_s = sb("cos_s", [128, 1, 8], F32)
            self.sin_s = sb("sin_s", [128, 1, 8], F32)
            self.pos_s = sb("pos_s", [128, 1], F32)
            self.padm = sb("padm", [128, 1], F32)
        f = lambda n, s, d: sb(n, s, d)
        self.ba_sb = f("ba_sb", [128, 16], F32)
        self.g_tok = f("g_tok", [128, 8], F32)
        self.b_tok = f("b_tok", [128, 8], F32)
        self.nb_tok = f("nb_tok", [128, 8], F32)
        self.sp_t = [f("sp_t%d" % i, [128, 8], F32) for i in range(4)]
        self.Gcol = f("Gcol", [128, 8], F32)
        self.eG = f("eG", [128, 8], F32)
        self.beG = f("beG", [128, 8], F32)
        self.Glast = f("Glast", [128, 8], F32)
        self.eGlast = f("eGlast", [128, 8], F32)
        self.kdc = f("kdc", [128, 8], F32)
        self.Xg = f("Xg", [128, 4, 128], F32)
        self.dd = f("dd", [128, 4, 128], F32)
        self.dec = f("dec", [128, 4, 128], F32)
        self.nbs = self.dd
        self.eGrow = f("eGrow", [128, 4, 128], F32)
        self.Kbg = f("Kbg", [128, 4, 128], BF16)
        self.Kdec = f("Kdec", [128, 4, 128], BF16)
        self.Vb = f("Vb", [128, 4, 128], BF16)
        self.Nm = [f("Nm%d" % i, [128, 4, 128], F32) for i in range(2)]
        self.NTm = [f("NTm%d" % i, [128, 4, 128], F32) for i in range(2)]
        self.Rb = f("Rb", [128, 4, 128], BF16)
        self.Amat = f("Amat", [128, 4, 128], BF16)
        self.ATm = f("ATm", [128, 4, 128], BF16)
        self.U = f("U", [128, 4, 128], F32)
        self.WT = f("WT", [128, 4, 128], BF16)
        self.QgT = f("QgT", [128, 4, 128], BF16)
        self.Vnew = f("Vnew", [128, 4, 128], BF16)
        self.S = f("S", [128, 8, 128], F32)
        self.Sb = f("Sb", [128, 8, 128], BF16)
        self.q_tok = f("q_tok", [128, 16, 64], BF16)
        self.k_pad = f("k_pad", [128, 4, 2, 128], BF16)
        self.kf32 = f("kf32", [128, 4, 64], F32)
        self.rt = [f("rt%d" % i, [128, 16, 8], F32) for i in range(4)]
        self.QTs = f("QTs", [128, 8, 128], BF16)
        self.KTs = [f("KTs%d" % i, [128, 8, 128], BF16) for i in range(2)]
        self.Vs = [f("Vs%d" % i, [128, 256], BF16) for i in range(2)]
        self.Eb = f("Eb", [128, 4, 256], BF16)
        self.ETb = f("ETb", [128, 4, 256], BF16)
        self.mx = f("mx", [128, 4], F32)
        self.nm = f("nm", [128, 4], F32)
        self.den = f("den", [128, 16], F32)
        self.esk = f("esk", [128, 4], F32)
        self.dacc = f("dacc", [128, 4, 16], F32)
        self.rden = f("rden", [128, 16], F32)
        self.osw = f("osw", [128, 16, 64], BF16)
        self.vf32 = self.osw.bitcast(F32).rearrange("p a b -> p (a b)")[:, 0:256].rearrange("p (h d) -> p h d", d=64)
        self.rp_a = self.Eb.bitcast(F32).rearrange("p a b -> p (a b)")[:, 0:NB * 8].rearrange("p (b f) -> p b f", f=8)
        self.rp_b = self.ETb.bitcast(F32).rearrange("p a b -> p (a b)")[:, 0:NB * 8].rearrange("p (b f) -> p b f", f=8)
        print("sbuf bytes remaining", self.nc.sbuf_bytes_remaining, flush=True)

    def transpose_rows(self, rows_ap, nrows, dst_ap, key_r, key_w, f32=True):
        b = self.bank()
        self.pe(lambda e: e.transpose(out=self.bk(b)[:, 0:nrows], in_=rows_ap, identity=self.ident_f[0:nrows, 0:nrows]),
                r=[key_r, "const"], w=[self.bkey(b)])
        self.dve(lambda e: e.tensor_copy(out=dst_ap, in_=self.bk(b)[:, 0:nrows]), r=[self.bkey(b)], w=[key_w])

    def prologue(self):
        W = self.W
        NB = self.NB
        iot = self.iot
        self.pool(lambda e: e.iota(iot[:], pattern=[[1, 128]], base=0, channel_multiplier=-1,
                                   allow_small_or_imprecise_dtypes=True), w=["iot"])
        self.pool(lambda e: e.iota(self.posf[:], pattern=[[128, NB]], base=0, channel_multiplier=1,
                                   allow_small_or_imprecise_dtypes=True), w=["posf"])
        C = ["const"]
        d = self.dve
        d(lambda e: e.tensor_single_scalar(out=self.ident_f[:], in_=iot[:], scalar=0.0, op=ALU.is_equal), r=["iot"], w=C)
        d(lambda e: e.tensor_single_scalar(out=self.identb[:], in_=iot[:], scalar=0.0, op=ALU.is_equal), r=["iot"], w=C)
        d(lambda e: e.tensor_single_scalar(out=self.triT_f[:], in_=iot[:], scalar=0.0, op=ALU.is_ge), r=["iot"], w=C)
        d(lambda e: e.tensor_scalar(out=self.maskbig_f[:], in0=iot[:], scalar1=0.0, scalar2=1.0e4, op0=ALU.is_gt, op1=ALU.mult), r=["iot"], w=C)
        d(lambda e: e.tensor_single_scalar(out=self.strict_f[:], in_=iot[:], scalar=0.0, op=ALU.is_lt), r=["iot"], w=C)
        d(lambda e: e.tensor_scalar(out=self.mrow[:, 128:256], in0=iot[:], scalar1=0.0, scalar2=-1.0e30, op0=ALU.is_gt, op1=ALU.mult), r=["iot"], w=C)
        d(lambda e: e.tensor_scalar(out=self.mrow[:, 0:128], in0=iot[:], scalar1=0.0, scalar2=-1.0e30, op0=ALU.is_lt, op1=ALU.mult), r=["iot"], w=C)
        d(lambda e: e.tensor_scalar(out=self.mrow0[:, 128:256], in0=iot[:], scalar1=0.0, scalar2=-1.0e30, op0=ALU.is_gt, op1=ALU.mult), r=["iot"], w=C)
        d(lambda e: e.memset(self.mrow0[:, 0:128], -1.0e30), w=C)
        d(lambda e: e.memset(self.k_pad[:], 0.0), w=["k_pad"])
        d(lambda e: e.memset(self.onesb[:], 1.0), w=C)
        d(lambda e: e.memset(self.ones_f[:], 1.0), w=C)
        d(lambda e: e.memset(self.eps6[:], EPS), w=C)
        for f in range(8):
            d(lambda e, f=f: e.memset(self.invf[:, f:f + 1], float(INV_FREQ[f])), w=["invf"])
        prev = None
        for li, bs_ in enumerate([16, 32, 64]):
            t_j = self.tmpc[0][:, 0:128]
            t_p = self.tmpc[1][:, 0:128]
            self.pool(lambda e, bs_=bs_, t_j=t_j: e.iota(t_j, pattern=[[1, 128 // bs_], [0, bs_]], base=0, channel_multiplier=0,
                                                      allow_small_or_imprecise_dtypes=True), w=["tmpc0"])
            b = self.bank()
            self.pe(lambda e, b=b, t_j=t_j: e.transpose(out=self.bk(b)[:, 0:128], in_=t_j, identity=self.ident_f[:]), r=["tmpc0", "const"], w=[self.bkey(b)])
            d(lambda e, b=b, t_p=t_p: e.tensor_copy(out=t_p, in_=self.bk(b)[:, 0:128]), r=[self.bkey(b)], w=["tmpc1"])
            eq = (self.rinv[0] if li % 2 == 0 else self.rstd)[:, 0:128]
            eqk = "rinv0" if li % 2 == 0 else "rstd"
            d(lambda e, eq=eq, t_j=t_j, t_p=t_p: e.tensor_tensor(out=eq, in0=t_j, in1=t_p, op=ALU.is_equal), r=["tmpc0", "tmpc1"], w=[eqk])
            if li == 0:
                d(lambda e, eq=eq: e.tensor_copy(out=self.lvl_masks[0][:], in_=eq), r=[eqk], w=C)
            else:
                d(lambda e, eq=eq, prev=prev, li=li: e.tensor_tensor(out=self.lvl_masks[li][:], in0=eq, in1=prev[0], op=ALU.subtract), r=[eqk, prev[1]], w=C)
            if li == 2:
                d(lambda e, eq=eq: e.tensor_scalar(out=self.lvl_masks[3][:], in0=eq, scalar1=-1.0, scalar2=1.0, op0=ALU.mult, op1=ALU.add), r=[eqk], w=C)
            prev = (eq, eqk)
        d(lambda e: e.memset(self.S[:], 0.0), w=["S"])
        d(lambda e: e.memset(self.Sb[:], 0.0), w=["Sb"])
        d(lambda e: e.memset(self.hist[:], 0.0), w=["hist"])
        d(lambda e: e.memset(self.KTs[1][:], 0.0), w=["KTs1"])
        d(lambda e: e.memset(self.Vs[1][:], 0.0), w=["Vs1"])
        self.rope_table(self.posf[:], NB, self.cos_t, self.sin_t, "rope")
        self.dump("cos", self.cos_t[:].rearrange("p b f -> p (b f)"), ["rope"])
        self.dump("sin", self.sin_t[:].rearrange("p b f -> p (b f)"), ["rope"])
        if self.sample:
            self.pool(lambda e: e.iota(self.pos_s[:], pattern=[[0, 1]], base=PAST, channel_multiplier=1, allow_small_or_imprecise_dtypes=True), w=["posf"])
            self.rope_table(self.pos_s[:], 1, self.cos_s, self.sin_s, "rope")
            self.dve(lambda e: e.tensor_single_scalar(out=self.padm[:], in_=self.pos_s[:], scalar=float(PAST + TS) - 0.5, op=ALU.is_lt), r=["posf"], w=["padm"])
        self.dma("sp", self.sinks_bc[:], W["sinks"].partition_broadcast(128), w=["sinks_bc"], sem="small0")
        self.dma("sp", self.alog_bc[:], W["a_log"].partition_broadcast(128), w=["alog_bc"], sem="small1")
        self.dma("sp", self.dtb_bc[:], W["dt_bias"].partition_broadcast(128), w=["dtb_bc"], sem="small2")
        self.dma("sp", self.dnw[:], W["dn_norm"].rearrange("(p o) -> p o", o=1), w=["dnw"], sem="small3")
        self.act(lambda e: e.activation(out=self.nexpA[:], in_=self.alog_bc[:], func=AF.Exp), r=["alog_bc"], w=["nexpA"])
        self.dve(lambda e: e.tensor_scalar_mul(out=self.nexpA[:], in0=self.nexpA[:], scalar1=-1.0), r=["nexpA"], w=["nexpA"])
        self.dma("pool", self.wba[:], W["w_in"].rearrange("(kc p) n -> p kc n", p=128)[:, :, OFF_B:OFF_B + 16],
                 w=["wba"], sem="wba")
        st = self.stage
        gains = ["n1pre", "n1post", "n2pre", "n2post", "n3pre", "n3post"]
        for gi, gn in enumerate(gains):
            self.dma("sp", st[gi * 8:(gi + 1) * 8, 0:128], W[gn].rearrange("(c p) -> c p", p=128), w=["xtok"], sem="xtok")
        self.transpose_rows(st[0:48, 0:128], 48, self.gT[:], "xtok", "gT")
        self.dma("sp", st[0:96, 0:128], W["conv_w"].rearrange("i (c p) -> (i c) p", p=128), r=[], w=["xtok"], sem="xtok")
        self.transpose_rows(st[0:96, 0:128], 96, self.cwT[:], "xtok", "cwT")
        self.dma("sp", st[0:72, 0:128], W["b_ada"].rearrange("(c p) -> c p", p=128), w=["xtok"], sem="xtok")
        self.transpose_rows(st[0:72, 0:128], 72, self.badaT[:], "xtok", "badaT")
        nseq = 17 if self.sample else 1
        self.nseq = nseq
        self.dma("sp", st[0:1, :], self.cp[:, :], w=["xtok"], sem="xtok")
        if self.sample:
            self.dma("sp", st[1:17, :], self.cs[:, :], w=["xtok"], sem="xtok")
        self.act(lambda e: e.activation(out=st[0:nseq, :], in_=st[0:nseq, :], func=AF.Silu), r=["xtok"], w=["xtok"])
        b = self.bank()
        for c in range(8):
            self.pe(lambda e, c=c, b=b: e.transpose(out=self.bk(b)[:, c * 32:c * 32 + nseq], in_=st[0:nseq, c * 128:(c + 1) * 128],
                                               identity=self.ident_f[0:nseq, 0:nseq]), r=["xtok", "const"], w=[self.bkey(b)])
        self.dve(lambda e, b=b: e.tensor_copy(out=self.scT[:, :, 0:nseq],
                                         in_=self.bk(b)[:, 0:256].rearrange("p (c s) -> p c s", s=32)[:, :, 0:nseq]),
                 r=[self.bkey(b)], w=["scT"])
        for blk in range(18):
            wt, wk = self.wload("w_ada", blk * 512, 512)
            b = self.bank()
            for j in range(4):
                for kc in range(8):
                    self.pe(lambda e, j=j, kc=kc, wt=wt, b=b: e.matmul(self.bk(b)[:, j * 32:j * 32 + nseq], lhsT=wt[:, kc, j * 128:(j + 1) * 128],
                                                                      rhs=self.scT[:, kc, 0:nseq], start=(kc == 0), stop=(kc == 7)),
                            r=[wk, "scT"], w=[self.bkey(b)])
            cc0 = blk * 4
            self.dve(lambda e, b=b, cc0=cc0: e.tensor_tensor(
                out=self.modT[:, cc0:cc0 + 4, 0:nseq],
                in0=self.bk(b)[:, 0:128].rearrange("p (c s) -> p c s", s=32)[:, :, 0:nseq],
                in1=bc_last(self.badaT[:, cc0:cc0 + 4], nseq), op=ALU.add),
                r=[self.bkey(b), "badaT"], w=["modT"])
        coefs = [0.5, 1.0, 0.5]
        for n in range(3):
            sc = self.modT[:, 24 * n + 8:24 * n + 16, 0:nseq]
            gg = self.modT[:, 24 * n + 16:24 * n + 24, 0:nseq]
            gpre = bc_last(self.gT[:, 16 * n:16 * n + 8], nseq)
            gpost = bc_last(self.gT[:, 16 * n + 8:16 * n + 16], nseq)
            self.dve(lambda e, n=n, sc=sc, gpre=gpre: e.scalar_tensor_tensor(out=self.mA[n][:, :, 0:nseq], in0=sc, scalar=1.0, in1=gpre,
                                                                             op0=ALU.add, op1=ALU.mult), r=["modT", "gT"], w=["mA%d" % n])
            self.dve(lambda e, n=n, gg=gg, gpost=gpost: e.scalar_tensor_tensor(out=self.mG[n][:, :, 0:nseq], in0=gg, scalar=coefs[n], in1=gpost,
                                                                               op0=ALU.mult, op1=ALU.mult), r=["modT", "gT"], w=["mG%d" % n])

        for n in ("wg1", "wu1", "wd1", "w_in", "w_out", "wg2", "wu2", "wd2"):
            if n in self.WB:
                rows = self.W[n].shape[0]
                r0 = 0
                while r0 < rows:
                    r1 = min(rows, r0 + 256)
                    self.dma("pool", self.WB[n][r0:r1, :], self.W[n][r0:r1, :], w=["wb_" + n], sem="wb_" + n)
                    r0 = r1

    def mB(self, n):
        return self.modT[:, 24 * n:24 * n + 8, :]

    def rope_table(self, pos_ap, nb, cos_t, sin_t, key):
        a, bb = self.rp_a, self.rp_b
        d = self.dve
        K = [key]
        d(lambda e: e.tensor_tensor(out=a[:, 0:nb, :], in0=bc_last(pos_ap, 8), in1=bc_mid(self.invf[:], nb), op=ALU.mult),
          r=["posf", "invf"], w=K)
        self.dump("posf", self.posf[:], ["posf"])
        self.dump("invf", self.invf[:], ["invf"])
        self.dump("ang", a[:, 0:nb, :].rearrange("p b f -> p (b f)"), K)
        d(lambda e: e.tensor_scalar_mul(out=bb[:, 0:nb, :], in0=a[:, 0:nb, :], scalar1=float(1.0 / TWO_PI)), r=K, w=K)
        d(lambda e: e.tensor_scalar_add(out=bb[:, 0:nb, :], in0=bb[:, 0:nb, :], scalar1=12582912.0), r=K, w=K)
        d(lambda e: e.tensor_scalar_add(out=bb[:, 0:nb, :], in0=bb[:, 0:nb, :], scalar1=-12582912.0), r=K, w=K)
        self.dump("kf", bb[:, 0:nb, :].rearrange("p b f -> p (b f)"), K)
        d(lambda e: e.scalar_tensor_tensor(out=a[:, 0:nb, :], in0=bb[:, 0:nb, :], scalar=-CW1, in1=a[:, 0:nb, :], op0=ALU.mult, op1=ALU.add), r=K, w=K)
        d(lambda e: e.scalar_tensor_tensor(out=a[:, 0:nb, :], in0=bb[:, 0:nb, :], scalar=-CW2, in1=a[:, 0:nb, :], op0=ALU.mult, op1=ALU.add), r=K, w=K)

        self.dump("red", a[:, 0:nb, :].rearrange("p b f -> p (b f)"), K)

        def wrap_and_sin(dst):
            d(lambda e: e.tensor_scalar(out=bb[:, 0:nb, :], in0=a[:, 0:nb, :], scalar1=float(np.pi), scalar2=float(-TWO_PI), op0=ALU.is_gt, op1=ALU.mult), r=K, w=K)
            d(lambda e: e.tensor_tensor(out=a[:, 0:nb, :], in0=a[:, 0:nb, :], in1=bb[:, 0:nb, :], op=ALU.add), r=K, w=K)
            d(lambda e: e.tensor_scalar(out=bb[:, 0:nb, :], in0=a[:, 0:nb, :], scalar1=float(-np.pi), scalar2=float(TWO_PI), op0=ALU.is_lt, op1=ALU.mult), r=K, w=K)
            d(lambda e: e.tensor_tensor(out=a[:, 0:nb, :], in0=a[:, 0:nb, :], in1=bb[:, 0:nb, :], op=ALU.add), r=K, w=K)
            d(lambda e: e.tensor_scalar(out=bb[:, 0:nb, :], in0=a[:, 0:nb, :], scalar1=3.14159, scalar2=-3.14159, op0=ALU.min, op1=ALU.max), r=K, w=K)
            self.act(lambda e: e.activation(out=dst[:, 0:nb, :], in_=bb[:, 0:nb, :], func=AF.Sin), r=K, w=K)

        wrap_and_sin(sin_t)
        d(lambda e: e.tensor_scalar_add(out=a[:, 0:nb, :], in0=a[:, 0:nb, :], scalar1=float(np.pi / 2)), r=K, w=K)
        wrap_and_sin(cos_t)

    def wload(self, wname, col0, ncols):
        if not hasattr(self, "_wrr"):
            self._wrr = 0
        i = self._wrr
        self._wrr = (i + 1) % len(self.wide)
        t = self.wide[i]
        key = "wide%d" % i
        if wname in self.WB:
            src = self.WB[wname].rearrange("(kc p) n -> p kc n", p=128)[:, :, col0:col0 + ncols]
            self.dma("sp", t[:, :, 0:ncols], src, r=["wb_" + wname], w=[key], sem=key)
        else:
            src = self.W[wname].rearrange("(kc p) n -> p kc n", p=128)[:, :, col0:col0 + ncols]
            self.dma("pool", t[:, :, 0:ncols], src, w=[key], sem=key)
        return t, key

    def wload_down(self, wname, dc):
        if not hasattr(self, "_drr"):
            self._drr = 0
        i = self._drr
        self._drr = (i + 1) % len(self.wdn)
        t = self.wdn[i]
        key = "wdn%d" % i
        if wname in self.WB:
            w = self.WB[wname]
            q, rr = "sp", ["wb_" + wname]
        else:
            w = self.W[wname]
            q, rr = "pool", []
        self.dma(q, t[:, 0:21, :], w[0:2688, dc * 128:(dc + 1) * 128].rearrange("(kc p) n -> p kc n", p=128), r=rr, w=[key], sem=key)
        self.dma(q, t[0:64, 21, :], w[2688:2752, dc * 128:(dc + 1) * 128], r=rr, w=[key], sem=key)
        return t, key

    def rms_rstd(self, src, src_key, N):
        b = self.bank()
        for c in range(8):
            sqh = self.sqh[c % 2]
            sk = "sqh0"
            self.act(lambda e, c=c, sqh=sqh: e.activation(out=sqh[:, 0:N], in_=src[:, c, 0:N], func=AF.Square), r=[src_key], w=[sk])
            self.pe(lambda e, c=c, sqh=sqh: e.matmul(self.bk(b)[:, 0:N], lhsT=self.onesb[:], rhs=sqh[:, 0:N], start=(c == 0), stop=(c == 7)),
                    r=[sk, "const"], w=[self.bkey(b)])
        self.act(lambda e: e.activation(out=self.rstd[:, 0:N], in_=self.bk(b)[:, 0:N], func=AF.Ln, bias=self.eps6[:, 0:1], scale=1.0 / D),
                 r=[self.bkey(b), "const"], w=["rstd"])
        self.act(lambda e: e.activation(out=self.rstd[:, 0:N], in_=self.rstd[:, 0:N], func=AF.Exp, scale=-0.5), r=["rstd"], w=["rstd"])

    def prenorm(self, n, N, smp):
        self.rms_rstd(self.xT, "xT", N)
        A = self.mA[n]
        B = self.mB(n)
        for c in range(8):
            t = self.tmpc[c % 2]
            tk = "tmpc%d" % (c % 2)
            self.dve(lambda e, c=c, t=t: e.tensor_tensor(out=t[:, 0:N], in0=self.xT[:, c, 0:N], in1=self.rstd[:, 0:N], op=ALU.mult),
                     r=["xT", "rstd"], w=[tk])
            if not smp:
                self.act(lambda e, c=c, t=t: e.activation(out=self.hT[:, c, 0:N], in_=t[:, 0:N], func=AF.Identity,
                                                          bias=B[:, c, 0:1], scale=A[:, c, 0:1]), r=[tk, "mA%d" % n, "modT"], w=["hT"])
            elif isinstance(smp, list):
                for (c0, c1, sq) in smp:
                    self.act(lambda e, c=c, t=t, c0=c0, c1=c1, sq=sq: e.activation(out=self.hT[:, c, c0:c1], in_=t[:, c0:c1], func=AF.Identity,
                                                                                   bias=B[:, c, sq:sq + 1], scale=A[:, c, sq:sq + 1]),
                             r=[tk, "mA%d" % n, "modT"], w=["hT"])
            else:
                tv = t[:, 0:N].rearrange("p (s t) -> p s t", t=TS)
                self.dve(lambda e, c=c, tv=tv: e.tensor_tensor(out=tv, in0=tv, in1=bc_last(A[:, c, 1:17], TS), op=ALU.mult), r=[tk, "mA%d" % n], w=[tk])
                self.dve(lambda e, c=c, tv=tv: e.tensor_tensor(out=self.hT[:, c, 0:N].rearrange("p (s t) -> p s t", t=TS), in0=tv,
                                                               in1=bc_last(B[:, c, 1:17], TS), op=ALU.add), r=[tk, "modT"], w=["hT"])

    def postnorm(self, n, N, smp):
        self.rms_rstd(self.yF, "yF", N)
        G = self.mG[n]
        for c in range(8):
            t = self.tmpc[c % 2]
            tk = "tmpc%d" % (c % 2)
            self.dve(lambda e, c=c, t=t: e.tensor_tensor(out=t[:, 0:N], in0=self.yF[:, c, 0:N], in1=self.rstd[:, 0:N], op=ALU.mult),
                     r=["yF", "rstd"], w=[tk])
            if not smp:
                self.dve(lambda e, c=c, t=t: e.scalar_tensor_tensor(out=self.xT[:, c, 0:N], in0=t[:, 0:N], scalar=G[:, c, 0:1],
                                                                    in1=self.xT[:, c, 0:N], op0=ALU.mult, op1=ALU.add),
                         r=[tk, "mG%d" % n, "xT"], w=["xT"])
            elif isinstance(smp, list):
                for (c0, c1, sq) in smp:
                    self.dve(lambda e, c=c, t=t, c0=c0, c1=c1, sq=sq: e.scalar_tensor_tensor(out=self.xT[:, c, c0:c1], in0=t[:, c0:c1], scalar=G[:, c, sq:sq + 1],
                                                                                             in1=self.xT[:, c, c0:c1], op0=ALU.mult, op1=ALU.add),
                             r=[tk, "mG%d" % n, "xT"], w=["xT"])
            else:
                tv = t[:, 0:N].rearrange("p (s t) -> p s t", t=TS)
                self.dve(lambda e, c=c, tv=tv: e.tensor_tensor(out=tv, in0=tv, in1=bc_last(G[:, c, 1:17], TS), op=ALU.mult), r=[tk, "mG%d" % n], w=[tk])
                self.dve(lambda e, c=c, t=t: e.tensor_tensor(out=self.xT[:, c, 0:N], in0=self.xT[:, c, 0:N], in1=t[:, 0:N], op=ALU.add),
                         r=[tk, "xT"], w=["xT"])

    def ffn(self, wg, wu, wd, N):
        blocks = [(i * 512, 512) for i in range(5)] + [(2560, 192)]
        ffc = 0
        for (c0, ncol) in blocks:
            gt, gk = self.wload(wg, c0, ncol)
            ut, uk = self.wload(wu, c0, ncol)
            j = 0
            while j * 128 < ncol:
                m = min(128, ncol - j * 128)
                bg = self.bank()
                bu = self.bank()
                for kc in range(8):
                    self.pe(lambda e, kc=kc, j=j, m=m, bg=bg, gt=gt: e.matmul(self.bk(bg)[0:m, 0:N], lhsT=gt[:, kc, j * 128:j * 128 + m],
                                                                               rhs=self.hT[:, kc, 0:N], start=(kc == 0), stop=(kc == 7)),
                            r=[gk, "hT"], w=[self.bkey(bg)])
                for kc in range(8):
                    self.pe(lambda e, kc=kc, j=j, m=m, bu=bu, ut=ut: e.matmul(self.bk(bu)[0:m, 0:N], lhsT=ut[:, kc, j * 128:j * 128 + m],
                                                                               rhs=self.hT[:, kc, 0:N], start=(kc == 0), stop=(kc == 7)),
                            r=[uk, "hT"], w=[self.bkey(bu)])
                t = self.tmpc[ffc % 2]
                tk = "tmpc%d" % (ffc % 2)
                self.act(lambda e, m=m, bg=bg, t=t: e.activation(out=t[0:m, 0:N], in_=self.bk(bg)[0:m, 0:N], func=AF.Silu),
                         r=[self.bkey(bg)], w=[tk])
                self.dve(lambda e, m=m, bu=bu, t=t, ffc=ffc: e.tensor_tensor(out=self.abuf[0:m, ffc, 0:N], in0=t[0:m, 0:N], in1=self.bk(bu)[0:m, 0:N], op=ALU.mult),
                         r=[tk, self.bkey(bu)], w=["abuf"])
                ffc += 1
                j += 1
        for dc in range(8):
            dt, dk = self.wload_down(wd, dc)
            b = self.bank()
            for f in range(NFF):
                kk = 128 if f < 21 else 64
                self.pe(lambda e, f=f, kk=kk, b=b, dt=dt: e.matmul(self.bk(b)[:, 0:N], lhsT=dt[0:kk, f, :], rhs=self.abuf[0:kk, f, 0:N],
                                                                    start=(f == 0), stop=(f == NFF - 1)),
                        r=[dk, "abuf"], w=[self.bkey(b)])
            self.act(lambda e, b=b, dc=dc: e.activation(out=self.yF[:, dc, 0:N], in_=self.bk(b)[:, 0:N], func=AF.Copy),
                     r=[self.bkey(b)], w=["yF"])

    def load_x(self, rows_ap, nblk, rows_per_blk):
        if rows_ap is not None:
            self.dma("sp", self.xtok[0:rows_per_blk, 0:nblk, :], rows_ap.rearrange("(b p) d -> p b d", p=rows_per_blk), w=["xtok"], sem="xin")
        for c in range(8):
            b = self.bank()
            for bi in range(nblk):
                self.pe(lambda e, c=c, bi=bi, b=b: e.transpose(out=self.bk(b)[:, bi * rows_per_blk:(bi + 1) * rows_per_blk],
                                                               in_=self.xtok[0:rows_per_blk, bi, c * 128:(c + 1) * 128],
                                                               identity=self.ident_f[0:rows_per_blk, 0:rows_per_blk]),
                        r=["xtok", "const"], w=[self.bkey(b)])
            n = nblk * rows_per_blk
            self.act(lambda e, c=c, b=b, n=n: e.activation(out=self.xT[:, c, 0:n], in_=self.bk(b)[:, 0:n], func=AF.Copy),
                     r=[self.bkey(b)], w=["xT"])

    def store_x(self, rows_ap, nblk, rows_per_blk):
        for bi in range(nblk):
            for half in range(2):
                b = self.bank()
                for c4 in range(4):
                    c = half * 4 + c4
                    self.pe(lambda e, c=c, c4=c4, bi=bi, b=b: e.transpose(out=self.bk(b)[0:rows_per_blk, c4 * 128:(c4 + 1) * 128],
                                                                          in_=self.xT[:, c, bi * rows_per_blk:(bi + 1) * rows_per_blk],
                                                                          identity=self.ident_f[:, :]),
                            r=["xT", "const"], w=[self.bkey(b)])
                self.act(lambda e, bi=bi, half=half, b=b: e.activation(out=self.xtok[0:rows_per_blk, bi, half * 512:(half + 1) * 512],
                                                                       in_=self.bk(b)[0:rows_per_blk, :], func=AF.Copy),
                         r=[self.bkey(b)], w=["xtok"])
        if rows_ap is not None:
            self.dma("sp", rows_ap.rearrange("(b p) d -> p b d", p=rows_per_blk), self.xtok[0:rows_per_blk, 0:nblk, :], r=["xtok"], sem="xout", output=True)

    def prompt_tile(self, it):
        N = NT
        L = self.stage_lim
        self.load_x(self.xp[it * NT:(it + 1) * NT, :], 4, 128)
        if L >= 1:
            self.prenorm(0, N, False)
        if L >= 2:
            self.ffn("wg1", "wu1", "wd1", N)
            self.postnorm(0, N, False)
        if L >= 3:
            self.prenorm(1, N, False)
            self.mixer_prompt(it)
        if L >= 8:
            self.postnorm(1, N, False)
        if L >= 9:
            self.prenorm(2, N, False)
            self.ffn("wg2", "wu2", "wd2", N)
            self.postnorm(2, N, False)
        self.store_x(self.yp[it * NT:(it + 1) * NT, :], 4, 128)

    def proj_fm(self, wt, wk, j, N, b):
        for kc in range(8):
            self.pe(lambda e, kc=kc: e.matmul(self.bk(b)[:, 0:N], lhsT=wt[:, kc, j * 128:(j + 1) * 128], rhs=self.hT[:, kc, 0:N],
                                              start=(kc == 0), stop=(kc == 7)), r=[wk, "hT"], w=[self.bkey(b)])

    def l2norm_chunk(self, src, src_key, N, dst, dst_key, scale, idx):
        sqh = self.sqh[idx % 2]
        sk = "sqh0"
        ri = self.rinv[idx % 2]
        rk = "rinv0"
        self.act(lambda e: e.activation(out=sqh[:, 0:N], in_=src, func=AF.Square), r=[src_key], w=[sk])
        b = self.bank()
        self.pe(lambda e: e.matmul(self.bk(b)[:, 0:N], lhsT=self.onesb[:], rhs=sqh[:, 0:N], start=True, stop=True), r=[sk, "const"], w=[self.bkey(b)])
        self.act(lambda e: e.activation(out=ri[:, 0:N], in_=self.bk(b)[:, 0:N], func=AF.Ln, bias=self.eps6[:, 0:1], scale=1.0),
                 r=[self.bkey(b), "const"], w=[rk])
        self.act(lambda e: e.activation(out=ri[:, 0:N], in_=ri[:, 0:N], func=AF.Exp, scale=-0.5), r=[rk], w=[rk])
        self.dve(lambda e: e.scalar_tensor_tensor(out=dst, in0=src, scalar=float(scale), in1=ri[:, 0:N], op0=ALU.mult, op1=ALU.mult),
                 r=[src_key, rk], w=[dst_key])

    def mixer_prompt(self, it):
        N = NT
        smp = it < 0
        last = (it == self.NPT - 1)
        wcur = {}

        def stage_a(cc):
            blk, j = cc // 4, cc % 4
            if j == 0:
                wcur["w"] = self.wload("w_in", blk * 512, 512)
            wt, wk = wcur["w"]
            b = self.bank()
            self.proj_fm(wt, wk, j, N, b)
            ub = self.ubuf[cc % 2]
            uk = "ubuf%d" % (cc % 2)
            if not smp:
                self.dve(lambda e, cc=cc, ub=ub: e.tensor_copy(out=ub[:, 0:3], in_=self.hist[:, cc, :]), r=["hist"], w=[uk])
                self.act(lambda e, b=b, ub=ub: e.activation(out=ub[:, 3:3 + N], in_=self.bk(b)[:, 0:N], func=AF.Copy), r=[self.bkey(b)], w=[uk])
                self.dve(lambda e, cc=cc, ub=ub: e.tensor_copy(out=self.hist[:, cc, :], in_=ub[:, N:N + 3]), r=[uk], w=["hist"])
                if last:
                    self.dve(lambda e, cc=cc, b=b: e.tensor_copy(out=self.convout[:, cc, :], in_=self.bk(b)[:, N - 3:N]), r=[self.bkey(b)], w=["convout"])
            else:
                ub4 = ub[:, 0:4 * 131].rearrange("p (u n) -> p u n", n=131)
                self.dve(lambda e, cc=cc, ub4=ub4: e.tensor_copy(out=ub4[:, :, 0:3], in_=self.hist4[:, cc, :].rearrange("p (u t) -> p u t", t=3)), r=["hist4"], w=[uk])
                self.act(lambda e, b=b, ub4=ub4: e.activation(out=ub4[:, :, 3:131], in_=self.bk(b)[:, 0:N].rearrange("p (u n) -> p u n", n=128), func=AF.Copy),
                         r=[self.bkey(b)], w=[uk])
                self.dve(lambda e, cc=cc, b=b: e.tensor_copy(out=self.convout4[:, cc, :].rearrange("p (u t) -> p u t", t=3),
                                                             in_=self.bk(b)[:, 0:N].rearrange("p (u n) -> p u n", n=128)[:, :, 1:4]), r=[self.bkey(b)], w=["convout4"])

        def stage_b(cc):
            ub = self.ubuf[cc % 2]
            uk = "ubuf%d" % (cc % 2)
            cd = self.cdiag[cc % 2]
            ck = "cdiag0"
            for i in range(4):
                self.dve(lambda e, i=i, cc=cc, cd=cd: e.tensor_scalar_mul(out=cd[:, i, :], in0=self.identb[:], scalar1=self.cwT[:, i * 24 + cc:i * 24 + cc + 1]),
                         r=["const", "cwT"], w=[ck])
            b2 = self.bank()
            if not smp:
                for i in range(4):
                    self.pe(lambda e, i=i, cd=cd, ub=ub, b2=b2: e.matmul(self.bk(b2)[:, 0:N], lhsT=cd[:, i, :], rhs=ub[:, i:i + N], start=(i == 0), stop=(i == 3)),
                            r=[ck, uk], w=[self.bkey(b2)])
            else:
                ub4 = ub[:, 0:4 * 131].rearrange("p (u n) -> p u n", n=131)
                for u in range(4):
                    for i in range(4):
                        self.pe(lambda e, i=i, u=u, cd=cd, ub4=ub4, b2=b2: e.matmul(self.bk(b2)[:, u * 128:(u + 1) * 128], lhsT=cd[:, i, :], rhs=ub4[:, u, i:i + 128],
                                                                                       start=(i == 0), stop=(i == 3)), r=[ck, uk], w=[self.bkey(b2)])
            if cc < 16:
                t = self.tmpc[cc % 2]
                tk = "tmpc%d" % (cc % 2)
                self.act(lambda e, b2=b2, t=t: e.activation(out=t[:, 0:N], in_=self.bk(b2)[:, 0:N], func=AF.Silu), r=[self.bkey(b2)], w=[tk])
            else:
                self.act(lambda e, b2=b2, cc=cc: e.activation(out=self.vT[:, cc - 16, :], in_=self.bk(b2)[:, 0:N], func=AF.Silu),
                         r=[self.bkey(b2)], w=["xtok"])

        def stage_c(cc):
            if cc >= 16:
                return
            t = self.tmpc[cc % 2]
            tk = "tmpc%d" % (cc % 2)
            if cc < 8:
                self.l2norm_chunk(t[:, 0:N], tk, N, self.QT[:, cc, :], "abuf", 128.0 ** -0.5, cc)
            else:
                self.l2norm_chunk(t[:, 0:N], tk, N, self.KT[:, cc - 8, :], "abuf", 1.0, cc)

        stage_a(0)
        for cc in range(24):
            if cc + 1 < 24:
                stage_a(cc + 1)
            stage_b(cc)
            if cc >= 1:
                stage_c(cc - 1)
        stage_c(23)
        if smp:
            jj = -1 - it
            stg = self.yF[:, :, :].rearrange("p c n -> p (c n)")
            for cc in range(24):
                b = self.bank()
                self.pe(lambda e, cc=cc, b=b: e.transpose(out=self.bk(b)[0:12, 0:128], in_=self.convout4[:, cc, :], identity=self.ident_f[:]),
                        r=["convout4", "const"], w=[self.bkey(b)])
                self.dve(lambda e, cc=cc, b=b: e.tensor_copy(out=stg[0:12, cc * 128:(cc + 1) * 128], in_=self.bk(b)[0:12, 0:128]), r=[self.bkey(b)], w=["yF"])
            self.dma("sp", self.convs_o[4 * jj:4 * jj + 4].rearrange("s t c -> (s t) c"), stg[0:12, 0:3072], r=["yF"], sem="cvs", output=True)
        if self.stage_lim < 4:
            return
        for blk in range(2):
            wz, wzk = self.wload("w_in", OFF_Z + blk * 512, 512)
            wa, wak = self.wload("w_in", OFF_GA + blk * 512, 512)
            for j in range(4):
                h = blk * 4 + j
                bz = self.bank()
                self.proj_fm(wz, wzk, j, N, bz)
                ba = self.bank()
                self.proj_fm(wa, wak, j, N, ba)
                t0 = self.tmpc[0]
                t1 = self.tmpc[1]
                self.act(lambda e, bz=bz: e.activation(out=t0[:, 0:N], in_=self.bk(bz)[:, 0:N], func=AF.Silu), r=[self.bkey(bz)], w=["tmpc0"])
                self.act(lambda e, ba=ba: e.activation(out=t1[:, 0:N], in_=self.bk(ba)[:, 0:N], func=AF.Sigmoid), r=[self.bkey(ba)], w=["tmpc1"])
                self.dve(lambda e, h=h: e.tensor_tensor(out=self.sz[:, h, :], in0=t0[:, 0:N], in1=t1[:, 0:N], op=ALU.mult),
                         r=["tmpc0", "tmpc1"], w=["xtok"])
        if self.stage_lim < 5:
            return
        wq0, wq0k = self.wload("w_in", OFF_QSW, 512)
        wq1, wq1k = self.wload("w_in", OFF_QSW + 512, 512)
        wkv, wkvk = self.wload("w_in", OFF_KV, 512)
        la = self.lane_swa(it, (wq0, wq0k), (wq1, wq1k), (wkv, wkvk))
        lb = self.lane_delta(it)
        a_alive = b_alive = True
        while a_alive or b_alive:
            if b_alive:
                b_alive = next(lb, "END") != "END"
            if a_alive:
                a_alive = next(la, "END") != "END"
        if self.stage_lim < 7:
            return
        for blk in range(2):
            wb, wbk = self.wload("w_in", OFF_GB + blk * 512, 512)
            for j in range(4):
                c = blk * 4 + j
                b = self.bank()
                self.proj_fm(wb, wbk, j, N, b)
                self.act(lambda e, b=b, c=c: e.activation(out=self.sgb[:, c, :], in_=self.bk(b)[:, 0:N], func=AF.Sigmoid), r=[self.bkey(b)], w=["abuf"])
        self.dump("ysw0", self.ysw[:].rearrange("p c n -> p (c n)"), ["ysw"], BF16)
        self.dump("sgb", self.sgb.rearrange("p c n -> p (c n)"), ["abuf"], BF16)
        for h in range(8):
            sqh = self.sqh[h % 2]
            sk = "sqh0"
            ri = self.rinv[h % 2]
            rk = "rinv0"
            self.act(lambda e, h=h, sqh=sqh: e.activation(out=sqh[:, 0:N], in_=self.oT[:, h, :], func=AF.Square), r=["yF"], w=[sk])
            b = self.bank()
            self.pe(lambda e, b=b, sqh=sqh: e.matmul(self.bk(b)[:, 0:N], lhsT=self.onesb[:], rhs=sqh[:, 0:N], start=True, stop=True),
                    r=[sk, "const"], w=[self.bkey(b)])
            self.act(lambda e, b=b, ri=ri: e.activation(out=ri[:, 0:N], in_=self.bk(b)[:, 0:N], func=AF.Ln, bias=self.eps6[:, 0:1], scale=1.0 / 128),
                     r=[self.bkey(b), "const"], w=[rk])
            self.act(lambda e, ri=ri: e.activation(out=ri[:, 0:N], in_=ri[:, 0:N], func=AF.Exp, scale=-0.5), r=[rk], w=[rk])
            t = self.tmpc[h % 2]
            tk = "tmpc%d" % (h % 2)
            self.dve(lambda e, h=h, t=t, ri=ri: e.scalar_tensor_tensor(out=t[:, 0:N], in0=self.oT[:, h, :], scalar=self.dnw[:, 0:1], in1=ri[:, 0:N],
                                                                       op0=ALU.mult, op1=ALU.mult), r=["yF", "dnw", rk], w=[tk])
            self.dve(lambda e, h=h, t=t: e.tensor_tensor(out=t[:, 0:N], in0=t[:, 0:N], in1=self.sz[:, h, :], op=ALU.mult), r=[tk, "xtok"], w=[tk])
            self.dve(lambda e, h=h: e.tensor_tensor(out=self.ysw[:, h, :], in0=self.ysw[:, h, :], in1=self.sgb[:, h, :], op=ALU.mult),
                     r=["ysw", "abuf"], w=["ysw"])
            self.dve(lambda e, h=h, t=t: e.tensor_tensor(out=self.yT[:, h, :], in0=t[:, 0:N], in1=self.ysw[:, h, :], op=ALU.add),
                     r=[tk, "ysw"], w=["hT"])
        self.dump("yswg", self.ysw[:].rearrange("p c n -> p (c n)"), ["ysw"], BF16)
        self.dump("yT", self.yT[:].rearrange("p c n -> p (c n)"), ["hT"], BF16)
        self.dump("oT", self.oT[:].rearrange("p c n -> p (c n)"), ["yF"])
        for blk in range(2):
            wo, wok = self.wload("w_out", blk * 512, 512)
            for j in range(4):
                dc = blk * 4 + j
                b = self.bank()
                self.proj_fm(wo, wok, j, N, b)
                self.act(lambda e, b=b, dc=dc: e.activation(out=self.yF[:, dc, 0:N], in_=self.bk(b)[:, 0:N], func=AF.Copy), r=[self.bkey(b)], w=["yF"])
        if last:
            for t_ in range(3):
                self.dma("sp", self.convp_o[t_].rearrange("(c p) -> p c", p=128), self.convout[:, :, t_], r=["convout"], sem="cvo%d" % t_, output=True, nc_ok=True)
            self.dma("sp", self.deltap_o.rearrange("h k v -> k h v"), self.S[:], r=["S"], sem="dlo", output=True)

    def unit_ctx(self, it, u):
        cols = slice(u * 128, (u + 1) * 128)
        blk = it * 4 + u
        last = (it == self.NPT - 1 and u == 3)
        cur = blk % 2
        prv = 1 - cur
        smp = it < 0
        sq = 4 * (-1 - it) + u if smp else None
        if smp:
            cur, prv = 0, 1
        return cols, blk, last, cur, prv, smp, sq

    def unit_gates(self, it, u):
        cols, blk, last, cur, prv, smp, sq = self.unit_ctx(it, u)
        bba = self.bank()
        for kc in range(8):
            self.pe(lambda e, kc=kc: e.matmul(self.bk(bba)[:, 0:16], lhsT=self.hT[:, kc, cols], rhs=self.wba[:, kc, :], start=(kc == 0), stop=(kc == 7)),
                    r=["hT", "wba"], w=[self.bkey(bba)])
        self.dve(lambda e: e.tensor_copy(out=self.ba_sb[:], in_=self.bk(bba)[:, 0:16]), r=[self.bkey(bba)], w=["ba_sb"])
        self.gate_scalars()
        if smp:
            self.dve(lambda e: e.tensor_scalar_mul(out=self.b_tok[:], in0=self.b_tok[:], scalar1=self.padm[:, 0:1]), r=["b_tok", "padm"], w=["b_tok"])
            self.dve(lambda e: e.tensor_scalar_mul(out=self.nb_tok[:], in0=self.nb_tok[:], scalar1=self.padm[:, 0:1]), r=["nb_tok", "padm"], w=["nb_tok"])
            self.dve(lambda e: e.tensor_scalar_mul(out=self.g_tok[:], in0=self.g_tok[:], scalar1=self.padm[:, 0:1]), r=["g_tok", "padm"], w=["g_tok"])
            self.dma("sp", self.S[:], self.sdelta[sq].rearrange("h k v -> k h v"), w=["S"], sem="sld")
            self.act(lambda e: e.activation(out=self.Sb[:], in_=self.S[:], func=AF.Copy), r=["S"], w=["Sb"])

    def lane_swa(self, it, wq0, wq1, wkv):
        for u in range(4):
            cols, blk, last, cur, prv, smp, sq = self.unit_ctx(it, u)
            yield from self.swa_unit_prompt(it, u, wq0, wq1, wkv, cols, blk, cur, prv, last, sq=sq)
            yield

    def lane_delta(self, it):
        for u in range(4):
            cols, blk, last, cur, prv, smp, sq = self.unit_ctx(it, u)
            self.unit_gates(it, u)
            yield
            for hg in range(2):
                yield from self.delta_unit(cols, hg, self.QT, self.KT, self.vT, "abuf", "xtok", mask_big=self.maskbig_f, strict=self.strict_f,
                                           tri=self.triT_f, nsteps=7, out_cols=cols)
                yield
            if smp:
                self.dma("sp", self.deltas_o[sq].rearrange("h k v -> k h v"), self.S[:], r=["S"], sem="sst", output=True)

    def gate_scalars(self):
        ba = self.ba_sb
        s0, s1, s2, s3 = self.sp_t
        K = ["gsc"]
        self.act(lambda e: e.activation(out=self.b_tok[:], in_=ba[:, 0:8], func=AF.Sigmoid), r=["ba_sb"], w=["b_tok"])
        self.dve(lambda e: e.tensor_scalar_mul(out=self.nb_tok[:], in0=self.b_tok[:], scalar1=-1.0), r=["b_tok"], w=["nb_tok"])
        self.dve(lambda e: e.tensor_tensor(out=s0[:], in0=ba[:, 8:16], in1=self.dtb_bc[:], op=ALU.add), r=["ba_sb", "dtb_bc"], w=K)
        self.dve(lambda e: e.scalar_tensor_tensor(out=s1[:], in0=s0[:], scalar=-1.0, in1=s0[:], op0=ALU.mult, op1=ALU.max), r=K, w=K)
        self.act(lambda e: e.activation(out=s2[:], in_=s1[:], func=AF.Exp, scale=-1.0), r=K, w=K)
        self.dve(lambda e: e.tensor_scalar_add(out=s2[:], in0=s2[:], scalar1=1.0), r=K, w=K)
        self.act(lambda e: e.activation(out=s2[:], in_=s2[:], func=AF.Ln), r=K, w=K)
        self.dve(lambda e: e.tensor_scalar_max(out=s3[:], in0=s0[:], scalar1=0.0), r=K, w=K)
        self.dve(lambda e: e.tensor_tensor(out=s3[:], in0=s3[:], in1=s2[:], op=ALU.add), r=K, w=K)
        self.dve(lambda e: e.tensor_tensor(out=self.g_tok[:], in0=s3[:], in1=self.nexpA[:], op=ALU.mult), r=K + ["nexpA"], w=["g_tok"])

    def delta_unit(self, cols, hg, QT, KT, vT, qk_key, v_key, mask_big, strict, tri, nsteps, out_cols, smp=False):
        H4 = slice(hg * 4, hg * 4 + 4)
        hs = [hg * 4 + i for i in range(4)]
        PE, DVE, ACT = self.pe, self.dve, self.act
        self._du = getattr(self, "_du", -1) + 1
        first = (self._du == 0)
        def DU(name, ap, keys, dt=F32):
            if first:
                self.dump(name, ap, keys, dt)
        C = "const"
        bG = self.bank()
        PE(lambda e: e.matmul(self.bk(bG)[:, 0:4], lhsT=tri[:], rhs=self.g_tok[:, H4], start=True, stop=True), r=[C, "g_tok"], w=[self.bkey(bG)])
        PE(lambda e: e.matmul(self.bk(bG)[:, 8:12], lhsT=self.ones_f[:], rhs=self.g_tok[:, H4], start=True, stop=True), r=[C, "g_tok"], w=[self.bkey(bG)])
        DVE(lambda e: e.tensor_copy(out=self.Gcol[:, H4], in_=self.bk(bG)[:, 0:4]), r=[self.bkey(bG)], w=["Gcol"])
        DVE(lambda e: e.tensor_copy(out=self.Glast[:, H4], in_=self.bk(bG)[:, 8:12]), r=[self.bkey(bG)], w=["Glast"])
        yield
        DVE(lambda e: e.tensor_copy(out=self.Xg[:], in_=bc_last(self.g_tok[:, H4], 128)), r=["g_tok"], w=["Xg"])
        bR = self.bank()
        for i in range(4):
            PE(lambda e, i=i: e.matmul(self.bk(bR)[:, i * 128:(i + 1) * 128], lhsT=self.Xg[:, i, :], rhs=tri[:], start=True, stop=True),
               r=["Xg", C], w=[self.bkey(bR)])
        bR3 = self.bk(bR).rearrange("p (h j) -> p h j", j=128)
        ACT(lambda e: e.activation(out=self.eGrow[:], in_=bR3, func=AF.Exp), r=[self.bkey(bR)], w=["eGrow"])
        DVE(lambda e: e.tensor_tensor(out=self.dd[:], in0=bR3, in1=bc_last(self.Gcol[:, H4], 128), op=ALU.subtract), r=[self.bkey(bR), "Gcol"], w=["dd"])
        DVE(lambda e: e.tensor_tensor(out=self.dd[:], in0=self.dd[:], in1=bc_mid(mask_big[:], 4), op=ALU.add), r=["dd", C], w=["dd"])
        ACT(lambda e: e.activation(out=self.dec[:], in_=self.dd[:], func=AF.Exp, scale=-1.0), r=["dd"], w=["dec"])
        yield
        ACT(lambda e: e.activation(out=self.eG[:, H4], in_=self.Gcol[:, H4], func=AF.Exp), r=["Gcol"], w=["eG"])
        DVE(lambda e: e.tensor_tensor(out=self.beG[:, H4], in0=self.eG[:, H4], in1=self.b_tok[:, H4], op=ALU.mult), r=["eG", "b_tok"], w=["beG"])
        DVE(lambda e: e.tensor_tensor(out=self.kdc[:, H4], in0=self.Glast[:, H4], in1=self.Gcol[:, H4], op=ALU.subtract), r=["Glast", "Gcol"], w=["kdc"])
        ACT(lambda e: e.activation(out=self.kdc[:, H4], in_=self.kdc[:, H4], func=AF.Exp), r=["kdc"], w=["kdc"])
        ACT(lambda e: e.activation(out=self.eGlast[:, H4], in_=self.Glast[:, H4], func=AF.Exp), r=["Glast"], w=["eGlast"])
        DU("g_tok", self.g_tok[:], ["g_tok"]); DU("b_tok", self.b_tok[:], ["b_tok"]); DU("Gcol", self.Gcol[:], ["Gcol"]); DU("Glast", self.Glast[:], ["Glast"])
        DU("dec", self.dec[:].rearrange("p h j -> p (h j)"), ["dec"])
        DU("eGrow", self.eGrow[:].rearrange("p h j -> p (h j)"), ["eGrow"])
        DVE(lambda e: e.tensor_tensor(out=self.nbs[:], in0=self.dec[:], in1=bc_mid(strict[:], 4), op=ALU.mult), r=["dec", C, "dd"], w=["dd"])
        DVE(lambda e: e.tensor_tensor(out=self.nbs[:], in0=self.nbs[:], in1=bc_last(self.nb_tok[:, H4], 128), op=ALU.mult), r=["dd", "nb_tok"], w=["dd"])
        bT = self.bank()
        for i, h in enumerate(hs):
            PE(lambda e, i=i, h=h: e.transpose(out=self.bkb(bT)[:, i * 128:(i + 1) * 128], in_=KT[:, h, cols], identity=self.identb[:]),
               r=[qk_key, C], w=[self.bkey(bT)])
        for i, h in enumerate(hs):
            PE(lambda e, i=i, h=h: e.transpose(out=self.bkb(bT)[:, 512 + i * 128:512 + (i + 1) * 128], in_=vT[:, h, cols], identity=self.identb[:]),
               r=[v_key, C], w=[self.bkey(bT)])
        kt3 = self.bkb(bT)[:, 0:512].rearrange("p (h j) -> p h j", j=128)
        vt3 = self.bkb(bT)[:, 512:1024].rearrange("p (h j) -> p h j", j=128)
        DVE(lambda e: e.tensor_tensor(out=self.Kbg[:], in0=kt3, in1=bc_last(self.beG[:, H4], 128), op=ALU.mult), r=[self.bkey(bT), "beG"], w=["Kbg"])
        DVE(lambda e: e.tensor_tensor(out=self.Kdec[:], in0=kt3, in1=bc_last(self.kdc[:, H4], 128), op=ALU.mult), r=[self.bkey(bT), "kdc"], w=["Kdec"])
        DVE(lambda e: e.tensor_tensor(out=self.Vb[:], in0=vt3, in1=bc_last(self.b_tok[:, H4], 128), op=ALU.mult), r=[self.bkey(bT), "b_tok"], w=["Vb"])
        yield
        bGr = self.bank()
        for i, h in enumerate(hs):
            PE(lambda e, i=i, h=h: e.matmul(self.bk(bGr)[:, i * 128:(i + 1) * 128], lhsT=KT[:, h, cols], rhs=KT[:, h, cols], start=True, stop=True),
               r=[qk_key], w=[self.bkey(bGr)])
        Lm = self.dd
        DVE(lambda e: e.tensor_tensor(out=Lm[:], in0=self.bk(bGr).rearrange("p (h j) -> p h j", j=128), in1=self.nbs[:], op=ALU.mult),
            r=[self.bkey(bGr), "dd"], w=["dd"])
        yield
        DU("N0", Lm[:].rearrange("p h j -> p (h j)"), ["dd"])
        DU("Kbg", self.Kbg[:].rearrange("p h j -> p (h j)"), ["Kbg"], BF16)
        DU("Vb", self.Vb[:].rearrange("p h j -> p (h j)"), ["Vb"], BF16)
        bQK = self.bank()
        for i, h in enumerate(hs):
            PE(lambda e, i=i, h=h: e.matmul(self.bk(bQK)[:, i * 128:(i + 1) * 128], lhsT=QT[:, h, cols], rhs=KT[:, h, cols], start=True, stop=True),
               r=[qk_key], w=[self.bkey(bQK)])
        DVE(lambda e: e.tensor_tensor(out=self.Amat[:], in0=self.bk(bQK).rearrange("p (h j) -> p h j", j=128), in1=self.dec[:], op=ALU.mult),
            r=[self.bkey(bQK), "dec"], w=["Amat"])
        yield
        DVE(lambda e: e.tensor_tensor(out=self.QgT[:], in0=QT[:, H4, cols], in1=self.eGrow[:], op=ALU.mult), r=[qk_key, "eGrow"], w=["QgT"])
        bTT = self.bank()
        for i in range(4):
            PE(lambda e, i=i: e.transpose(out=self.bkb(bTT)[:, i * 128:(i + 1) * 128], in_=self.Amat[:, i, :], identity=self.identb[:]),
               r=["Amat", C], w=[self.bkey(bTT)])
        ACT(lambda e: e.activation(out=self.ATm[:], in_=self.bkb(bTT)[:, 0:512].rearrange("p (h j) -> p h j", j=128), func=AF.Copy),
            r=[self.bkey(bTT)], w=["ATm"])
        LTm = self.Xg
        bLT = self.bank()
        for i in range(4):
            PE(lambda e, i=i: e.transpose(out=self.bk(bLT)[:, i * 128:(i + 1) * 128], in_=Lm[:, i, :], identity=self.ident_f[:]), r=["dd", C], w=[self.bkey(bLT)])
        ACT(lambda e: e.activation(out=LTm[:], in_=self.bk(bLT).rearrange("p (h j) -> p h j", j=128), func=AF.Copy), r=[self.bkey(bLT)], w=["Xg"])
        yield
        rb = 6 + hg
        RK = self.bkey(rb)
        TT = self.eGrow
        Tn = self.dec
        Yb = self.U
        m = self.lvl_masks
        nbase = 4 if not smp else 2
        DVE(lambda e: e.tensor_tensor(out=self.Nm[0][:], in0=Lm[:], in1=bc_mid(m[0][:], 4), op=ALU.mult), r=["dd", C], w=["Nm0"])
        DVE(lambda e: e.tensor_tensor(out=self.NTm[0][:], in0=LTm[:], in1=bc_mid(m[0][:], 4), op=ALU.mult), r=["Xg", C], w=["NTm0"])
        DVE(lambda e: e.tensor_tensor(out=TT[:], in0=self.NTm[0][:], in1=bc_mid(self.ident_f[:], 4), op=ALU.add), r=["NTm0", C, "QgT"], w=["eGrow"])
        for i in range(4):
            PE(lambda e, i=i: e.matmul(self.bk(rb)[:, i * 128:(i + 1) * 128], lhsT=self.ident_f[:], rhs=TT[:, i, :], start=(i == 0), stop=False,
                                       skip_group_check=True), r=["eGrow", C], w=[RK])
        for k in range(1, nbase):
            pN, pNT = self.Nm[(k - 1) % 2], self.NTm[(k - 1) % 2]
            cN, cNT = self.Nm[k % 2], self.NTm[k % 2]
            pNk, pNTk = "Nm%d" % ((k - 1) % 2), "NTm%d" % ((k - 1) % 2)
            cNk, cNTk = "Nm%d" % (k % 2), "NTm%d" % (k % 2)
            bn = self.bank()
            for i in range(4):
                PE(lambda e, i=i, pN=pN, pNT=pNT, bn=bn: e.matmul(self.bk(bn)[:, i * 128:(i + 1) * 128], lhsT=pNT[:, i, :], rhs=pN[:, i, :], start=True, stop=True),
                   r=[pNk, pNTk], w=[self.bkey(bn)])
            DVE(lambda e, cN=cN, bn=bn: e.tensor_copy(out=cN[:], in_=self.bk(bn).rearrange("p (h j) -> p h j", j=128)), r=[self.bkey(bn)], w=[cNk])
            if k < nbase - 1:
                bnt = self.bank()
                for i in range(4):
                    PE(lambda e, i=i, pN=pN, pNT=pNT, bnt=bnt: e.matmul(self.bk(bnt)[:, i * 128:(i + 1) * 128], lhsT=pN[:, i, :], rhs=pNT[:, i, :], start=True, stop=True),
                       r=[pNk, pNTk], w=[self.bkey(bnt)])
                ACT(lambda e, cNT=cNT, bnt=bnt: e.activation(out=cNT[:], in_=self.bk(bnt).rearrange("p (h j) -> p h j", j=128), func=AF.Copy),
                    r=[self.bkey(bnt)], w=[cNTk])
            lastk = (k == nbase - 1)
            for i in range(4):
                PE(lambda e, i=i, cN=cN, lastk=lastk: e.matmul(self.bk(rb)[:, i * 128:(i + 1) * 128], lhsT=cN[:, i, :], rhs=TT[:, i, :], start=False, stop=lastk,
                                                               skip_group_check=True), r=[cNk, "eGrow", RK], w=[RK])
            DVE(lambda e: e.tensor_copy(out=TT[:], in_=self.bk(rb).rearrange("p (h j) -> p h j", j=128)), r=[RK], w=["eGrow"])
            yield
        if not smp:
            bt_ = self.bank()
            for i in range(4):
                PE(lambda e, i=i: e.transpose(out=self.bk(bt_)[:, i * 128:(i + 1) * 128], in_=TT[:, i, :], identity=self.ident_f[:]), r=["eGrow", C], w=[self.bkey(bt_)])
            ACT(lambda e: e.activation(out=Tn[:], in_=self.bk(bt_).rearrange("p (h j) -> p h j", j=128), func=AF.Copy), r=[self.bkey(bt_), "Amat"], w=["dec"])
            yield
            Tn_b = self.Nm[0].bitcast(BF16)[:, :, 0:128]
            TT_b = self.Nm[1].bitcast(BF16)[:, :, 0:128]
            Yb_b = self.Rb
            ACT(lambda e: e.activation(out=Tn_b, in_=Tn[:], func=AF.Copy), r=["dec"], w=["Nm0"])
            ACT(lambda e: e.activation(out=TT_b, in_=TT[:], func=AF.Copy), r=["eGrow"], w=["Nm1"])
            for l in range(1, 4):
                BlT = self.NTm[l % 2].bitcast(BF16)[:, :, 0:128]
                BlTk = "NTm%d" % (l % 2)
                DVE(lambda e, l=l, BlT=BlT: e.tensor_tensor(out=BlT, in0=LTm[:], in1=bc_mid(m[l][:], 4), op=ALU.mult), r=["Xg", C], w=[BlTk])
                by = self.bank()
                for i in range(4):
                    PE(lambda e, i=i, BlT=BlT, by=by: e.matmul(self.bk(by)[:, i * 128:(i + 1) * 128], lhsT=BlT[:, i, :], rhs=Tn_b[:, i, :], start=True, stop=True),
                       r=[BlTk, "Nm0"], w=[self.bkey(by)])
                ACT(lambda e, by=by: e.activation(out=Yb_b[:], in_=self.bk(by).rearrange("p (h j) -> p h j", j=128), func=AF.Copy), r=[self.bkey(by)], w=["Rb"])
                yield
                bzt = self.bank()
                for i in range(4):
                    PE(lambda e, i=i, bzt=bzt: e.matmul(self.bk(bzt)[:, i * 128:(i + 1) * 128], lhsT=Yb_b[:, i, :], rhs=TT_b[:, i, :], start=True, stop=True),
                       r=["Rb", "Nm1"], w=[self.bkey(bzt)])
                if l < 3:
                    bz = self.bank()
                    for i in range(4):
                        PE(lambda e, i=i, bz=bz: e.matmul(self.bk(bz)[:, i * 128:(i + 1) * 128], lhsT=TT_b[:, i, :], rhs=Yb_b[:, i, :], start=True, stop=True),
                           r=["Rb", "Nm1"], w=[self.bkey(bz)])
                    DVE(lambda e, bz=bz: e.tensor_tensor(out=Tn[:], in0=Tn[:], in1=self.bk(bz).rearrange("p (h j) -> p h j", j=128), op=ALU.add),
                        r=["dec", self.bkey(bz)], w=["dec"])
                    ACT(lambda e: e.activation(out=Tn_b, in_=Tn[:], func=AF.Copy), r=["dec"], w=["Nm0"])
                DVE(lambda e, bzt=bzt: e.tensor_tensor(out=TT[:], in0=TT[:], in1=self.bk(bzt).rearrange("p (h j) -> p h j", j=128), op=ALU.add),
                    r=["eGrow", self.bkey(bzt)], w=["eGrow"])
                if l < 3:
                    ACT(lambda e: e.activation(out=TT_b, in_=TT[:], func=AF.Copy), r=["eGrow"], w=["Nm1"])
                yield
        ACT(lambda e: e.activation(out=self.Rb[:], in_=TT[:], func=AF.Copy), r=["eGrow"], w=["Rb"])
        DU("TT", self.Rb[:].rearrange("p h j -> p (h j)"), ["Rb"], BF16)
        DU("AT", self.ATm[:].rearrange("p h j -> p (h j)"), ["ATm"], BF16)
        bU = self.bank()
        for i in range(4):
            PE(lambda e, i=i: e.matmul(self.bk(bU)[:, i * 128:(i + 1) * 128], lhsT=self.Rb[:, i, :], rhs=self.Vb[:, i, :], start=True, stop=True),
               r=["Rb", "Vb"], w=[self.bkey(bU)])
        ACT(lambda e: e.activation(out=self.U[:], in_=self.bk(bU).rearrange("p (h j) -> p h j", j=128), func=AF.Copy), r=[self.bkey(bU)], w=["U"])
        bW = self.bank()
        for i in range(4):
            PE(lambda e, i=i: e.matmul(self.bk(bW)[:, i * 128:(i + 1) * 128], lhsT=self.Kbg[:, i, :], rhs=self.Rb[:, i, :], start=True, stop=True),
               r=["Rb", "Kbg"], w=[self.bkey(bW)])
        ACT(lambda e: e.activation(out=self.WT[:], in_=self.bk(bW).rearrange("p (h j) -> p h j", j=128), func=AF.Copy), r=[self.bkey(bW)], w=["WT"])
        yield
        DU("U", self.U[:].rearrange("p h j -> p (h j)"), ["U"])
        DU("WT", self.WT[:].rearrange("p h j -> p (h j)"), ["WT"], BF16)
        DU("QgT", self.QgT[:].rearrange("p h j -> p (h j)"), ["QgT"], BF16)
        if smp:
            return
        bS = self.bank()
        for i, h in enumerate(hs):
            PE(lambda e, i=i, h=h: e.matmul(self.bk(bS)[:, i * 128:(i + 1) * 128], lhsT=self.WT[:, i, :], rhs=self.Sb[:, h, :], start=True, stop=True),
               r=["WT", "Sb"], w=[self.bkey(bS)])
        DVE(lambda e: e.tensor_tensor(out=self.Vnew[:], in0=self.U[:], in1=self.bk(bS).rearrange("p (h j) -> p h j", j=128), op=ALU.subtract),
            r=["U", self.bkey(bS)], w=["Vnew"])
        yield
        bO = self.bank()
        for i, h in enumerate(hs):
            PE(lambda e, i=i, h=h: e.matmul(self.bk(bO)[:, i * 128:(i + 1) * 128], lhsT=self.Sb[:, h, :], rhs=self.QgT[:, i, :], start=True, stop=False),
               r=["Sb", "QgT"], w=[self.bkey(bO)])
            PE(lambda e, i=i, h=h: e.matmul(self.bk(bO)[:, i * 128:(i + 1) * 128], lhsT=self.Vnew[:, i, :], rhs=self.ATm[:, i, :], start=False, stop=True),
               r=["Vnew", "ATm"], w=[self.bkey(bO)])
        ACT(lambda e: e.activation(out=self.oT[:, H4, out_cols], in_=self.bk(bO).rearrange("p (h j) -> p h j", j=128), func=AF.Copy),
            r=[self.bkey(bO)], w=["yF"])
        yield
        bD = self.bank()
        for i, h in enumerate(hs):
            PE(lambda e, i=i, h=h: e.matmul(self.bk(bD)[:, i * 128:(i + 1) * 128], lhsT=self.Kdec[:, i, :], rhs=self.Vnew[:, i, :], start=True, stop=True),
               r=["Kdec", "Vnew"], w=[self.bkey(bD)])
        DVE(lambda e: e.tensor_tensor(out=self.S[:, H4, :], in0=self.S[:, H4, :], in1=bc_last(self.eGlast[:, H4], 128), op=ALU.mult), r=["S", "eGlast"], w=["S"])
        DVE(lambda e: e.tensor_tensor(out=self.S[:, H4, :], in0=self.S[:, H4, :], in1=self.bk(bD).rearrange("p (h j) -> p h j", j=128), op=ALU.add),
            r=["S", self.bkey(bD)], w=["S"])
        ACT(lambda e: e.activation(out=self.Sb[:, H4, :], in_=self.S[:, H4, :], func=AF.Copy), r=["S"], w=["Sb"])
        DU("Vnew", self.Vnew[:].rearrange("p h j -> p (h j)"), ["Vnew"], BF16)
        DU("S1", self.S[:, H4, :].rearrange("p h j -> p (h j)"), ["S"])
        DU("oT1", self.oT[:, hg * 4, out_cols], ["yF"])

    def rope_apply(self, src3, nh, cosb, sinb, dst3, n, rk, wk, extra_dst=None):
        t0, t1, t2, t3 = [t[0:n, 0:nh, :] for t in self.rt]
        x1 = src3[:, :, 0:8]
        x2 = src3[:, :, 8:16]
        cb = bc_mid(cosb, nh)
        sb_ = bc_mid(sinb, nh)
        D_ = self.dve
        K = ["rt"]
        D_(lambda e: e.tensor_tensor(out=t0, in0=x1, in1=cb, op=ALU.mult), r=rk + ["rope"], w=K)
        D_(lambda e: e.tensor_tensor(out=t1, in0=x2, in1=sb_, op=ALU.mult), r=rk + ["rope"], w=K)
        D_(lambda e: e.tensor_tensor(out=t2, in0=x2, in1=cb, op=ALU.mult), r=rk + ["rope"], w=K)
        D_(lambda e: e.tensor_tensor(out=t3, in0=x1, in1=sb_, op=ALU.mult), r=rk + ["rope"], w=K)
        for dst in ([dst3] + ([extra_dst] if extra_dst is not None else [])):
            D_(lambda e, dst=dst: e.tensor_tensor(out=dst[:, :, 0:8], in0=t0, in1=t1, op=ALU.subtract), r=K, w=wk)
            D_(lambda e, dst=dst: e.tensor_tensor(out=dst[:, :, 8:16], in0=t2, in1=t3, op=ALU.add), r=K, w=wk)

    def swa_unit_prompt(self, it, u, wq0, wq1, wkv, cols, blk, cur, prv, last, sq=None):
        PE, DVE, ACT = self.pe, self.dve, self.act
        C = "const"
        smp = sq is not None
        if smp:
            KTp_, KTpk_ = self.KTs[prv], "KTs%d" % prv
            self.dma("sp", self.kf32[:].rearrange("p h d -> p (h d)"), self.ck[sq], w=["kf32"], sem="ckl")
            for dpl in range(2):
                DVE(lambda e, dpl=dpl: e.tensor_copy(out=self.k_pad[:, :, dpl, dpl * 64:(dpl + 1) * 64], in_=self.kf32[:]), r=["kf32"], w=["k_pad"])
            btc = self.bank()
            for gi in range(8):
                PE(lambda e, gi=gi, btc=btc: e.transpose(out=self.bkb(btc)[:, gi * 128:(gi + 1) * 128], in_=self.k_pad[:, gi // 2, gi % 2, :], identity=self.identb[:]),
                   r=["k_pad", C], w=[self.bkey(btc)])
            ACT(lambda e, btc=btc: e.activation(out=KTp_[:], in_=self.bkb(btc).rearrange("p (c q) -> p c q", q=128), func=AF.Copy), r=[self.bkey(btc)], w=[KTpk_])
            self.dma("sp", self.vf32.rearrange("p h d -> p (h d)"), self.cv[sq], w=["osw"], sem="cvl")
            DVE(lambda e: e.tensor_copy(out=self.Vs[prv][:], in_=self.vf32.rearrange("p h d -> p (h d)")), r=["osw"], w=["Vs%d" % prv])
            self.dma("sp", self.ks_o[sq, 0:124, :], self.ck[sq, 4:128, :], sem="kso", output=True)
            self.dma("sp", self.vs_o[sq, 0:124, :], self.cv[sq, 4:128, :], sem="vso", output=True)
            yield
        bq = [self.bank(), self.bank()]
        for half, (wt, wk) in enumerate([wq0, wq1]):
            for kc in range(8):
                PE(lambda e, kc=kc, half=half, wt=wt: e.matmul(self.bk(bq[half])[:, :], lhsT=self.hT[:, kc, cols], rhs=wt[:, kc, :], start=(kc == 0), stop=(kc == 7)),
                   r=["hT", wk], w=[self.bkey(bq[half])])
        bkv = self.bank()
        for kc in range(8):
            PE(lambda e, kc=kc: e.matmul(self.bk(bkv)[:, :], lhsT=self.hT[:, kc, cols], rhs=wkv[0][:, kc, :], start=(kc == 0), stop=(kc == 7)),
               r=["hT", wkv[1]], w=[self.bkey(bkv)])
        if SUB < 3:
            return
        cosb = self.cos_t[:, blk, :] if not smp else self.cos_s[:, 0, :]
        sinb = self.sin_t[:, blk, :] if not smp else self.sin_s[:, 0, :]
        for half in range(2):
            ACT(lambda e, half=half: e.activation(out=self.q_tok[:, half * 8:(half + 1) * 8, :], in_=self.bk(bq[half]).rearrange("p (h d) -> p h d", d=64), func=AF.Copy),
                r=[self.bkey(bq[half])], w=["q_tok"])
            self.rope_apply(self.bk(bq[half]).rearrange("p (h d) -> p h d", d=64), 8, cosb, sinb, self.q_tok[:, half * 8:(half + 1) * 8, :], 128,
                            [self.bkey(bq[half])], ["q_tok"])
        if SUB < 3.3:
            return
        k3 = self.bk(bkv)[:, 0:256].rearrange("p (h d) -> p h d", d=64)
        ACT(lambda e: e.activation(out=self.kf32[:], in_=k3, func=AF.Copy), r=[self.bkey(bkv)], w=["kf32"])
        self.rope_apply(k3, 4, cosb, sinb, self.kf32[:], 128, [self.bkey(bkv)], ["kf32"])
        for dpl in range(2):
            DVE(lambda e, dpl=dpl: e.tensor_copy(out=self.k_pad[:, :, dpl, dpl * 64:(dpl + 1) * 64], in_=self.kf32[:]), r=["kf32"], w=["k_pad"])
        if SUB < 3.6:
            return
        Vc, Vck = self.Vs[cur], "Vs%d" % cur
        Vp, Vpk = self.Vs[prv], "Vs%d" % prv
        ACT(lambda e: e.activation(out=Vc[:], in_=self.bk(bkv)[:, 256:512], func=AF.Copy), r=[self.bkey(bkv)], w=[Vck])
        if SUB < 3.8:
            return
        if smp:
            DVE(lambda e: e.tensor_copy(out=self.vf32, in_=self.bk(bkv)[:, 256:512].rearrange("p (h d) -> p h d", d=64)), r=[self.bkey(bkv)], w=["osw"])
            self.dma("sp", self.ks_o[sq, 124:128, :], self.kf32[0:4].rearrange("p h d -> p (h d)"), r=["kf32"], sem="kso", output=True)
            self.dma("sp", self.vs_o[sq, 124:128, :], self.vf32[0:4].rearrange("p h d -> p (h d)"), r=["osw"], sem="vso", output=True)
        if last:
            DVE(lambda e: e.tensor_copy(out=self.vf32, in_=self.bk(bkv)[:, 256:512].rearrange("p (h d) -> p h d", d=64)), r=[self.bkey(bkv)], w=["osw"])
            self.dma("sp", self.kp_o[:, :], self.kf32[:].rearrange("p h d -> p (h d)"), r=["kf32"], sem="kvo", output=True)
            self.dma("sp", self.vp_o[:, :], self.vf32.rearrange("p h d -> p (h d)"), r=["osw"], sem="kvo", output=True)
        if SUB < 4:
            return
        yield
        bt = self.bank()
        for c in range(8):
            PE(lambda e, c=c: e.transpose(out=self.bkb(bt)[:, c * 128:(c + 1) * 128], in_=self.q_tok[:, 2 * c:2 * c + 2, :].rearrange("p a d -> p (a d)"),
                                          identity=self.identb[:]), r=["q_tok", C], w=[self.bkey(bt)])
        ACT(lambda e: e.activation(out=self.QTs[:], in_=self.bkb(bt).rearrange("p (c q) -> p c q", q=128), func=AF.Copy), r=[self.bkey(bt)], w=["QTs"])
        KTc, KTck = self.KTs[cur], "KTs%d" % cur
        KTp, KTpk = self.KTs[prv], "KTs%d" % prv
        bt2 = self.bank()
        for gi in range(8):
            PE(lambda e, gi=gi: e.transpose(out=self.bkb(bt2)[:, gi * 128:(gi + 1) * 128], in_=self.k_pad[:, gi // 2, gi % 2, :], identity=self.identb[:]),
               r=["k_pad", C], w=[self.bkey(bt2)])
        ACT(lambda e: e.activation(out=KTc[:], in_=self.bkb(bt2).rearrange("p (c q) -> p c q", q=128), func=AF.Copy), r=[self.bkey(bt2)], w=[KTck])
        yield
        if SUB < 5:
            return
        mrow = self.mrow0 if (blk == 0 and not smp) else self.mrow
        bo = [6, 7]
        for g in range(4):
            DVE(lambda e: e.memset(self.dacc[:], 0.0), w=["dacc"])
            bs = [self.bank(), self.bank()]
            for r_ in range(4):
                h = 4 * g + r_
                sb_ = bs[r_ // 2]
                o0 = (r_ % 2) * 256
                lq = self.QTs[:, h // 2, :]
                kidx = g * 2 + (h % 2)
                PE(lambda e, lq=lq, sb_=sb_, o0=o0, kidx=kidx: e.matmul(self.bk(sb_)[:, o0:o0 + 128], lhsT=lq, rhs=KTp[:, kidx, :], start=True, stop=True),
                   r=["QTs", KTpk], w=[self.bkey(sb_)])
                PE(lambda e, lq=lq, sb_=sb_, o0=o0, kidx=kidx: e.matmul(self.bk(sb_)[:, o0 + 128:o0 + 256], lhsT=lq, rhs=KTc[:, kidx, :], start=True, stop=True),
                   r=["QTs", KTck], w=[self.bkey(sb_)])
            scs = []
            for hp in range(2):
                t = self.tmpc[hp]
                DVE(lambda e, hp=hp, t=t, bs=bs: e.tensor_tensor(out=t[:, :].rearrange("p (a s) -> p a s", s=256), in0=self.bk(bs[hp]).rearrange("p (a s) -> p a s", s=256),
                                                          in1=bc_mid(mrow[:], 2), op=ALU.add), r=[self.bkey(bs[hp]), C], w=["tmpc%d" % hp])
                scs.append(t)
            yield
            if SUB < 6:
                continue
            if blk == 1 and g == 1 and "g1_sc0" in DEBUG:
                self.dump("g1_sc0", self.tmpc[0][:, :], ["tmpc0"])
                self.dump("g1_sc1", self.tmpc[1][:, :], ["tmpc1"])
            for hp in range(2):
                DVE(lambda e, hp=hp: e.tensor_reduce(out=self.mx[:, hp * 2:hp * 2 + 2], in_=self.tmpc[hp][:, :].rearrange("p (a s) -> p a s", s=256), axis=AX.X, op=ALU.max),
                    r=["tmpc%d" % hp], w=["mx"])
            DVE(lambda e, g=g: e.scalar_tensor_tensor(out=self.nm[:], in0=self.mx[:], scalar=0.125, in1=self.sinks_bc[:, 4 * g:4 * g + 4], op0=ALU.mult, op1=ALU.max),
                r=["mx", "sinks_bc"], w=["nm"])
            DVE(lambda e: e.tensor_scalar_mul(out=self.nm[:], in0=self.nm[:], scalar1=-1.0), r=["nm"], w=["nm"])
            for r_ in range(4):
                sb_ = bs[r_ // 2]
                o0 = (r_ % 2) * 256
                ACT(lambda e, r_=r_, o0=o0, g=g: e.activation(out=self.Eb[:, r_, :], in_=self.tmpc[r_ // 2][:, o0:o0 + 256], func=AF.Exp, bias=self.nm[:, r_:r_ + 1],
                                                                scale=0.125, accum_out=self.dacc[:, r_, 0:1]),
                    r=["tmpc%d" % (r_ // 2), "nm"], w=["Eb", "dacc"])
            DVE(lambda e, g=g: e.tensor_tensor(out=self.esk[:], in0=self.sinks_bc[:, 4 * g:4 * g + 4], in1=self.nm[:], op=ALU.add), r=["sinks_bc", "nm"], w=["esk"])
            ACT(lambda e: e.activation(out=self.esk[:], in_=self.esk[:], func=AF.Exp), r=["esk"], w=["esk"])
            DVE(lambda e, g=g: e.tensor_tensor(out=self.den[:, 4 * g:4 * g + 4], in0=self.dacc[:, :, 0], in1=self.esk[:], op=ALU.add), r=["dacc", "esk"], w=["den"])
            if SUB < 7:
                continue
            if blk == 1 and g == 0:
                self.dump("den_g0", self.den[:], ["den"])
                self.dump("nm_g0", self.nm[:], ["nm"])
                self.dump("mx_g0", self.mx[:], ["mx"])
            be = self.bank()
            for r_ in range(4):
                for kb in range(2):
                    PE(lambda e, r_=r_, kb=kb, be=be: e.transpose(out=self.bkb(be)[:, r_ * 256 + kb * 128:r_ * 256 + (kb + 1) * 128], in_=self.Eb[:, r_, kb * 128:(kb + 1) * 128],
                                                           identity=self.identb[:]), r=["Eb", C], w=[self.bkey(be)])
            DVE(lambda e, be=be: e.tensor_copy(out=self.ETb[:], in_=self.bkb(be).rearrange("p (r s) -> p r s", s=256)), r=[self.bkey(be)], w=["ETb"])
            yield
            ob = self.bank()
            for r_ in range(4):
                oc = r_ * 64
                PE(lambda e, r_=r_, ob=ob, oc=oc, g=g: e.matmul(self.bk(ob)[:, oc:oc + 64], lhsT=self.ETb[:, r_, 0:128], rhs=Vp[:, g * 64:(g + 1) * 64], start=True, stop=False),
                   r=["ETb", Vpk], w=[self.bkey(ob)])
                PE(lambda e, r_=r_, ob=ob, oc=oc, g=g: e.matmul(self.bk(ob)[:, oc:oc + 64], lhsT=self.ETb[:, r_, 128:256], rhs=Vc[:, g * 64:(g + 1) * 64], start=False, stop=True),
                   r=["ETb", Vck], w=[self.bkey(ob)])
            if blk == 1 and g == 1 and "g1_O" in DEBUG:
                self.dump("g1_Eb", self.Eb[:].rearrange("p h d -> p (h d)"), ["Eb"], BF16)
                self.dump("g1_ETb", self.ETb[:].rearrange("p h d -> p (h d)"), ["ETb"], BF16)
                self.dump("g1_den", self.den[:], ["den"])
                DVE(lambda e, ob=ob: e.tensor_copy(out=self.tmpc[0][:, 0:256], in_=self.bk(ob)[:, 0:256]), r=[self.bkey(ob)], w=["tmpc0"])
                self.dump("g1_O", self.tmpc[0][:, 0:256], ["tmpc0"])
            DVE(lambda e, g=g: e.reciprocal(out=self.rden[:, 4 * g:4 * g + 4], in_=self.den[:, 4 * g:4 * g + 4]), r=["den"], w=["rden"])
            DVE(lambda e, g=g, ob=ob: e.tensor_tensor(out=self.osw[:, 4 * g:4 * g + 4, :], in0=self.bk(ob)[:, 0:256].rearrange("p (h d) -> p h d", d=64),
                                                      in1=bc_last(self.rden[:, 4 * g:4 * g + 4], 64), op=ALU.mult),
                r=[self.bkey(ob), "rden"], w=["osw"])
            yield
        if blk == 1:
            self.dump("osw", self.osw[:].rearrange("p h d -> p (h d)"), ["osw"], BF16)
        bt3 = self.bank()
        for c in range(8):
            PE(lambda e, c=c: e.transpose(out=self.bkb(bt3)[:, c * 128:(c + 1) * 128], in_=self.osw[:, 2 * c:2 * c + 2, :].rearrange("p a d -> p (a d)"),
                                          identity=self.identb[:]), r=["osw", C], w=[self.bkey(bt3)])
        ACT(lambda e: e.activation(out=self.ysw[:, :, cols], in_=self.bkb(bt3).rearrange("p (c q) -> p c q", q=128), func=AF.Copy), r=[self.bkey(bt3)], w=["ysw"])

    def sample_tile(self):
        for j in range(NS // 4):
            self.sample_ptile(j)

    def sample_ptile(self, j):
        N = NT
        seqs = [(u * 128, (u + 1) * 128, 1 + 4 * j + u) for u in range(4)]
        self.smp_j = j
        stg = self.yF[:, :, :].rearrange("p c n -> p (c n)")
        self.dma("sp", stg[0:12, 0:3072], self.sconv[4 * j:4 * j + 4].rearrange("s t c -> (s t) c"), w=["yF"], sem="scv")
        for cc in range(24):
            b = self.bank()
            self.pe(lambda e, cc=cc, b=b: e.transpose(out=self.bk(b)[:, 0:12], in_=stg[0:12, cc * 128:(cc + 1) * 128], identity=self.ident_f[0:12, 0:12]),
                    r=["yF", "const"], w=[self.bkey(b)])
            self.dve(lambda e, cc=cc, b=b: e.tensor_copy(out=self.hist4[:, cc, :], in_=self.bk(b)[:, 0:12]), r=[self.bkey(b)], w=["hist4"])
        self.dve(lambda e: e.memset(self.xtok[:], 0.0), w=["xtok"])
        for u in range(4):
            sq = 4 * j + u
            self.dma("sp", self.xtok[0:4, u, :], self.xs[sq * 4:(sq + 1) * 4, :], w=["xtok"], sem="xin")
        self.load_x(None, 4, 128)
        self.prenorm(0, N, seqs)
        self.ffn("wg1", "wu1", "wd1", N)
        self.postnorm(0, N, seqs)
        self.prenorm(1, N, seqs)
        self.mixer_prompt(-1 - j)
        self.postnorm(1, N, seqs)
        self.prenorm(2, N, seqs)
        self.ffn("wg2", "wu2", "wd2", N)
        self.postnorm(2, N, seqs)
        self.store_x(None, 4, 128)
        for u in range(4):
            sq = 4 * j + u
            self.dma("sp", self.ys[sq * 4:(sq + 1) * 4, :], self.xtok[0:4, u, :], r=["xtok"], sem="xout", output=True)


_NC_CACHE = {}


STAGE = 99
TRACE = False
DEBUG = set()
LAST = None
SUB = 99


def _get_nc(NPT, sample):
    key = (NPT, sample)
    if key not in _NC_CACHE:
        kb = KB(NPT=NPT, sample=sample, stage=STAGE)
        nc = kb.build()
        _NC_CACHE[key] = nc
    return _NC_CACHE[key]


def _weights(inputs):
    m = {
        "w_ada": "w_ada", "b_ada": "b_ada", "n1pre": "ffn1_norm_pre", "n1post": "ffn1_norm_post",
        "wg1": "ffn1_w_gate", "wu1": "ffn1_w_up", "wd1": "ffn1_w_down", "n2pre": "mix_norm_pre", "n2post": "mix_norm_post",
        "w_in": "w_in", "conv_w": "conv_w", "a_log": "a_log", "dt_bias": "dt_bias", "dn_norm": "dn_norm", "sinks": "sinks",
        "w_out": "w_out", "n3pre": "ffn2_norm_pre", "n3post": "ffn2_norm_post", "wg2": "ffn2_w_gate", "wu2": "ffn2_w_up",
        "wd2": "ffn2_w_down",
    }
    return {k: np.ascontiguousarray(np.asarray(inputs[v], dtype=np.float32)[0]) for k, v in m.items()}


def kernel(**inputs):
    x_prompt = np.asarray(inputs["x_prompt"], dtype=np.float32)
    B, T, _ = x_prompt.shape
    NPT = T // NT
    sample = "x_sample" in inputs and inputs["x_sample"] is not None and not inputs.get("_no_sample", False)
    nc = _get_nc(NPT, sample)
    Wd = _weights(inputs)
    ncores = 8
    in_maps = []
    for c in range(ncores):
        b = c % B
        m = dict(Wd)
        m["xp"] = np.ascontiguousarray(x_prompt[b])
        m["cp"] = np.ascontiguousarray(np.asarray(inputs["c_prompt"], dtype=np.float32)[b:b + 1])
        if sample:
            sl = slice(c * NS, (c + 1) * NS)
            m["xs"] = np.ascontiguousarray(np.asarray(inputs["x_sample"], dtype=np.float32)[sl].reshape(NS * TS, D))
            m["cs"] = np.ascontiguousarray(np.asarray(inputs["c_sample"], dtype=np.float32)[sl])
            m["ck"] = np.ascontiguousarray(np.asarray(inputs["cache_swa_k"], dtype=np.float32)[0, sl].reshape(NS, 128, 256))
            m["cv"] = np.ascontiguousarray(np.asarray(inputs["cache_swa_v"], dtype=np.float32)[0, sl].reshape(NS, 128, 256))
            m["sconv"] = np.ascontiguousarray(np.asarray(inputs["state_conv"], dtype=np.float32)[0, sl])
            m["sdelta"] = np.ascontiguousarray(np.asarray(inputs["state_delta"], dtype=np.float32)[0, sl])
        in_maps.append(m)
    if TRACE:
        res = run_bass_kernel_spmd(nc, in_maps, core_ids=list(range(ncores)), trace=True)
        print("EXEC_TIME_NS", res.exec_time_ns, flush=True)
    else:
        res = run_bass_kernel_spmd(nc, in_maps, core_ids=list(range(ncores)))
    R = res.results
    global LAST
    LAST = R
    y_p = np.stack([R[b]["yp"] for b in range(B)])
    kp = np.stack([R[b]["kp"].reshape(128, 4, 64) for b in range(B)])[None]
    vp = np.stack([R[b]["vp"].reshape(128, 4, 64) for b in range(B)])[None]
    convp = np.stack([R[b]["convp"] for b in range(B)])[None]
    deltap = np.stack([R[b]["deltap"] for b in range(B)])[None]
    if not sample:
        return (y_p, kp, vp, convp, deltap)
    y_s = np.concatenate([R[c]["ys"].reshape(NS, TS, D) for c in range(ncores)])
    ks = np.concatenate([R[c]["ks"].reshape(NS, 128, 4, 64) for c in range(ncores)])[None]
    vs = np.concatenate([R[c]["vs"].reshape(NS, 128, 4, 64) for c in range(ncores)])[None]
    convs = np.concatenate([R[c]["convs"] for c in range(ncores)])[None]
    deltas = np.concatenate([R[c]["deltas"] for c in range(ncores)])[None]
    return (y_p, y_s, kp, vp, convp, deltap, ks, vs, convs, deltas)
```

```python
import numpy as np
from contextlib import ExitStack
import concourse.bass as bass
import concourse.mybir as mybir
from concourse.bass_utils import run_bass_kernel_spmd

F32 = mybir.dt.float32
BF16 = mybir.dt.bfloat16
I32 = mybir.dt.int32
AF = mybir.ActivationFunctionType
ALU = mybir.AluOpType
AX = mybir.AxisListType

ENGS = ["pe", "act", "dve", "pool", "sp"]
FUSE_WAIT = True
RELAX_SAME_ENGINE_RAW = False
SMALL_KEYS = {"gsc", "b_tok", "nb_tok", "g_tok", "ba_sb", "Gcol", "Glast", "eG", "beG", "kdc", "eGlast", "mx", "nm", "esk", "den", "rden",
              "dacc", "rope", "invf", "posf", "padm", "rt", "kf32", "Xg", "hist", "hist4", "convout", "convout4", "nexpA", "k_pad"}
BF16_SCRATCH = True


class Ev:
    __slots__ = ("op", "sem", "val")

    def __init__(self, op=None, sem=None, val=None):
        self.op = op
        self.sem = sem
        self.val = val


class Op:
    __slots__ = ("fn", "eng", "deps", "inc", "cnt", "dma", "ev")


class DmaSem:
    def __init__(self, handle):
        self.h = handle
        self.count = 0


class Prog:
    def __init__(self, nc):
        self.nc = nc
        self.streams = {e: [] for e in ENGS}
        self.res = {}
        self.final_deps = []
        self.relaxed = False

    def _state(self, k):
        st = self.res.get(k)
        if st is None:
            st = [None, {}]
            self.res[k] = st
        return st

    def op(self, eng, fn, reads=(), writes=(), dma=None, output=False, fast=False):
        pbr = [k for k in reads if isinstance(k, str) and k.startswith("pb")]
        if pbr:
            reads = [k for k in reads if k not in pbr]
            writes = list(writes) + pbr
        o = Op()
        o.fn = fn
        o.eng = eng
        o.inc = False
        o.cnt = None
        o.dma = dma
        if dma is not None:
            dma.count += 16
            o.ev = Ev(op=None, sem=dma.h, val=dma.count)
        else:
            o.ev = Ev(op=o)
        deps = []
        for k in reads:
            st = self._state(k)
            if st[0] is not None:
                deps.append((st[0], True))
        for k in writes:
            st = self._state(k)
            if st[0] is not None:
                deps.append((st[0], False))
            deps.extend((x, False) for x in st[1].values())
        fd = []
        seen = set()
        for d, raw in deps:
            if id(d) in seen:
                continue
            if d.op is not None and d.op.eng == eng:
                if not (raw and not fast and eng in ("act", "dve", "pool")):
                    continue
                if self.relaxed and not any(k in SMALL_KEYS for k in reads if self.res.get(k) and self.res[k][0] is d):
                    continue
            seen.add(id(d))
            if dma is not None and d.op is None and d.sem is dma.h:
                continue
            fd.append(d)
            if d.op is not None:
                d.op.inc = True
        o.deps = fd
        rk = ("d", id(dma.h)) if dma is not None else ("e", eng)
        for k in reads:
            self._state(k)[1][rk] = o.ev
        for k in writes:
            st = self._state(k)
            st[0] = o.ev
            st[1] = {}
        self.streams[eng].append(o)
        if output:
            self.final_deps.append(o.ev)
        return o

    def finalize(self, block, eng_sems):
        fin = Op()
        fin.fn = None
        fin.eng = "sp"
        fin.inc = False
        fin.cnt = None
        fin.dma = None
        fin.ev = Ev(op=fin)
        fin.deps = list(self.final_deps)
        for d in fin.deps:
            if d.op is not None:
                d.op.inc = True
        self.streams["sp"].append(fin)
        for e in ENGS:
            c = 0
            for o in self.streams[e]:
                if o.inc and o.dma is None:
                    c += 1
                    o.cnt = c
        self.n_waits = 0
        self.n_ops = sum(len(s) for s in self.streams.values())
        self.max_cnt = {e: max([o.cnt or 0 for o in self.streams[e]] + [0]) for e in ENGS}
        print("max sem counts", self.max_cnt, flush=True)

        def replay(ename, eng):
            waited = {}
            mysem = eng_sems[ename]
            for o in self.streams[ename]:
                pend = []
                for d in o.deps:
                    if d.op is not None:
                        sem = eng_sems[d.op.eng]
                        val = d.op.cnt
                    else:
                        sem = d.sem
                        val = d.val
                    key = id(sem)
                    if waited.get(key, 0) >= val:
                        continue
                    waited[key] = val
                    pend.append((sem, val))
                    self.n_waits += 1
                best = {}
                for sem, val in pend:
                    if id(sem) not in best or best[id(sem)][1] < val:
                        best[id(sem)] = (sem, val)
                pend = list(best.values())
                fuse = None
                if o.fn is not None and pend and FUSE_WAIT:
                    fuse = pend.pop()
                for sem, val in pend:
                    eng.wait_ge(sem, val)
                if o.fn is None:
                    continue
                ins = o.fn(eng)
                if fuse is not None:
                    ins._wait_ge(fuse[0], fuse[1])
                if o.dma is not None:
                    ins.then_inc(o.dma.h, 16)
                elif o.inc:
                    ins.then_inc(mysem, 1)

        @block.tensor
        def _(eng):
            replay("pe", eng)

        @block.scalar
        def _(eng):
            replay("act", eng)

        @block.vector
        def _(eng):
            replay("dve", eng)

        @block.gpsimd
        def _(eng):
            replay("pool", eng)

        @block.sync
        def _(eng):
            replay("sp", eng)


D = 1024
DFF = 2752
NFF = 22
DIN = 7696
NT = 512
NS = 16
TS = 4
EPS = 1e-6
OFF_Z = 3072
OFF_B = 4096
OFF_QSW = 4112
OFF_KV = 5136
OFF_GA = 5648
OFF_GB = 6672
PAST = 8192
INV_FREQ = (np.float32(500000.0) ** (-np.arange(8, dtype=np.float32) * np.float32(2.0 / 16))).astype(np.float32)
TWO_PI = 2.0 * np.pi
CW1 = 6.28125
CW2 = TWO_PI - CW1

WEIGHT_NAMES = [
    ("w_ada", [D, 9 * D]), ("b_ada", [9 * D]),
    ("n1pre", [D]), ("n1post", [D]), ("wg1", [D, DFF]), ("wu1", [D, DFF]), ("wd1", [DFF, D]),
    ("n2pre", [D]), ("n2post", [D]), ("w_in", [D, DIN]), ("conv_w", [4, 3072]),
    ("a_log", [8]), ("dt_bias", [8]), ("dn_norm", [128]), ("sinks", [16]), ("w_out", [D, D]),
    ("n3pre", [D]), ("n3post", [D]), ("wg2", [D, DFF]), ("wu2", [D, DFF]), ("wd2", [DFF, D]),
]


def bc_last(ap, n):
    a = [list(x) for x in ap.ap]
    return bass.AP(ap.tensor, ap.offset, a + [[0, n]])


def bc_mid(ap, n):
    a = [list(x) for x in ap.ap]
    return bass.AP(ap.tensor, ap.offset, [a[0], [0, n]] + a[1:])


class KB:
    def __init__(self, NPT=8, sample=True, dbg=(), stage=99):
        self.stage_lim = stage
        self.NPT = NPT
        self.sample = sample
        self.dbg = set(dbg)
        self.nc = bass.Bass("TRN2", target_bir_lowering=False)
        self.es = ExitStack()
        self.P = Prog(self.nc)
        self.dmasems = {}
        self.bank_rr = 0
        self.outs = {}

    def sb(self, name, shape, dt):
        return self.es.enter_context(self.nc.sbuf_tensor(name, shape, dt))

    def sem(self, name):
        return self.es.enter_context(self.nc.semaphore(name))

    def dsem(self, name):
        if name not in self.dmasems:
            self.dmasems[name] = DmaSem(self.sem("d_" + name))
        return self.dmasems[name]

    def din(self, name, shape, dt=F32):
        return self.nc.dram_tensor(name, list(shape), dt, kind="ExternalInput").ap()

    def dout(self, name, shape, dt=F32):
        ap = self.nc.dram_tensor(name, list(shape), dt, kind="ExternalOutput").ap()
        self.outs[name] = ap
        return ap

    def dump(self, name, ap, keys, dt=F32):
        if name not in DEBUG:
            return
        shape = list(ap.shape)
        o = self.nc.dram_tensor("dbg_" + name, shape, dt, kind="ExternalOutput").ap()
        self.dma("sp", o, ap, r=list(keys), sem="dbg_" + name, output=True)

    def pe(self, fn, r=(), w=()):
        return self.P.op("pe", fn, r, w)

    def act(self, fn, r=(), w=(), fast=False):
        return self.P.op("act", fn, r, w, fast=fast)

    def dve(self, fn, r=(), w=(), fast=False):
        return self.P.op("dve", fn, r, w, fast=fast)

    def pool(self, fn, r=(), w=()):
        return self.P.op("pool", fn, r, w)

    def dma(self, q, out, in_, r=(), w=(), sem=None, output=False, nc_ok=False):
        ds = self.dsem(sem)
        if nc_ok:
            fn = lambda e: e.dma_start(out=out, in_=in_, allow_slow_non_contiguous=True)
        else:
            fn = lambda e: e.dma_start(out=out, in_=in_)
        return self.P.op(q, fn, r, w, dma=ds, output=output)

    def bank(self):
        i = self.bank_rr
        self.bank_rr = (self.bank_rr + 1) % 6
        return i

    def bk(self, i):
        t = self.pbig[i // 2]
        return t[:, (i % 2) * 512:(i % 2) * 512 + 512]

    def bkb(self, i):
        t = self.pbigb[i // 2]
        return t[:, (i % 2) * 1024:(i % 2) * 1024 + 1024]

    @staticmethod
    def bkey(i):
        return "pb%d" % i

    def build(self):
        nc = self.nc
        NPT = self.NPT
        T = NPT * NT
        NB = NPT * 4
        self.NB = NB
        self.xp = self.din("xp", [T, D])
        self.cp = self.din("cp", [1, D])
        self.W = {n: self.din(n, s) for n, s in WEIGHT_NAMES}
        self.WB = {}
        if BF16_SCRATCH:
            for n, shp in WEIGHT_NAMES:
                if n in ("wg1", "wu1", "wd1", "w_in", "w_out", "wg2", "wu2", "wd2"):
                    self.WB[n] = self.nc.dram_tensor("wb_" + n, list(shp), BF16).ap()
        self.yp = self.dout("yp", [T, D])
        self.kp_o = self.dout("kp", [128, 256])
        self.vp_o = self.dout("vp", [128, 256])
        self.convp_o = self.dout("convp", [3, 3072])
        self.deltap_o = self.dout("deltap", [8, 128, 128])
        if self.sample:
            self.xs = self.din("xs", [NS * TS, D])
            self.cs = self.din("cs", [NS, D])
            self.ck = self.din("ck", [NS, 128, 256])
            self.cv = self.din("cv", [NS, 128, 256])
            self.sconv = self.din("sconv", [NS, 3, 3072])
            self.sdelta = self.din("sdelta", [NS, 8, 128, 128])
            self.ys = self.dout("ys", [NS * TS, D])
            self.ks_o = self.dout("ks", [NS, 128, 256])
            self.vs_o = self.dout("vs", [NS, 128, 256])
            self.convs_o = self.dout("convs", [NS, 3, 3072])
            self.deltas_o = self.dout("deltas", [NS, 8, 128, 128])
        self.dbg_o = {}

        es = self.es
        with es:
            self.pbig = [es.enter_context(nc.psum_tensor("pbig%d" % i, [128, 1024], F32)) for i in range(4)]
            self.pbigb = [t.bitcast(BF16) for t in self.pbig]
            self.alloc_sbuf()
            self.esems = {e: self.sem("s_" + e) for e in ENGS}
            block = es.enter_context(nc.Block())
            self.prologue()
            self.P.relaxed = RELAX_SAME_ENGINE_RAW
            for it in range(NPT):
                self.prompt_tile(it)
            if self.sample:
                self.sample_tile()
            self.P.finalize(block, self.esems)
            print("program ops", self.P.n_ops, "waits", self.P.n_waits,
                  {e: len(s) for e, s in self.P.streams.items()}, flush=True)
        return nc

    def alloc_sbuf(self):
        sb = self.sb
        NB = self.NB
        self.iot = sb("iot", [128, 128], F32)
        self.ident_f = sb("ident_f", [128, 128], F32)
        self.identb = sb("identb", [128, 128], BF16)
        self.onesb = sb("onesb", [128, 128], BF16)
        self.ones_f = sb("ones_f", [128, 128], F32)
        self.triT_f = sb("triT_f", [128, 128], F32)
        self.maskbig_f = sb("maskbig_f", [128, 128], F32)
        self.strict_f = sb("strict_f", [128, 128], F32)
        self.mrow = sb("mrow", [128, 256], BF16)
        self.mrow0 = sb("mrow0", [128, 256], BF16)
        self.eps6 = sb("eps6", [128, 1], F32)
        self.lvl_masks = [sb("lvlm%d" % i, [128, 128], F32) for i in range(4)]
        self.cos_t = sb("cos_t", [128, NB, 8], F32)
        self.sin_t = sb("sin_t", [128, NB, 8], F32)
        self.posf = sb("posf", [128, NB], F32)
        self.invf = sb("invf", [128, 8], F32)
        self.sinks_bc = sb("sinks_bc", [128, 16], F32)
        self.alog_bc = sb("alog_bc", [128, 8], F32)
        self.dtb_bc = sb("dtb_bc", [128, 8], F32)
        self.nexpA = sb("nexpA", [128, 8], F32)
        self.dnw = sb("dnw", [128, 1], F32)
        NSQ = 17 if self.sample else 1
        self.modT = sb("modT", [128, 72, NSQ], F32)
        self.mA = [sb("mA%d" % i, [128, 8, NSQ], F32) for i in range(3)]
        self.mG = [sb("mG%d" % i, [128, 8, NSQ], F32) for i in range(3)]
        self.gT = sb("gT", [128, 48], F32)
        self.cwT = sb("cwT", [128, 96], F32)
        self.badaT = sb("badaT", [128, 72], F32)
        self.wba = sb("wba", [128, 8, 16], BF16)
        self.scT = sb("scT", [128, 8, NSQ], BF16)
        self.xtok = sb("xtok", [128, 4, 1024], F32)
        self.stage = self.xtok[:, 0, :]
        self.xT = sb("xT", [128, 8, NT], F32)
        self.yF = sb("yF", [128, 8, NT], F32)
        self.hT = sb("hT", [128, 8, NT], BF16)
        self.tmpc = [sb("tmpc%d" % i, [128, NT], F32) for i in range(2)]
        self.rstd = sb("rstd", [128, NT], F32)
        self.abuf = sb("abuf", [128, NFF, NT], BF16)
        self.wide = [sb("wide%d" % i, [128, 8, 512], BF16) for i in range(3)]
        self.wdn = [sb("wdn%d" % i, [128, NFF, 128], BF16) for i in range(2)]
        ab = self.abuf
        self.QT = ab[:, 0:8, :]
        self.KT = ab[:, 8:16, :]
        xt_b = self.xtok.bitcast(BF16)
        self.vT = None
        self.xtok_b = xt_b
        self.vT = xt_b[:, 0:2, :].rearrange("p a (b n) -> p (a b) n", n=NT)
        self.sz = xt_b[:, 2:4, :].rearrange("p a (b n) -> p (a b) n", n=NT)
        self.sgb = ab[:, 0:8, :]
        self.oT = self.yF
        self.yT = self.hT
        self.ysw = sb("ysw", [128, 8, NT], BF16)
        self.ubuf = [sb("ubuf%d" % i, [128, 4 * 131], BF16) for i in range(2)]
        self.hist = sb("hist", [128, 24, 3], BF16)
        _c0 = sb("cdiag0", [128, 4, 128], BF16)
        self.cdiag = [_c0, _c0]
        self.convout = sb("convout", [128, 24, 3], F32)
        _q0 = sb("sqh0", [128, NT], BF16)
        self.sqh = [_q0, _q0]
        _r0 = sb("rinv0", [128, NT], F32)
        self.rinv = [_r0, _r0]
        if self.sample:
            self.hist4 = sb("hist4", [128, 24, 12], BF16)
            self.convout4 = sb("convout4", [128, 24, 12], F32)
            self.cos_s = sb("cos_s", [128, 1, 8], F32)
            self.sin_s = sb("sin_s", [128, 1, 8], F32)
            self.pos_s = sb("pos_s", [128, 1], F32)
            self.padm = sb("padm", [128, 1], F32)
        f = lambda n, s, d: sb(n, s, d)
        self.ba_sb = f("ba_sb", [128, 16], F32)
        self.g_tok = f("g_tok", [128, 8], F32)
        self.b_tok = f("b_tok", [128, 8], F32)
        self.nb_tok = f("nb_tok", [128, 8], F32)
        self.sp_t = [f("sp_t%d" % i, [128, 8], F32) for i in range(4)]
        self.Gcol = f("Gcol", [128, 8], F32)
        self.eG = f("eG", [128, 8], F32)
        self.beG = f("beG", [128, 8], F32)
        self.Glast = f("Glast", [128, 8], F32)
        self.eGlast = f("eGlast", [128, 8], F32)
        self.kdc = f("kdc", [128, 8], F32)
        self.Xg = f("Xg", [128, 4, 128], F32)
        self.dd = f("dd", [128, 4, 128], F32)
        self.dec = f("dec", [128, 4, 128], F32)
        self.nbs = self.dd
        self.eGrow = f("eGrow", [128, 4, 128], F32)
        self.Kbg = f("Kbg", [128, 4, 128], BF16)
        self.Kdec = f("Kdec", [128, 4, 128], BF16)
        self.Vb = f("Vb", [128, 4, 128], BF16)
        self.Nm = [f("Nm%d" % i, [128, 4, 128], F32) for i in range(2)]
        self.NTm = [f("NTm%d" % i, [128, 4, 128], F32) for i in range(2)]
        self.Rb = f("Rb", [128, 4, 128], BF16)
        self.Amat = f("Amat", [128, 4, 128], BF16)
        self.ATm = f("ATm", [128, 4, 128], BF16)
        self.U = f("U", [128, 4, 128], F32)
        self.WT = f("WT", [128, 4, 128], BF16)
        self.QgT = f("QgT", [128, 4, 128], BF16)
        self.Vnew = f("Vnew", [128, 4, 128], BF16)
        self.S = f("S", [128, 8, 128], F32)
        self.Sb = f("Sb", [128, 8, 128], BF16)
        self.q_tok = f("q_tok", [128, 16, 64], BF16)
        self.k_pad = f("k_pad", [128, 4, 2, 128], BF16)
        self.kf32 = f("kf32", [128, 4, 64], F32)
        self.rt = [f("rt%d" % i, [128, 16, 8], F32) for i in range(4)]
        self.QTs = f("QTs", [128, 8, 128], BF16)
        self.KTs = [f("KTs%d" % i, [128, 8, 128], BF16) for i in range(2)]
        self.Vs = [f("Vs%d" % i, [128, 256], BF16) for i in range(2)]
        self.Eb = f("Eb", [128, 4, 256], BF16)
        self.ETb = f("ETb", [128, 4, 256], BF16)
        self.mx = f("mx", [128, 4], F32)
        self.nm = f("nm", [128, 4], F32)
        self.den = f("den", [128, 16], F32)
        self.esk = f("esk", [128, 4], F32)
        self.dacc = f("dacc", [128, 4, 16], F32)
        self.rden = f("rden", [128, 16], F32)
        self.osw = f("osw", [128, 16, 64], BF16)
        self.vf32 = self.osw.bitcast(F32).rearrange("p a b -> p (a b)")[:, 0:256].rearrange("p (h d) -> p h d", d=64)
        self.rp_a = self.Eb.bitcast(F32).rearrange("p a b -> p (a b)")[:, 0:NB * 8].rearrange("p (b f) -> p b f", f=8)
        self.rp_b = self.ETb.bitcast(F32).rearrange("p a b -> p (a b)")[:, 0:NB * 8].rearrange("p (b f) -> p b f", f=8)
        print("sbuf bytes remaining", self.nc.sbuf_bytes_remaining, flush=True)

    def transpose_rows(self, rows_ap, nrows, dst_ap, key_r, key_w, f32=True):
        b = self.bank()
        self.pe(lambda e: e.transpose(out=self.bk(b)[:, 0:nrows], in_=rows_ap, identity=self.ident_f[0:nrows, 0:nrows]),
                r=[key_r, "const"], w=[self.bkey(b)])
        self.dve(lambda e: e.tensor_copy(out=dst_ap, in_=self.bk(b)[:, 0:nrows]), r=[self.bkey(b)], w=[key_w])

    def prologue(self):
        W = self.W
        NB = self.NB
        iot = self.iot
        self.pool(lambda e: e.iota(iot[:], pattern=[[1, 128]], base=0, channel_multiplier=-1,
                                   allow_small_or_imprecise_dtypes=True), w=["iot"])
        self.pool(lambda e: e.iota(self.posf[:], pattern=[[128, NB]], base=0, channel_multiplier=1,
                                   allow_small_or_imprecise_dtypes=True), w=["posf"])
        C = ["const"]
        d = self.dve
        d(lambda e: e.tensor_single_scalar(out=self.ident_f[:], in_=iot[:], scalar=0.0, op=ALU.is_equal), r=["iot"], w=C)
        d(lambda e: e.tensor_single_scalar(out=self.identb[:], in_=iot[:], scalar=0.0, op=ALU.is_equal), r=["iot"], w=C)
        d(lambda e: e.tensor_single_scalar(out=self.triT_f[:], in_=iot[:], scalar=0.0, op=ALU.is_ge), r=["iot"], w=C)
        d(lambda e: e.tensor_scalar(out=self.maskbig_f[:], in0=iot[:], scalar1=0.0, scalar2=1.0e4, op0=ALU.is_gt, op1=ALU.mult), r=["iot"], w=C)
        d(lambda e: e.tensor_single_scalar(out=self.strict_f[:], in_=iot[:], scalar=0.0, op=ALU.is_lt), r=["iot"], w=C)
        d(lambda e: e.tensor_scalar(out=self.mrow[:, 128:256], in0=iot[:], scalar1=0.0, scalar2=-1.0e30, op0=ALU.is_gt, op1=ALU.mult), r=["iot"], w=C)
        d(lambda e: e.tensor_scalar(out=self.mrow[:, 0:128], in0=iot[:], scalar1=0.0, scalar2=-1.0e30, op0=ALU.is_lt, op1=ALU.mult), r=["iot"], w=C)
        d(lambda e: e.tensor_scalar(out=self.mrow0[:, 128:256], in0=iot[:], scalar1=0.0, scalar2=-1.0e30, op0=ALU.is_gt, op1=ALU.mult), r=["iot"], w=C)
        d(lambda e: e.memset(self.mrow0[:, 0:128], -1.0e30), w=C)
        d(lambda e: e.memset(self.k_pad[:], 0.0), w=["k_pad"])
        d(lambda e: e.memset(self.onesb[:], 1.0), w=C)
        d(lambda e: e.memset(self.ones_f[:], 1.0), w=C)
        d(lambda e: e.memset(self.eps6[:], EPS), w=C)
        for f in range(8):
            d(lambda e, f=f: e.memset(self.invf[:, f:f + 1], float(INV_FREQ[f])), w=["invf"])
        prev = None
        for li, bs_ in enumerate([16, 32, 64]):
            t_j = self.tmpc[0][:, 0:128]
            t_p = self.tmpc[1][:, 0:128]
            self.pool(lambda e, bs_=bs_, t_j=t_j: e.iota(t_j, pattern=[[1, 128 // bs_], [0, bs_]], base=0, channel_multiplier=0,
                                                      allow_small_or_imprecise_dtypes=True), w=["tmpc0"])
            b = self.bank()
            self.pe(lambda e, b=b, t_j=t_j: e.transpose(out=self.bk(b)[:, 0:128], in_=t_j, identity=self.ident_f[:]), r=["tmpc0", "const"], w=[self.bkey(b)])
            d(lambda e, b=b, t_p=t_p: e.tensor_copy(out=t_p, in_=self.bk(b)[:, 0:128]), r=[self.bkey(b)], w=["tmpc1"])
            eq = (self.rinv[0] if li % 2 == 0 else self.rstd)[:, 0:128]
            eqk = "rinv0" if li % 2 == 0 else "rstd"
            d(lambda e, eq=eq, t_j=t_j, t_p=t_p: e.tensor_tensor(out=eq, in0=t_j, in1=t_p, op=ALU.is_equal), r=["tmpc0", "tmpc1"], w=[eqk])
            if li == 0:
                d(lambda e, eq=eq: e.tensor_copy(out=self.lvl_masks[0][:], in_=eq), r=[eqk], w=C)
            else:
                d(lambda e, eq=eq, prev=prev, li=li: e.tensor_tensor(out=self.lvl_masks[li][:], in0=eq, in1=prev[0], op=ALU.subtract), r=[eqk, prev[1]], w=C)
            if li == 2:
                d(lambda e, eq=eq: e.tensor_scalar(out=self.lvl_masks[3][:], in0=eq, scalar1=-1.0, scalar2=1.0, op0=ALU.mult, op1=ALU.add), r=[eqk], w=C)
            prev = (eq, eqk)
        d(lambda e: e.memset(self.S[:], 0.0), w=["S"])
        d(lambda e: e.memset(self.Sb[:], 0.0), w=["Sb"])
        d(lambda e: e.memset(self.hist[:], 0.0), w=["hist"])
        d(lambda e: e.memset(self.KTs[1][:], 0.0), w=["KTs1"])
        d(lambda e: e.memset(self.Vs[1][:], 0.0), w=["Vs1"])
        self.rope_table(self.posf[:], NB, self.cos_t, self.sin_t, "rope")
        self.dump("cos", self.cos_t[:].rearrange("p b f -> p (b f)"), ["rope"])
        self.dump("sin", self.sin_t[:].rearrange("p b f -> p (b f)"), ["rope"])
        if self.sample:
            self.pool(lambda e: e.iota(self.pos_s[:], pattern=[[0, 1]], base=PAST, channel_multiplier=1, allow_small_or_imprecise_dtypes=True), w=["posf"])
            self.rope_table(self.pos_s[:], 1, self.cos_s, self.sin_s, "rope")
            self.dve(lambda e: e.tensor_single_scalar(out=self.padm[:], in_=self.pos_s[:], scalar=float(PAST + TS) - 0.5, op=ALU.is_lt), r=["posf"], w=["padm"])
        self.dma("sp", self.sinks_bc[:], W["sinks"].partition_broadcast(128), w=["sinks_bc"], sem="small0")
        self.dma("sp", self.alog_bc[:], W["a_log"].partition_broadcast(128), w=["alog_bc"], sem="small1")
        self.dma("sp", self.dtb_bc[:], W["dt_bias"].partition_broadcast(128), w=["dtb_bc"], sem="small2")
        self.dma("sp", self.dnw[:], W["dn_norm"].rearrange("(p o) -> p o", o=1), w=["dnw"], sem="small3")
        self.act(lambda e: e.activation(out=self.nexpA[:], in_=self.alog_bc[:], func=AF.Exp), r=["alog_bc"], w=["nexpA"])
        self.dve(lambda e: e.tensor_scalar_mul(out=self.nexpA[:], in0=self.nexpA[:], scalar1=-1.0), r=["nexpA"], w=["nexpA"])
        self.dma("pool", self.wba[:], W["w_in"].rearrange("(kc p) n -> p kc n", p=128)[:, :, OFF_B:OFF_B + 16],
                 w=["wba"], sem="wba")
        st = self.stage
        gains = ["n1pre", "n1post", "n2pre", "n2post", "n3pre", "n3post"]
        for gi, gn in enumerate(gains):
            self.dma("sp", st[gi * 8:(gi + 1) * 8, 0:128], W[gn].rearrange("(c p) -> c p", p=128), w=["xtok"], sem="xtok")
        self.transpose_rows(st[0:48, 0:128], 48, self.gT[:], "xtok", "gT")
        self.dma("sp", st[0:96, 0:128], W["conv_w"].rearrange("i (c p) -> (i c) p", p=128), r=[], w=["xtok"], sem="xtok")
        self.transpose_rows(st[0:96, 0:128], 96, self.cwT[:], "xtok", "cwT")
        self.dma("sp", st[0:72, 0:128], W["b_ada"].rearrange("(c p) -> c p", p=128), w=["xtok"], sem="xtok")
        self.transpose_rows(st[0:72, 0:128], 72, self.badaT[:], "xtok", "badaT")
        nseq = 17 if self.sample else 1
        self.nseq = nseq
        self.dma("sp", st[0:1, :], self.cp[:, :], w=["xtok"], sem="xtok")
        if self.sample:
            self.dma("sp", st[1:17, :], self.cs[:, :], w=["xtok"], sem="xtok")
        self.act(lambda e: e.activation(out=st[0:nseq, :], in_=st[0:nseq, :], func=AF.Silu), r=["xtok"], w=["xtok"])
        b = self.bank()
        for c in range(8):
            self.pe(lambda e, c=c, b=b: e.transpose(out=self.bk(b)[:, c * 32:c * 32 + nseq], in_=st[0:nseq, c * 128:(c + 1) * 128],
                                               identity=self.ident_f[0:nseq, 0:nseq]), r=["xtok", "const"], w=[self.bkey(b)])
        self.dve(lambda e, b=b: e.tensor_copy(out=self.scT[:, :, 0:nseq],
                                         in_=self.bk(b)[:, 0:256].rearrange("p (c s) -> p c s", s=32)[:, :, 0:nseq]),
                 r=[self.bkey(b)], w=["scT"])
        for blk in range(18):
            wt, wk = self.wload("w_ada", blk * 512, 512)
            b = self.bank()
            for j in range(4):
                for kc in range(8):
                    self.pe(lambda e, j=j, kc=kc, wt=wt, b=b: e.matmul(self.bk(b)[:, j * 32:j * 32 + nseq], lhsT=wt[:, kc, j * 128:(j + 1) * 128],
                                                                      rhs=self.scT[:, kc, 0:nseq], start=(kc == 0), stop=(kc == 7)),
                            r=[wk, "scT"], w=[self.bkey(b)])
            cc0 = blk * 4
            self.dve(lambda e, b=b, cc0=cc0: e.tensor_tensor(
                out=self.modT[:, cc0:cc0 + 4, 0:nseq],
                in0=self.bk(b)[:, 0:128].rearrange("p (c s) -> p c s", s=32)[:, :, 0:nseq],
                in1=bc_last(self.badaT[:, cc0:cc0 + 4], nseq), op=ALU.add),
                r=[self.bkey(b), "badaT"], w=["modT"])
        coefs = [0.5, 1.0, 0.5]
        for n in range(3):
            sc = self.modT[:, 24 * n + 8:24 * n + 16, 0:nseq]
            gg = self.modT[:, 24 * n + 16:24 * n + 24, 0:nseq]
            gpre = bc_last(self.gT[:, 16 * n:16 * n + 8], nseq)
            gpost = bc_last(self.gT[:, 16 * n + 8:16 * n + 16], nseq)
            self.dve(lambda e, n=n, sc=sc, gpre=gpre: e.scalar_tensor_tensor(out=self.mA[n][:, :, 0:nseq], in0=sc, scalar=1.0, in1=gpre,
                                                                             op0=ALU.add, op1=ALU.mult), r=["modT", "gT"], w=["mA%d" % n])
            self.dve(lambda e, n=n, gg=gg, gpost=gpost: e.scalar_tensor_tensor(out=self.mG[n][:, :, 0:nseq], in0=gg, scalar=coefs[n], in1=gpost,
                                                                               op0=ALU.mult, op1=ALU.mult), r=["modT", "gT"], w=["mG%d" % n])

        for n in ("wg1", "wu1", "wd1", "w_in", "w_out", "wg2", "wu2", "wd2"):
            if n in self.WB:
                rows = self.W[n].shape[0]
                r0 = 0
                while r0 < rows:
                    r1 = min(rows, r0 + 256)
                    self.dma("pool", self.WB[n][r0:r1, :], self.W[n][r0:r1, :], w=["wb_" + n], sem="wb_" + n)
                    r0 = r1

    def mB(self, n):
        return self.modT[:, 24 * n:24 * n + 8, :]

    def rope_table(self, pos_ap, nb, cos_t, sin_t, key):
        a, bb = self.rp_a, self.rp_b
        d = self.dve
        K = [key]
        d(lambda e: e.tensor_tensor(out=a[:, 0:nb, :], in0=bc_last(pos_ap, 8), in1=bc_mid(self.invf[:], nb), op=ALU.mult),
          r=["posf", "invf"], w=K)
        self.dump("posf", self.posf[:], ["posf"])
        self.dump("invf", self.invf[:], ["invf"])
        self.dump("ang", a[:, 0:nb, :].rearrange("p b f -> p (b f)"), K)
        d(lambda e: e.tensor_scalar_mul(out=bb[:, 0:nb, :], in0=a[:, 0:nb, :], scalar1=float(1.0 / TWO_PI)), r=K, w=K)
        d(lambda e: e.tensor_scalar_add(out=bb[:, 0:nb, :], in0=bb[:, 0:nb, :], scalar1=12582912.0), r=K, w=K)
        d(lambda e: e.tensor_scalar_add(out=bb[:, 0:nb, :], in0=bb[:, 0:nb, :], scalar1=-12582912.0), r=K, w=K)
        self.dump("kf", bb[:, 0:nb, :].rearrange("p b f -> p (b f)"), K)
        d(lambda e: e.scalar_tensor_tensor(out=a[:, 0:nb, :], in0=bb[:, 0:nb, :], scalar=-CW1, in1=a[:, 0:nb, :], op0=ALU.mult, op1=ALU.add), r=K, w=K)
        d(lambda e: e.scalar_tensor_tensor(out=a[:, 0:nb, :], in0=bb[:, 0:nb, :], scalar=-CW2, in1=a[:, 0:nb, :], op0=ALU.mult, op1=ALU.add), r=K, w=K)

        self.dump("red", a[:, 0:nb, :].rearrange("p b f -> p (b f)"), K)

        def wrap_and_sin(dst):
            d(lambda e: e.tensor_scalar(out=bb[:, 0:nb, :], in0=a[:, 0:nb, :], scalar1=float(np.pi), scalar2=float(-TWO_PI), op0=ALU.is_gt, op1=ALU.mult), r=K, w=K)
            d(lambda e: e.tensor_tensor(out=a[:, 0:nb, :], in0=a[:, 0:nb, :], in1=bb[:, 0:nb, :], op=ALU.add), r=K, w=K)
            d(lambda e: e.tensor_scalar(out=bb[:, 0:nb, :], in0=a[:, 0:nb, :], scalar1=float(-np.pi), scalar2=float(TWO_PI), op0=ALU.is_lt, op1=ALU.mult), r=K, w=K)
            d(lambda e: e.tensor_tensor(out=a[:, 0:nb, :], in0=a[:, 0:nb, :], in1=bb[:, 0:nb, :], op=ALU.add), r=K, w=K)
            d(lambda e: e.tensor_scalar(out=bb[:, 0:nb, :], in0=a[:, 0:nb, :], scalar1=3.14159, scalar2=-3.14159, op0=ALU.min, op1=ALU.max), r=K, w=K)
            self.act(lambda e: e.activation(out=dst[:, 0:nb, :], in_=bb[:, 0:nb, :], func=AF.Sin), r=K, w=K)

        wrap_and_sin(sin_t)
        d(lambda e: e.tensor_scalar_add(out=a[:, 0:nb, :], in0=a[:, 0:nb, :], scalar1=float(np.pi / 2)), r=K, w=K)
        wrap_and_sin(cos_t)

    def wload(self, wname, col0, ncols):
        if not hasattr(self, "_wrr"):
            self._wrr = 0
        i = self._wrr
        self._wrr = (i + 1) % len(self.wide)
        t = self.wide[i]
        key = "wide%d" % i
        if wname in self.WB:
            src = self.WB[wname].rearrange("(kc p) n -> p kc n", p=128)[:, :, col0:col0 + ncols]
            self.dma("sp", t[:, :, 0:ncols], src, r=["wb_" + wname], w=[key], sem=key)
        else:
            src = self.W[wname].rearrange("(kc p) n -> p kc n", p=128)[:, :, col0:col0 + ncols]
            self.dma("pool", t[:, :, 0:ncols], src, w=[key], sem=key)
        return t, key

    def wload_down(self, wname, dc):
        if not hasattr(self, "_drr"):
            self._drr = 0
        i = self._drr
        self._drr = (i + 1) % len(self.wdn)
        t = self.wdn[i]
        key = "wdn%d" % i
        if wname in self.WB:
            w = self.WB[wname]
            q, rr = "sp", ["wb_" + wname]
        else:
            w = self.W[wname]
            q, rr = "pool", []
        self.dma(q, t[:, 0:21, :], w[0:2688, dc * 128:(dc + 1) * 128].rearrange("(kc p) n -> p kc n", p=128), r=rr, w=[key], sem=key)
        self.dma(q, t[0:64, 21, :], w[2688:2752, dc * 128:(dc + 1) * 128], r=rr, w=[key], sem=key)
        return t, key

    def rms_rstd(self, src, src_key, N):
        b = self.bank()
        for c in range(8):
            sqh = self.sqh[c % 2]
            sk = "sqh0"
            self.act(lambda e, c=c, sqh=sqh: e.activation(out=sqh[:, 0:N], in_=src[:, c, 0:N], func=AF.Square), r=[src_key], w=[sk])
            self.pe(lambda e, c=c, sqh=sqh: e.matmul(self.bk(b)[:, 0:N], lhsT=self.onesb[:], rhs=sqh[:, 0:N], start=(c == 0), stop=(c == 7)),
                    r=[sk, "const"], w=[self.bkey(b)])
        self.act(lambda e: e.activation(out=self.rstd[:, 0:N], in_=self.bk(b)[:, 0:N], func=AF.Ln, bias=self.eps6[:, 0:1], scale=1.0 / D),
                 r=[self.bkey(b), "const"], w=["rstd"])
        self.act(lambda e: e.activation(out=self.rstd[:, 0:N], in_=self.rstd[:, 0:N], func=AF.Exp, scale=-0.5), r=["rstd"], w=["rstd"])

    def prenorm(self, n, N, smp):
        self.rms_rstd(self.xT, "xT", N)
        A = self.mA[n]
        B = self.mB(n)
        for c in range(8):
            t = self.tmpc[c % 2]
            tk = "tmpc%d" % (c % 2)
            self.dve(lambda e, c=c, t=t: e.tensor_tensor(out=t[:, 0:N], in0=self.xT[:, c, 0:N], in1=self.rstd[:, 0:N], op=ALU.mult),
                     r=["xT", "rstd"], w=[tk])
            if not smp:
                self.act(lambda e, c=c, t=t: e.activation(out=self.hT[:, c, 0:N], in_=t[:, 0:N], func=AF.Identity,
                                                          bias=B[:, c, 0:1], scale=A[:, c, 0:1]), r=[tk, "mA%d" % n, "modT"], w=["hT"])
            elif isinstance(smp, list):
                for (c0, c1, sq) in smp:
                    self.act(lambda e, c=c, t=t, c0=c0, c1=c1, sq=sq: e.activation(out=self.hT[:, c, c0:c1], in_=t[:, c0:c1], func=AF.Identity,
                                                                                   bias=B[:, c, sq:sq + 1], scale=A[:, c, sq:sq + 1]),
                             r=[tk, "mA%d" % n, "modT"], w=["hT"])
            else:
                tv = t[:, 0:N].rearrange("p (s t) -> p s t", t=TS)
                self.dve(lambda e, c=c, tv=tv: e.tensor_tensor(out=tv, in0=tv, in1=bc_last(A[:, c, 1:17], TS), op=ALU.mult), r=[tk, "mA%d" % n], w=[tk])
                self.dve(lambda e, c=c, tv=tv: e.tensor_tensor(out=self.hT[:, c, 0:N].rearrange("p (s t) -> p s t", t=TS), in0=tv,
                                                               in1=bc_last(B[:, c, 1:17], TS), op=ALU.add), r=[tk, "modT"], w=["hT"])

    def postnorm(self, n, N, smp):
        self.rms_rstd(self.yF, "yF", N)
        G = self.mG[n]
        for c in range(8):
            t = self.tmpc[c % 2]
            tk = "tmpc%d" % (c % 2)
            self.dve(lambda e, c=c, t=t: e.tensor_tensor(out=t[:, 0:N], in0=self.yF[:, c, 0:N], in1=self.rstd[:, 0:N], op=ALU.mult),
                     r=["yF", "rstd"], w=[tk])
            if not smp:
                self.dve(lambda e, c=c, t=t: e.scalar_tensor_tensor(out=self.xT[:, c, 0:N], in0=t[:, 0:N], scalar=G[:, c, 0:1],
                                                                    in1=self.xT[:, c, 0:N], op0=ALU.mult, op1=ALU.add),
                         r=[tk, "mG%d" % n, "xT"], w=["xT"])
            elif isinstance(smp, list):
                for (c0, c1, sq) in smp:
                    self.dve(lambda e, c=c, t=t, c0=c0, c1=c1, sq=sq: e.scalar_tensor_tensor(out=self.xT[:, c, c0:c1], in0=t[:, c0:c1], scalar=G[:, c, sq:sq + 1],
                                                                                             in1=self.xT[:, c, c0:c1], op0=ALU.mult, op1=ALU.add),
                             r=[tk, "mG%d" % n, "xT"], w=["xT"])
            else:
                tv = t[:, 0:N].rearrange("p (s t) -> p s t", t=TS)
                self.dve(lambda e, c=c, tv=tv: e.tensor_tensor(out=tv, in0=tv, in1=bc_last(G[:, c, 1:17], TS), op=ALU.mult), r=[tk, "mG%d" % n], w=[tk])
                self.dve(lambda e, c=c, t=t: e.tensor_tensor(out=self.xT[:, c, 0:N], in0=self.xT[:, c, 0:N], in1=t[:, 0:N], op=ALU.add),
                         r=[tk, "xT"], w=["xT"])

    def ffn(self, wg, wu, wd, N):
        blocks = [(i * 512, 512) for i in range(5)] + [(2560, 192)]
        ffc = 0
        for (c0, ncol) in blocks:
            gt, gk = self.wload(wg, c0, ncol)
            ut, uk = self.wload(wu, c0, ncol)
            j = 0
            while j * 128 < ncol:
                m = min(128, ncol - j * 128)
                bg = self.bank()
                bu = self.bank()
                for kc in range(8):
                    self.pe(lambda e, kc=kc, j=j, m=m, bg=bg, gt=gt: e.matmul(self.bk(bg)[0:m, 0:N], lhsT=gt[:, kc, j * 128:j * 128 + m],
                                                                               rhs=self.hT[:, kc, 0:N], start=(kc == 0), stop=(kc == 7)),
                            r=[gk, "hT"], w=[self.bkey(bg)])
                for kc in range(8):
                    self.pe(lambda e, kc=kc, j=j, m=m, bu=bu, ut=ut: e.matmul(self.bk(bu)[0:m, 0:N], lhsT=ut[:, kc, j * 128:j * 128 + m],
                                                                               rhs=self.hT[:, kc, 0:N], start=(kc == 0), stop=(kc == 7)),
                            r=[uk, "hT"], w=[self.bkey(bu)])
                t = self.tmpc[ffc % 2]
                tk = "tmpc%d" % (ffc % 2)
                self.act(lambda e, m=m, bg=bg, t=t: e.activation(out=t[0:m, 0:N], in_=self.bk(bg)[0:m, 0:N], func=AF.Silu),
                         r=[self.bkey(bg)], w=[tk])
                self.dve(lambda e, m=m, bu=bu, t=t, ffc=ffc: e.tensor_tensor(out=self.abuf[0:m, ffc, 0:N], in0=t[0:m, 0:N], in1=self.bk(bu)[0:m, 0:N], op=ALU.mult),
                         r=[tk, self.bkey(bu)], w=["abuf"])
                ffc += 1
                j += 1
        for dc in range(8):
            dt, dk = self.wload_down(wd, dc)
            b = self.bank()
            for f in range(NFF):
                kk = 128 if f < 21 else 64
                self.pe(lambda e, f=f, kk=kk, b=b, dt=dt: e.matmul(self.bk(b)[:, 0:N], lhsT=dt[0:kk, f, :], rhs=self.abuf[0:kk, f, 0:N],
                                                                    start=(f == 0), stop=(f == NFF - 1)),
                        r=[dk, "abuf"], w=[self.bkey(b)])
            self.act(lambda e, b=b, dc=dc: e.activation(out=self.yF[:, dc, 0:N], in_=self.bk(b)[:, 0:N], func=AF.Copy),
                     r=[self.bkey(b)], w=["yF"])

    def load_x(self, rows_ap, nblk, rows_per_blk):
        if rows_ap is not None:
            self.dma("sp", self.xtok[0:rows_per_blk, 0:nblk, :], rows_ap.rearrange("(b p) d -> p b d", p=rows_per_blk), w=["xtok"], sem="xin")
        for c in range(8):
            b = self.bank()
            for bi in range(nblk):
                self.pe(lambda e, c=c, bi=bi, b=b: e.transpose(out=self.bk(b)[:, bi * rows_per_blk:(bi + 1) * rows_per_blk],
                                                               in_=self.xtok[0:rows_per_blk, bi, c * 128:(c + 1) * 128],
                                                               identity=self.ident_f[0:rows_per_blk, 0:rows_per_blk]),
                        r=["xtok", "const"], w=[self.bkey(b)])
            n = nblk * rows_per_blk
            self.act(lambda e, c=c, b=b, n=n: e.activation(out=self.xT[:, c, 0:n], in_=self.bk(b)[:, 0:n], func=AF.Copy),
                     r=[self.bkey(b)], w=["xT"])

    def store_x(self, rows_ap, nblk, rows_per_blk):
        for bi in range(nblk):
            for half in range(2):
                b = self.bank()
                for c4 in range(4):
                    c = half * 4 + c4
                    self.pe(lambda e, c=c, c4=c4, bi=bi, b=b: e.transpose(out=self.bk(b)[0:rows_per_blk, c4 * 128:(c4 + 1) * 128],
                                                                          in_=self.xT[:, c, bi * rows_per_blk:(bi + 1) * rows_per_blk],
                                                                          identity=self.ident_f[:, :]),
                            r=["xT", "const"], w=[self.bkey(b)])
                self.act(lambda e, bi=bi, half=half, b=b: e.activation(out=self.xtok[0:rows_per_blk, bi, half * 512:(half + 1) * 512],
                                                                       in_=self.bk(b)[0:rows_per_blk, :], func=AF.Copy),
                         r=[self.bkey(b)], w=["xtok"])
        if rows_ap is not None:
            self.dma("sp", rows_ap.rearrange("(b p) d -> p b d", p=rows_per_blk), self.xtok[0:rows_per_blk, 0:nblk, :], r=["xtok"], sem="xout", output=True)

    def prompt_tile(self, it):
        N = NT
        L = self.stage_lim
        self.load_x(self.xp[it * NT:(it + 1) * NT, :], 4, 128)
        if L >= 1:
            self.prenorm(0, N, False)
        if L >= 2:
            self.ffn("wg1", "wu1", "wd1", N)
            self.postnorm(0, N, False)
        if L >= 3:
            self.prenorm(1, N, False)
            self.mixer_prompt(it)
        if L >= 8:
            self.postnorm(1, N, False)
        if L >= 9:
            self.prenorm(2, N, False)
            self.ffn("wg2", "wu2", "wd2", N)
            self.postnorm(2, N, False)
        self.store_x(self.yp[it * NT:(it + 1) * NT, :], 4, 128)

    def proj_fm(self, wt, wk, j, N, b):
        for kc in range(8):
            self.pe(lambda e, kc=kc: e.matmul(self.bk(b)[:, 0:N], lhsT=wt[:, kc, j * 128:(j + 1) * 128], rhs=self.hT[:, kc, 0:N],
                                              start=(kc == 0), stop=(kc == 7)), r=[wk, "hT"], w=[self.bkey(b)])

    def l2norm_chunk(self, src, src_key, N, dst, dst_key, scale, idx):
        sqh = self.sqh[idx % 2]
        sk = "sqh0"
        ri = self.rinv[idx % 2]
        rk = "rinv0"
        self.act(lambda e: e.activation(out=sqh[:, 0:N], in_=src, func=AF.Square), r=[src_key], w=[sk])
        b = self.bank()
        self.pe(lambda e: e.matmul(self.bk(b)[:, 0:N], lhsT=self.onesb[:], rhs=sqh[:, 0:N], start=True, stop=True), r=[sk, "const"], w=[self.bkey(b)])
        self.act(lambda e: e.activation(out=ri[:, 0:N], in_=self.bk(b)[:, 0:N], func=AF.Ln, bias=self.eps6[:, 0:1], scale=1.0),
                 r=[self.bkey(b), "const"], w=[rk])
        self.act(lambda e: e.activation(out=ri[:, 0:N], in_=ri[:, 0:N], func=AF.Exp, scale=-0.5), r=[rk], w=[rk])
        self.dve(lambda e: e.scalar_tensor_tensor(out=dst, in0=src, scalar=float(scale), in1=ri[:, 0:N], op0=ALU.mult, op1=ALU.mult),
                 r=[src_key, rk], w=[dst_key])

    def mixer_prompt(self, it):
        N = NT
        smp = it < 0
        last = (it == self.NPT - 1)
        for blk in range(6):
            wt, wk = self.wload("w_in", blk * 512, 512)
            for j in range(4):
                cc = blk * 4 + j
                b = self.bank()
                self.proj_fm(wt, wk, j, N, b)
                ub = self.ubuf[cc % 2]
                uk = "ubuf%d" % (cc % 2)
                cd = self.cdiag[cc % 2]
                ck = "cdiag0"
                for i in range(4):
                    self.dve(lambda e, i=i, cc=cc, cd=cd: e.tensor_scalar_mul(out=cd[:, i, :], in0=self.identb[:], scalar1=self.cwT[:, i * 24 + cc:i * 24 + cc + 1]),
                             r=["const", "cwT"], w=[ck])
                b2 = self.bank()
                if not smp:
                    self.dve(lambda e, cc=cc, ub=ub: e.tensor_copy(out=ub[:, 0:3], in_=self.hist[:, cc, :]), r=["hist"], w=[uk])
                    self.act(lambda e, b=b, ub=ub: e.activation(out=ub[:, 3:3 + N], in_=self.bk(b)[:, 0:N], func=AF.Copy), r=[self.bkey(b)], w=[uk])
                    self.dve(lambda e, cc=cc, ub=ub: e.tensor_copy(out=self.hist[:, cc, :], in_=ub[:, N:N + 3]), r=[uk], w=["hist"])
                    if last:
                        self.dve(lambda e, cc=cc, b=b: e.tensor_copy(out=self.convout[:, cc, :], in_=self.bk(b)[:, N - 3:N]), r=[self.bkey(b)], w=["convout"])
                    for i in range(4):
                        self.pe(lambda e, i=i, cd=cd, ub=ub, b2=b2: e.matmul(self.bk(b2)[:, 0:N], lhsT=cd[:, i, :], rhs=ub[:, i:i + N], start=(i == 0), stop=(i == 3)),
                                r=[ck, uk], w=[self.bkey(b2)])
                else:
                    ub4 = ub[:, 0:4 * 131].rearrange("p (u n) -> p u n", n=131)
                    self.dve(lambda e, cc=cc, ub4=ub4: e.tensor_copy(out=ub4[:, :, 0:3], in_=self.hist4[:, cc, :].rearrange("p (u t) -> p u t", t=3)), r=["hist4"], w=[uk])
                    self.act(lambda e, b=b, ub4=ub4: e.activation(out=ub4[:, :, 3:131], in_=self.bk(b)[:, 0:N].rearrange("p (u n) -> p u n", n=128), func=AF.Copy),
                             r=[self.bkey(b)], w=[uk])
                    self.dve(lambda e, cc=cc, b=b: e.tensor_copy(out=self.convout4[:, cc, :].rearrange("p (u t) -> p u t", t=3),
                                                                 in_=self.bk(b)[:, 0:N].rearrange("p (u n) -> p u n", n=128)[:, :, 1:4]), r=[self.bkey(b)], w=["convout4"])
                    for u in range(4):
                        for i in range(4):
                            self.pe(lambda e, i=i, u=u, cd=cd, ub4=ub4, b2=b2: e.matmul(self.bk(b2)[:, u * 128:(u + 1) * 128], lhsT=cd[:, i, :], rhs=ub4[:, u, i:i + 128],
                                                                                           start=(i == 0), stop=(i == 3)), r=[ck, uk], w=[self.bkey(b2)])
                if cc < 16:
                    t = self.tmpc[cc % 2]
                    tk = "tmpc%d" % (cc % 2)
                    self.act(lambda e, b2=b2, t=t: e.activation(out=t[:, 0:N], in_=self.bk(b2)[:, 0:N], func=AF.Silu), r=[self.bkey(b2)], w=[tk])
                    if cc < 8:
                        self.l2norm_chunk(t[:, 0:N], tk, N, self.QT[:, cc, :], "abuf", 128.0 ** -0.5, cc)
                    else:
                        self.l2norm_chunk(t[:, 0:N], tk, N, self.KT[:, cc - 8, :], "abuf", 1.0, cc)
                else:
                    self.act(lambda e, b2=b2, cc=cc: e.activation(out=self.vT[:, cc - 16, :], in_=self.bk(b2)[:, 0:N], func=AF.Silu),
                             r=[self.bkey(b2)], w=["xtok"])
        if smp:
            jj = -1 - it
            stg = self.yF[:, :, :].rearrange("p c n -> p (c n)")
            for cc in range(24):
                b = self.bank()
                self.pe(lambda e, cc=cc, b=b: e.transpose(out=self.bk(b)[0:12, 0:128], in_=self.convout4[:, cc, :], identity=self.ident_f[:]),
                        r=["convout4", "const"], w=[self.bkey(b)])
                self.dve(lambda e, cc=cc, b=b: e.tensor_copy(out=stg[0:12, cc * 128:(cc + 1) * 128], in_=self.bk(b)[0:12, 0:128]), r=[self.bkey(b)], w=["yF"])
            self.dma("sp", self.convs_o[4 * jj:4 * jj + 4].rearrange("s t c -> (s t) c"), stg[0:12, 0:3072], r=["yF"], sem="cvs", output=True)
        if self.stage_lim < 4:
            return
        for blk in range(2):
            wz, wzk = self.wload("w_in", OFF_Z + blk * 512, 512)
            wa, wak = self.wload("w_in", OFF_GA + blk * 512, 512)
            for j in range(4):
                h = blk * 4 + j
                bz = self.bank()
                self.proj_fm(wz, wzk, j, N, bz)
                ba = self.bank()
                self.proj_fm(wa, wak, j, N, ba)
                t0 = self.tmpc[0]
                t1 = self.tmpc[1]
                self.act(lambda e, bz=bz: e.activation(out=t0[:, 0:N], in_=self.bk(bz)[:, 0:N], func=AF.Silu), r=[self.bkey(bz)], w=["tmpc0"])
                self.act(lambda e, ba=ba: e.activation(out=t1[:, 0:N], in_=self.bk(ba)[:, 0:N], func=AF.Sigmoid), r=[self.bkey(ba)], w=["tmpc1"])
                self.dve(lambda e, h=h: e.tensor_tensor(out=self.sz[:, h, :], in0=t0[:, 0:N], in1=t1[:, 0:N], op=ALU.mult),
                         r=["tmpc0", "tmpc1"], w=["xtok"])
        if self.stage_lim < 5:
            return
        wq0, wq0k = self.wload("w_in", OFF_QSW, 512)
        wq1, wq1k = self.wload("w_in", OFF_QSW + 512, 512)
        wkv, wkvk = self.wload("w_in", OFF_KV, 512)
        la = self.lane_swa(it, (wq0, wq0k), (wq1, wq1k), (wkv, wkvk))
        lb = self.lane_delta(it)
        a_alive = b_alive = True
        while a_alive or b_alive:
            if b_alive:
                b_alive = next(lb, "END") != "END"
            if a_alive:
                a_alive = next(la, "END") != "END"
        if self.stage_lim < 7:
            return
        for blk in range(2):
            wb, wbk = self.wload("w_in", OFF_GB + blk * 512, 512)
            for j in range(4):
                c = blk * 4 + j
                b = self.bank()
                self.proj_fm(wb, wbk, j, N, b)
                self.act(lambda e, b=b, c=c: e.activation(out=self.sgb[:, c, :], in_=self.bk(b)[:, 0:N], func=AF.Sigmoid), r=[self.bkey(b)], w=["abuf"])
        self.dump("ysw0", self.ysw[:].rearrange("p c n -> p (c n)"), ["ysw"], BF16)
        self.dump("sgb", self.sgb.rearrange("p c n -> p (c n)"), ["abuf"], BF16)
        for h in range(8):
            sqh = self.sqh[h % 2]
            sk = "sqh0"
            ri = self.rinv[h % 2]
            rk = "rinv0"
            self.act(lambda e, h=h, sqh=sqh: e.activation(out=sqh[:, 0:N], in_=self.oT[:, h, :], func=AF.Square), r=["yF"], w=[sk])
            b = self.bank()
            self.pe(lambda e, b=b, sqh=sqh: e.matmul(self.bk(b)[:, 0:N], lhsT=self.onesb[:], rhs=sqh[:, 0:N], start=True, stop=True),
                    r=[sk, "const"], w=[self.bkey(b)])
            self.act(lambda e, b=b, ri=ri: e.activation(out=ri[:, 0:N], in_=self.bk(b)[:, 0:N], func=AF.Ln, bias=self.eps6[:, 0:1], scale=1.0 / 128),
                     r=[self.bkey(b), "const"], w=[rk])
            self.act(lambda e, ri=ri: e.activation(out=ri[:, 0:N], in_=ri[:, 0:N], func=AF.Exp, scale=-0.5), r=[rk], w=[rk])
            t = self.tmpc[h % 2]
            tk = "tmpc%d" % (h % 2)
            self.dve(lambda e, h=h, t=t, ri=ri: e.scalar_tensor_tensor(out=t[:, 0:N], in0=self.oT[:, h, :], scalar=self.dnw[:, 0:1], in1=ri[:, 0:N],
                                                                       op0=ALU.mult, op1=ALU.mult), r=["yF", "dnw", rk], w=[tk])
            self.dve(lambda e, h=h, t=t: e.tensor_tensor(out=t[:, 0:N], in0=t[:, 0:N], in1=self.sz[:, h, :], op=ALU.mult), r=[tk, "xtok"], w=[tk])
            self.dve(lambda e, h=h: e.tensor_tensor(out=self.ysw[:, h, :], in0=self.ysw[:, h, :], in1=self.sgb[:, h, :], op=ALU.mult),
                     r=["ysw", "abuf"], w=["ysw"])
            self.dve(lambda e, h=h, t=t: e.tensor_tensor(out=self.yT[:, h, :], in0=t[:, 0:N], in1=self.ysw[:, h, :], op=ALU.add),
                     r=[tk, "ysw"], w=["hT"])
        self.dump("yswg", self.ysw[:].rearrange("p c n -> p (c n)"), ["ysw"], BF16)
        self.dump("yT", self.yT[:].rearrange("p c n -> p (c n)"), ["hT"], BF16)
        self.dump("oT", self.oT[:].rearrange("p c n -> p (c n)"), ["yF"])
        for blk in range(2):
            wo, wok = self.wload("w_out", blk * 512, 512)
            for j in range(4):
                dc = blk * 4 + j
                b = self.bank()
                self.proj_fm(wo, wok, j, N, b)
                self.act(lambda e, b=b, dc=dc: e.activation(out=self.yF[:, dc, 0:N], in_=self.bk(b)[:, 0:N], func=AF.Copy), r=[self.bkey(b)], w=["yF"])
        if last:
            for t_ in range(3):
                self.dma("sp", self.convp_o[t_].rearrange("(c p) -> p c", p=128), self.convout[:, :, t_], r=["convout"], sem="cvo%d" % t_, output=True, nc_ok=True)
            self.dma("sp", self.deltap_o.rearrange("h k v -> k h v"), self.S[:], r=["S"], sem="dlo", output=True)

    def unit_ctx(self, it, u):
        cols = slice(u * 128, (u + 1) * 128)
        blk = it * 4 + u
        last = (it == self.NPT - 1 and u == 3)
        cur = blk % 2
        prv = 1 - cur
        smp = it < 0
        sq = 4 * (-1 - it) + u if smp else None
        if smp:
            cur, prv = 0, 1
        return cols, blk, last, cur, prv, smp, sq

    def unit_gates(self, it, u):
        cols, blk, last, cur, prv, smp, sq = self.unit_ctx(it, u)
        bba = self.bank()
        for kc in range(8):
            self.pe(lambda e, kc=kc: e.matmul(self.bk(bba)[:, 0:16], lhsT=self.hT[:, kc, cols], rhs=self.wba[:, kc, :], start=(kc == 0), stop=(kc == 7)),
                    r=["hT", "wba"], w=[self.bkey(bba)])
        self.dve(lambda e: e.tensor_copy(out=self.ba_sb[:], in_=self.bk(bba)[:, 0:16]), r=[self.bkey(bba)], w=["ba_sb"])
        self.gate_scalars()
        if smp:
            self.dve(lambda e: e.tensor_scalar_mul(out=self.b_tok[:], in0=self.b_tok[:], scalar1=self.padm[:, 0:1]), r=["b_tok", "padm"], w=["b_tok"])
            self.dve(lambda e: e.tensor_scalar_mul(out=self.nb_tok[:], in0=self.nb_tok[:], scalar1=self.padm[:, 0:1]), r=["nb_tok", "padm"], w=["nb_tok"])
            self.dve(lambda e: e.tensor_scalar_mul(out=self.g_tok[:], in0=self.g_tok[:], scalar1=self.padm[:, 0:1]), r=["g_tok", "padm"], w=["g_tok"])
            self.dma("sp", self.S[:], self.sdelta[sq].rearrange("h k v -> k h v"), w=["S"], sem="sld")
            self.act(lambda e: e.activation(out=self.Sb[:], in_=self.S[:], func=AF.Copy), r=["S"], w=["Sb"])

    def lane_swa(self, it, wq0, wq1, wkv):
        for u in range(4):
            cols, blk, last, cur, prv, smp, sq = self.unit_ctx(it, u)
            yield from self.swa_unit_prompt(it, u, wq0, wq1, wkv, cols, blk, cur, prv, last, sq=sq)
            yield

    def lane_delta(self, it):
        for u in range(4):
            cols, blk, last, cur, prv, smp, sq = self.unit_ctx(it, u)
            self.unit_gates(it, u)
            yield
            for hg in range(2):
                yield from self.delta_unit(cols, hg, self.QT, self.KT, self.vT, "abuf", "xtok", mask_big=self.maskbig_f, strict=self.strict_f,
                                           tri=self.triT_f, nsteps=7, out_cols=cols)
                yield
            if smp:
                self.dma("sp", self.deltas_o[sq].rearrange("h k v -> k h v"), self.S[:], r=["S"], sem="sst", output=True)

    def gate_scalars(self):
        ba = self.ba_sb
        s0, s1, s2, s3 = self.sp_t
        K = ["gsc"]
        self.act(lambda e: e.activation(out=self.b_tok[:], in_=ba[:, 0:8], func=AF.Sigmoid), r=["ba_sb"], w=["b_tok"])
        self.dve(lambda e: e.tensor_scalar_mul(out=self.nb_tok[:], in0=self.b_tok[:], scalar1=-1.0), r=["b_tok"], w=["nb_tok"])
        self.dve(lambda e: e.tensor_tensor(out=s0[:], in0=ba[:, 8:16], in1=self.dtb_bc[:], op=ALU.add), r=["ba_sb", "dtb_bc"], w=K)
        self.dve(lambda e: e.scalar_tensor_tensor(out=s1[:], in0=s0[:], scalar=-1.0, in1=s0[:], op0=ALU.mult, op1=ALU.max), r=K, w=K)
        self.act(lambda e: e.activation(out=s2[:], in_=s1[:], func=AF.Exp, scale=-1.0), r=K, w=K)
        self.dve(lambda e: e.tensor_scalar_add(out=s2[:], in0=s2[:], scalar1=1.0), r=K, w=K)
        self.act(lambda e: e.activation(out=s2[:], in_=s2[:], func=AF.Ln), r=K, w=K)
        self.dve(lambda e: e.tensor_scalar_max(out=s3[:], in0=s0[:], scalar1=0.0), r=K, w=K)
        self.dve(lambda e: e.tensor_tensor(out=s3[:], in0=s3[:], in1=s2[:], op=ALU.add), r=K, w=K)
        self.dve(lambda e: e.tensor_tensor(out=self.g_tok[:], in0=s3[:], in1=self.nexpA[:], op=ALU.mult), r=K + ["nexpA"], w=["g_tok"])

    def delta_unit(self, cols, hg, QT, KT, vT, qk_key, v_key, mask_big, strict, tri, nsteps, out_cols, smp=False):
        H4 = slice(hg * 4, hg * 4 + 4)
        hs = [hg * 4 + i for i in range(4)]
        PE, DVE, ACT = self.pe, self.dve, self.act
        self._du = getattr(self, "_du", -1) + 1
        first = (self._du == 0)
        def DU(name, ap, keys, dt=F32):
            if first:
                self.dump(name, ap, keys, dt)
        C = "const"
        bG = self.bank()
        PE(lambda e: e.matmul(self.bk(bG)[:, 0:4], lhsT=tri[:], rhs=self.g_tok[:, H4], start=True, stop=True), r=[C, "g_tok"], w=[self.bkey(bG)])
        PE(lambda e: e.matmul(self.bk(bG)[:, 8:12], lhsT=self.ones_f[:], rhs=self.g_tok[:, H4], start=True, stop=True), r=[C, "g_tok"], w=[self.bkey(bG)])
        DVE(lambda e: e.tensor_copy(out=self.Gcol[:, H4], in_=self.bk(bG)[:, 0:4]), r=[self.bkey(bG)], w=["Gcol"])
        DVE(lambda e: e.tensor_copy(out=self.Glast[:, H4], in_=self.bk(bG)[:, 8:12]), r=[self.bkey(bG)], w=["Glast"])
        yield
        DVE(lambda e: e.tensor_copy(out=self.Xg[:], in_=bc_last(self.g_tok[:, H4], 128)), r=["g_tok"], w=["Xg"])
        bR = self.bank()
        for i in range(4):
            PE(lambda e, i=i: e.matmul(self.bk(bR)[:, i * 128:(i + 1) * 128], lhsT=self.Xg[:, i, :], rhs=tri[:], start=True, stop=True),
               r=["Xg", C], w=[self.bkey(bR)])
        bR3 = self.bk(bR).rearrange("p (h j) -> p h j", j=128)
        ACT(lambda e: e.activation(out=self.eGrow[:], in_=bR3, func=AF.Exp), r=[self.bkey(bR)], w=["eGrow"])
        DVE(lambda e: e.tensor_tensor(out=self.dd[:], in0=bR3, in1=bc_last(self.Gcol[:, H4], 128), op=ALU.subtract), r=[self.bkey(bR), "Gcol"], w=["dd"])
        DVE(lambda e: e.tensor_tensor(out=self.dd[:], in0=self.dd[:], in1=bc_mid(mask_big[:], 4), op=ALU.add), r=["dd", C], w=["dd"])
        ACT(lambda e: e.activation(out=self.dec[:], in_=self.dd[:], func=AF.Exp, scale=-1.0), r=["dd"], w=["dec"])
        yield
        ACT(lambda e: e.activation(out=self.eG[:, H4], in_=self.Gcol[:, H4], func=AF.Exp), r=["Gcol"], w=["eG"])
        DVE(lambda e: e.tensor_tensor(out=self.beG[:, H4], in0=self.eG[:, H4], in1=self.b_tok[:, H4], op=ALU.mult), r=["eG", "b_tok"], w=["beG"])
        DVE(lambda e: e.tensor_tensor(out=self.kdc[:, H4], in0=self.Glast[:, H4], in1=self.Gcol[:, H4], op=ALU.subtract), r=["Glast", "Gcol"], w=["kdc"])
        ACT(lambda e: e.activation(out=self.kdc[:, H4], in_=self.kdc[:, H4], func=AF.Exp), r=["kdc"], w=["kdc"])
        ACT(lambda e: e.activation(out=self.eGlast[:, H4], in_=self.Glast[:, H4], func=AF.Exp), r=["Glast"], w=["eGlast"])
        DU("g_tok", self.g_tok[:], ["g_tok"]); DU("b_tok", self.b_tok[:], ["b_tok"]); DU("Gcol", self.Gcol[:], ["Gcol"]); DU("Glast", self.Glast[:], ["Glast"])
        DU("dec", self.dec[:].rearrange("p h j -> p (h j)"), ["dec"])
        DU("eGrow", self.eGrow[:].rearrange("p h j -> p (h j)"), ["eGrow"])
        DVE(lambda e: e.tensor_tensor(out=self.nbs[:], in0=self.dec[:], in1=bc_mid(strict[:], 4), op=ALU.mult), r=["dec", C, "dd"], w=["dd"])
        DVE(lambda e: e.tensor_tensor(out=self.nbs[:], in0=self.nbs[:], in1=bc_last(self.nb_tok[:, H4], 128), op=ALU.mult), r=["dd", "nb_tok"], w=["dd"])
        bT = self.bank()
        for i, h in enumerate(hs):
            PE(lambda e, i=i, h=h: e.transpose(out=self.bkb(bT)[:, i * 128:(i + 1) * 128], in_=KT[:, h, cols], identity=self.identb[:]),
               r=[qk_key, C], w=[self.bkey(bT)])
        for i, h in enumerate(hs):
            PE(lambda e, i=i, h=h: e.transpose(out=self.bkb(bT)[:, 512 + i * 128:512 + (i + 1) * 128], in_=vT[:, h, cols], identity=self.identb[:]),
               r=[v_key, C], w=[self.bkey(bT)])
        kt3 = self.bkb(bT)[:, 0:512].rearrange("p (h j) -> p h j", j=128)
        vt3 = self.bkb(bT)[:, 512:1024].rearrange("p (h j) -> p h j", j=128)
        DVE(lambda e: e.tensor_tensor(out=self.Kbg[:], in0=kt3, in1=bc_last(self.beG[:, H4], 128), op=ALU.mult), r=[self.bkey(bT), "beG"], w=["Kbg"])
        DVE(lambda e: e.tensor_tensor(out=self.Kdec[:], in0=kt3, in1=bc_last(self.kdc[:, H4], 128), op=ALU.mult), r=[self.bkey(bT), "kdc"], w=["Kdec"])
        DVE(lambda e: e.tensor_tensor(out=self.Vb[:], in0=vt3, in1=bc_last(self.b_tok[:, H4], 128), op=ALU.mult), r=[self.bkey(bT), "b_tok"], w=["Vb"])
        yield
        bGr = self.bank()
        for i, h in enumerate(hs):
            PE(lambda e, i=i, h=h: e.matmul(self.bk(bGr)[:, i * 128:(i + 1) * 128], lhsT=KT[:, h, cols], rhs=KT[:, h, cols], start=True, stop=True),
               r=[qk_key], w=[self.bkey(bGr)])
        Lm = self.dd
        DVE(lambda e: e.tensor_tensor(out=Lm[:], in0=self.bk(bGr).rearrange("p (h j) -> p h j", j=128), in1=self.nbs[:], op=ALU.mult),
            r=[self.bkey(bGr), "dd"], w=["dd"])
        yield
        DU("N0", Lm[:].rearrange("p h j -> p (h j)"), ["dd"])
        DU("Kbg", self.Kbg[:].rearrange("p h j -> p (h j)"), ["Kbg"], BF16)
        DU("Vb", self.Vb[:].rearrange("p h j -> p (h j)"), ["Vb"], BF16)
        bQK = self.bank()
        for i, h in enumerate(hs):
            PE(lambda e, i=i, h=h: e.matmul(self.bk(bQK)[:, i * 128:(i + 1) * 128], lhsT=QT[:, h, cols], rhs=KT[:, h, cols], start=True, stop=True),
               r=[qk_key], w=[self.bkey(bQK)])
        DVE(lambda e: e.tensor_tensor(out=self.Amat[:], in0=self.bk(bQK).rearrange("p (h j) -> p h j", j=128), in1=self.dec[:], op=ALU.mult),
            r=[self.bkey(bQK), "dec"], w=["Amat"])
        yield
        DVE(lambda e: e.tensor_tensor(out=self.QgT[:], in0=QT[:, H4, cols], in1=self.eGrow[:], op=ALU.mult), r=[qk_key, "eGrow"], w=["QgT"])
        bTT = self.bank()
        for i in range(4):
            PE(lambda e, i=i: e.transpose(out=self.bkb(bTT)[:, i * 128:(i + 1) * 128], in_=self.Amat[:, i, :], identity=self.identb[:]),
               r=["Amat", C], w=[self.bkey(bTT)])
        ACT(lambda e: e.activation(out=self.ATm[:], in_=self.bkb(bTT)[:, 0:512].rearrange("p (h j) -> p h j", j=128), func=AF.Copy),
            r=[self.bkey(bTT)], w=["ATm"])
        LTm = self.Xg
        bLT = self.bank()
        for i in range(4):
            PE(lambda e, i=i: e.transpose(out=self.bk(bLT)[:, i * 128:(i + 1) * 128], in_=Lm[:, i, :], identity=self.ident_f[:]), r=["dd", C], w=[self.bkey(bLT)])
        ACT(lambda e: e.activation(out=LTm[:], in_=self.bk(bLT).rearrange("p (h j) -> p h j", j=128), func=AF.Copy), r=[self.bkey(bLT)], w=["Xg"])
        yield
        rb = 6 + hg
        RK = self.bkey(rb)
        TT = self.eGrow
        Tn = self.dec
        Yb = self.U
        m = self.lvl_masks
        nbase = 4 if not smp else 2
        DVE(lambda e: e.tensor_tensor(out=self.Nm[0][:], in0=Lm[:], in1=bc_mid(m[0][:], 4), op=ALU.mult), r=["dd", C], w=["Nm0"])
        DVE(lambda e: e.tensor_tensor(out=self.NTm[0][:], in0=LTm[:], in1=bc_mid(m[0][:], 4), op=ALU.mult), r=["Xg", C], w=["NTm0"])
        DVE(lambda e: e.tensor_tensor(out=TT[:], in0=self.NTm[0][:], in1=bc_mid(self.ident_f[:], 4), op=ALU.add), r=["NTm0", C, "QgT"], w=["eGrow"])
        for i in range(4):
            PE(lambda e, i=i: e.matmul(self.bk(rb)[:, i * 128:(i + 1) * 128], lhsT=self.ident_f[:], rhs=TT[:, i, :], start=(i == 0), stop=False,
                                       skip_group_check=True), r=["eGrow", C], w=[RK])
        for k in range(1, nbase):
            pN, pNT = self.Nm[(k - 1) % 2], self.NTm[(k - 1) % 2]
            cN, cNT = self.Nm[k % 2], self.NTm[k % 2]
            pNk, pNTk = "Nm%d" % ((k - 1) % 2), "NTm%d" % ((k - 1) % 2)
            cNk, cNTk = "Nm%d" % (k % 2), "NTm%d" % (k % 2)
            bn = self.bank()
            for i in range(4):
                PE(lambda e, i=i, pN=pN, pNT=pNT, bn=bn: e.matmul(self.bk(bn)[:, i * 128:(i + 1) * 128], lhsT=pNT[:, i, :], rhs=pN[:, i, :], start=True, stop=True),
                   r=[pNk, pNTk], w=[self.bkey(bn)])
            DVE(lambda e, cN=cN, bn=bn: e.tensor_copy(out=cN[:], in_=self.bk(bn).rearrange("p (h j) -> p h j", j=128)), r=[self.bkey(bn)], w=[cNk])
            if k < nbase - 1:
                bnt = self.bank()
                for i in range(4):
                    PE(lambda e, i=i, pN=pN, pNT=pNT, bnt=bnt: e.matmul(self.bk(bnt)[:, i * 128:(i + 1) * 128], lhsT=pN[:, i, :], rhs=pNT[:, i, :], start=True, stop=True),
                       r=[pNk, pNTk], w=[self.bkey(bnt)])
                ACT(lambda e, cNT=cNT, bnt=bnt: e.activation(out=cNT[:], in_=self.bk(bnt).rearrange("p (h j) -> p h j", j=128), func=AF.Copy),
                    r=[self.bkey(bnt)], w=[cNTk])
            lastk = (k == nbase - 1)
            for i in range(4):
                PE(lambda e, i=i, cN=cN, lastk=lastk: e.matmul(self.bk(rb)[:, i * 128:(i + 1) * 128], lhsT=cN[:, i, :], rhs=TT[:, i, :], start=False, stop=lastk,
                                                               skip_group_check=True), r=[cNk, "eGrow", RK], w=[RK])
            DVE(lambda e: e.tensor_copy(out=TT[:], in_=self.bk(rb).rearrange("p (h j) -> p h j", j=128)), r=[RK], w=["eGrow"])
            yield
        if not smp:
            bt_ = self.bank()
            for i in range(4):
                PE(lambda e, i=i: e.transpose(out=self.bk(bt_)[:, i * 128:(i + 1) * 128], in_=TT[:, i, :], identity=self.ident_f[:]), r=["eGrow", C], w=[self.bkey(bt_)])
            ACT(lambda e: e.activation(out=Tn[:], in_=self.bk(bt_).rearrange("p (h j) -> p h j", j=128), func=AF.Copy), r=[self.bkey(bt_), "Amat"], w=["dec"])
            yield
            Tn_b = self.Nm[0].bitcast(BF16)[:, :, 0:128]
            TT_b = self.Nm[1].bitcast(BF16)[:, :, 0:128]
            Yb_b = self.Rb
            ACT(lambda e: e.activation(out=Tn_b, in_=Tn[:], func=AF.Copy), r=["dec"], w=["Nm0"])
            ACT(lambda e: e.activation(out=TT_b, in_=TT[:], func=AF.Copy), r=["eGrow"], w=["Nm1"])
            for l in range(1, 4):
                BlT = self.NTm[l % 2].bitcast(BF16)[:, :, 0:128]
                BlTk = "NTm%d" % (l % 2)
                DVE(lambda e, l=l, BlT=BlT: e.tensor_tensor(out=BlT, in0=LTm[:], in1=bc_mid(m[l][:], 4), op=ALU.mult), r=["Xg", C], w=[BlTk])
                by = self.bank()
                for i in range(4):
                    PE(lambda e, i=i, BlT=BlT, by=by: e.matmul(self.bk(by)[:, i * 128:(i + 1) * 128], lhsT=BlT[:, i, :], rhs=Tn_b[:, i, :], start=True, stop=True),
                       r=[BlTk, "Nm0"], w=[self.bkey(by)])
                ACT(lambda e, by=by: e.activation(out=Yb_b[:], in_=self.bk(by).rearrange("p (h j) -> p h j", j=128), func=AF.Copy), r=[self.bkey(by)], w=["Rb"])
                yield
                bzt = self.bank()
                for i in range(4):
                    PE(lambda e, i=i, bzt=bzt: e.matmul(self.bk(bzt)[:, i * 128:(i + 1) * 128], lhsT=Yb_b[:, i, :], rhs=TT_b[:, i, :], start=True, stop=True),
                       r=["Rb", "Nm1"], w=[self.bkey(bzt)])
                if l < 3:
                    bz = self.bank()
                    for i in range(4):
                        PE(lambda e, i=i, bz=bz: e.matmul(self.bk(bz)[:, i * 128:(i + 1) * 128], lhsT=TT_b[:, i, :], rhs=Yb_b[:, i, :], start=True, stop=True),
                           r=["Rb", "Nm1"], w=[self.bkey(bz)])
                    DVE(lambda e, bz=bz: e.tensor_tensor(out=Tn[:], in0=Tn[:], in1=self.bk(bz).rearrange("p (h j) -> p h j", j=128), op=ALU.add),
                        r=["dec", self.bkey(bz)], w=["dec"])
                    ACT(lambda e: e.activation(out=Tn_b, in_=Tn[:], func=AF.Copy), r=["dec"], w=["Nm0"])
                DVE(lambda e, bzt=bzt: e.tensor_tensor(out=TT[:], in0=TT[:], in1=self.bk(bzt).rearrange("p (h j) -> p h j", j=128), op=ALU.add),
                    r=["eGrow", self.bkey(bzt)], w=["eGrow"])
                if l < 3:
                    ACT(lambda e: e.activation(out=TT_b, in_=TT[:], func=AF.Copy), r=["eGrow"], w=["Nm1"])
                yield
        ACT(lambda e: e.activation(out=self.Rb[:], in_=TT[:], func=AF.Copy), r=["eGrow"], w=["Rb"])
        DU("TT", self.Rb[:].rearrange("p h j -> p (h j)"), ["Rb"], BF16)
        DU("AT", self.ATm[:].rearrange("p h j -> p (h j)"), ["ATm"], BF16)
        bU = self.bank()
        for i in range(4):
            PE(lambda e, i=i: e.matmul(self.bk(bU)[:, i * 128:(i + 1) * 128], lhsT=self.Rb[:, i, :], rhs=self.Vb[:, i, :], start=True, stop=True),
               r=["Rb", "Vb"], w=[self.bkey(bU)])
        ACT(lambda e: e.activation(out=self.U[:], in_=self.bk(bU).rearrange("p (h j) -> p h j", j=128), func=AF.Copy), r=[self.bkey(bU)], w=["U"])
        bW = self.bank()
        for i in range(4):
            PE(lambda e, i=i: e.matmul(self.bk(bW)[:, i * 128:(i + 1) * 128], lhsT=self.Kbg[:, i, :], rhs=self.Rb[:, i, :], start=True, stop=True),
               r=["Rb", "Kbg"], w=[self.bkey(bW)])
        ACT(lambda e: e.activation(out=self.WT[:], in_=self.bk(bW).rearrange("p (h j) -> p h j", j=128), func=AF.Copy), r=[self.bkey(bW)], w=["WT"])
        yield
        DU("U", self.U[:].rearrange("p h j -> p (h j)"), ["U"])
        DU("WT", self.WT[:].rearrange("p h j -> p (h j)"), ["WT"], BF16)
        DU("QgT", self.QgT[:].rearrange("p h j -> p (h j)"), ["QgT"], BF16)
        if smp:
            return
        bS = self.bank()
        for i, h in enumerate(hs):
            PE(lambda e, i=i, h=h: e.matmul(self.bk(bS)[:, i * 128:(i + 1) * 128], lhsT=self.WT[:, i, :], rhs=self.Sb[:, h, :], start=True, stop=True),
               r=["WT", "Sb"], w=[self.bkey(bS)])
        DVE(lambda e: e.tensor_tensor(out=self.Vnew[:], in0=self.U[:], in1=self.bk(bS).rearrange("p (h j) -> p h j", j=128), op=ALU.subtract),
            r=["U", self.bkey(bS)], w=["Vnew"])
        yield
        bO = self.bank()
        for i, h in enumerate(hs):
            PE(lambda e, i=i, h=h: e.matmul(self.bk(bO)[:, i * 128:(i + 1) * 128], lhsT=self.Sb[:, h, :], rhs=self.QgT[:, i, :], start=True, stop=False),
               r=["Sb", "QgT"], w=[self.bkey(bO)])
            PE(lambda e, i=i, h=h: e.matmul(self.bk(bO)[:, i * 128:(i + 1) * 128], lhsT=self.Vnew[:, i, :], rhs=self.ATm[:, i, :], start=False, stop=True),
               r=["Vnew", "ATm"], w=[self.bkey(bO)])
        ACT(lambda e: e.activation(out=self.oT[:, H4, out_cols], in_=self.bk(bO).rearrange("p (h j) -> p h j", j=128), func=AF.Copy),
            r=[self.bkey(bO)], w=["yF"])
        yield
        bD = self.bank()
        for i, h in enumerate(hs):
            PE(lambda e, i=i, h=h: e.matmul(self.bk(bD)[:, i * 128:(i + 1) * 128], lhsT=self.Kdec[:, i, :], rhs=self.Vnew[:, i, :], start=True, stop=True),
               r=["Kdec", "Vnew"], w=[self.bkey(bD)])
        DVE(lambda e: e.tensor_tensor(out=self.S[:, H4, :], in0=self.S[:, H4, :], in1=bc_last(self.eGlast[:, H4], 128), op=ALU.mult), r=["S", "eGlast"], w=["S"])
        DVE(lambda e: e.tensor_tensor(out=self.S[:, H4, :], in0=self.S[:, H4, :], in1=self.bk(bD).rearrange("p (h j) -> p h j", j=128), op=ALU.add),
            r=["S", self.bkey(bD)], w=["S"])
        ACT(lambda e: e.activation(out=self.Sb[:, H4, :], in_=self.S[:, H4, :], func=AF.Copy), r=["S"], w=["Sb"])
        DU("Vnew", self.Vnew[:].rearrange("p h j -> p (h j)"), ["Vnew"], BF16)
        DU("S1", self.S[:, H4, :].rearrange("p h j -> p (h j)"), ["S"])
        DU("oT1", self.oT[:, hg * 4, out_cols], ["yF"])

    def rope_apply(self, src3, nh, cosb, sinb, dst3, n, rk, wk, extra_dst=None):
        t0, t1, t2, t3 = [t[0:n, 0:nh, :] for t in self.rt]
        x1 = src3[:, :, 0:8]
        x2 = src3[:, :, 8:16]
        cb = bc_mid(cosb, nh)
        sb_ = bc_mid(sinb, nh)
        D_ = self.dve
        K = ["rt"]
        D_(lambda e: e.tensor_tensor(out=t0, in0=x1, in1=cb, op=ALU.mult), r=rk + ["rope"], w=K)
        D_(lambda e: e.tensor_tensor(out=t1, in0=x2, in1=sb_, op=ALU.mult), r=rk + ["rope"], w=K)
        D_(lambda e: e.tensor_tensor(out=t2, in0=x2, in1=cb, op=ALU.mult), r=rk + ["rope"], w=K)
        D_(lambda e: e.tensor_tensor(out=t3, in0=x1, in1=sb_, op=ALU.mult), r=rk + ["rope"], w=K)
        for dst in ([dst3] + ([extra_dst] if extra_dst is not None else [])):
            D_(lambda e, dst=dst: e.tensor_tensor(out=dst[:, :, 0:8], in0=t0, in1=t1, op=ALU.subtract), r=K, w=wk)
            D_(lambda e, dst=dst: e.tensor_tensor(out=dst[:, :, 8:16], in0=t2, in1=t3, op=ALU.add), r=K, w=wk)

    def swa_unit_prompt(self, it, u, wq0, wq1, wkv, cols, blk, cur, prv, last, sq=None):
        PE, DVE, ACT = self.pe, self.dve, self.act
        C = "const"
        smp = sq is not None
        if smp:
            KTp_, KTpk_ = self.KTs[prv], "KTs%d" % prv
            self.dma("sp", self.kf32[:].rearrange("p h d -> p (h d)"), self.ck[sq], w=["kf32"], sem="ckl")
            for dpl in range(2):
                DVE(lambda e, dpl=dpl: e.tensor_copy(out=self.k_pad[:, :, dpl, dpl * 64:(dpl + 1) * 64], in_=self.kf32[:]), r=["kf32"], w=["k_pad"])
            btc = self.bank()
            for gi in range(8):
                PE(lambda e, gi=gi, btc=btc: e.transpose(out=self.bkb(btc)[:, gi * 128:(gi + 1) * 128], in_=self.k_pad[:, gi // 2, gi % 2, :], identity=self.identb[:]),
                   r=["k_pad", C], w=[self.bkey(btc)])
            ACT(lambda e, btc=btc: e.activation(out=KTp_[:], in_=self.bkb(btc).rearrange("p (c q) -> p c q", q=128), func=AF.Copy), r=[self.bkey(btc)], w=[KTpk_])
            self.dma("sp", self.vf32.rearrange("p h d -> p (h d)"), self.cv[sq], w=["osw"], sem="cvl")
            DVE(lambda e: e.tensor_copy(out=self.Vs[prv][:], in_=self.vf32.rearrange("p h d -> p (h d)")), r=["osw"], w=["Vs%d" % prv])
            self.dma("sp", self.ks_o[sq, 0:124, :], self.ck[sq, 4:128, :], sem="kso", output=True)
            self.dma("sp", self.vs_o[sq, 0:124, :], self.cv[sq, 4:128, :], sem="vso", output=True)
            yield
        bq = [self.bank(), self.bank()]
        for half, (wt, wk) in enumerate([wq0, wq1]):
            for kc in range(8):
                PE(lambda e, kc=kc, half=half, wt=wt: e.matmul(self.bk(bq[half])[:, :], lhsT=self.hT[:, kc, cols], rhs=wt[:, kc, :], start=(kc == 0), stop=(kc == 7)),
                   r=["hT", wk], w=[self.bkey(bq[half])])
        bkv = self.bank()
        for kc in range(8):
            PE(lambda e, kc=kc: e.matmul(self.bk(bkv)[:, :], lhsT=self.hT[:, kc, cols], rhs=wkv[0][:, kc, :], start=(kc == 0), stop=(kc == 7)),
               r=["hT", wkv[1]], w=[self.bkey(bkv)])
        if SUB < 3:
            return
        cosb = self.cos_t[:, blk, :] if not smp else self.cos_s[:, 0, :]
        sinb = self.sin_t[:, blk, :] if not smp else self.sin_s[:, 0, :]
        for half in range(2):
            ACT(lambda e, half=half: e.activation(out=self.q_tok[:, half * 8:(half + 1) * 8, :], in_=self.bk(bq[half]).rearrange("p (h d) -> p h d", d=64), func=AF.Copy),
                r=[self.bkey(bq[half])], w=["q_tok"])
            self.rope_apply(self.bk(bq[half]).rearrange("p (h d) -> p h d", d=64), 8, cosb, sinb, self.q_tok[:, half * 8:(half + 1) * 8, :], 128,
                            [self.bkey(bq[half])], ["q_tok"])
        if SUB < 3.3:
            return
        k3 = self.bk(bkv)[:, 0:256].rearrange("p (h d) -> p h d", d=64)
        ACT(lambda e: e.activation(out=self.kf32[:], in_=k3, func=AF.Copy), r=[self.bkey(bkv)], w=["kf32"])
        self.rope_apply(k3, 4, cosb, sinb, self.kf32[:], 128, [self.bkey(bkv)], ["kf32"])
        for dpl in range(2):
            DVE(lambda e, dpl=dpl: e.tensor_copy(out=self.k_pad[:, :, dpl, dpl * 64:(dpl + 1) * 64], in_=self.kf32[:]), r=["kf32"], w=["k_pad"])
        if SUB < 3.6:
            return
        Vc, Vck = self.Vs[cur], "Vs%d" % cur
        Vp, Vpk = self.Vs[prv], "Vs%d" % prv
        ACT(lambda e: e.activation(out=Vc[:], in_=self.bk(bkv)[:, 256:512], func=AF.Copy), r=[self.bkey(bkv)], w=[Vck])
        if SUB < 3.8:
            return
        if smp:
            DVE(lambda e: e.tensor_copy(out=self.vf32, in_=self.bk(bkv)[:, 256:512].rearrange("p (h d) -> p h d", d=64)), r=[self.bkey(bkv)], w=["osw"])
            self.dma("sp", self.ks_o[sq, 124:128, :], self.kf32[0:4].rearrange("p h d -> p (h d)"), r=["kf32"], sem="kso", output=True)
            self.dma("sp", self.vs_o[sq, 124:128, :], self.vf32[0:4].rearrange("p h d -> p (h d)"), r=["osw"], sem="vso", output=True)
        if last:
            DVE(lambda e: e.tensor_copy(out=self.vf32, in_=self.bk(bkv)[:, 256:512].rearrange("p (h d) -> p h d", d=64)), r=[self.bkey(bkv)], w=["osw"])
            self.dma("sp", self.kp_o[:, :], self.kf32[:].rearrange("p h d -> p (h d)"), r=["kf32"], sem="kvo", output=True)
            self.dma("sp", self.vp_o[:, :], self.vf32.rearrange("p h d -> p (h d)"), r=["osw"], sem="kvo", output=True)
        if SUB < 4:
            return
        yield
        bt = self.bank()
        for c in range(8):
            PE(lambda e, c=c: e.transpose(out=self.bkb(bt)[:, c * 128:(c + 1) * 128], in_=self.q_tok[:, 2 * c:2 * c + 2, :].rearrange("p a d -> p (a d)"),
                                          identity=self.identb[:]), r=["q_tok", C], w=[self.bkey(bt)])
        ACT(lambda e: e.activation(out=self.QTs[:], in_=self.bkb(bt).rearrange("p (c q) -> p c q", q=128), func=AF.Copy), r=[self.bkey(bt)], w=["QTs"])
        KTc, KTck = self.KTs[cur], "KTs%d" % cur
        KTp, KTpk = self.KTs[prv], "KTs%d" % prv
        bt2 = self.bank()
        for gi in range(8):
            PE(lambda e, gi=gi: e.transpose(out=self.bkb(bt2)[:, gi * 128:(gi + 1) * 128], in_=self.k_pad[:, gi // 2, gi % 2, :], identity=self.identb[:]),
               r=["k_pad", C], w=[self.bkey(bt2)])
        ACT(lambda e: e.activation(out=KTc[:], in_=self.bkb(bt2).rearrange("p (c q) -> p c q", q=128), func=AF.Copy), r=[self.bkey(bt2)], w=[KTck])
        yield
        if SUB < 5:
            return
        mrow = self.mrow0 if (blk == 0 and not smp) else self.mrow
        bo = [6, 7]
        for g in range(4):
            DVE(lambda e: e.memset(self.dacc[:], 0.0), w=["dacc"])
            bs = [self.bank(), self.bank()]
            for r_ in range(4):
                h = 4 * g + r_
                sb_ = bs[r_ // 2]
                o0 = (r_ % 2) * 256
                lq = self.QTs[:, h // 2, :]
                kidx = g * 2 + (h % 2)
                PE(lambda e, lq=lq, sb_=sb_, o0=o0, kidx=kidx: e.matmul(self.bk(sb_)[:, o0:o0 + 128], lhsT=lq, rhs=KTp[:, kidx, :], start=True, stop=True),
                   r=["QTs", KTpk], w=[self.bkey(sb_)])
                PE(lambda e, lq=lq, sb_=sb_, o0=o0, kidx=kidx: e.matmul(self.bk(sb_)[:, o0 + 128:o0 + 256], lhsT=lq, rhs=KTc[:, kidx, :], start=True, stop=True),
                   r=["QTs", KTck], w=[self.bkey(sb_)])
            scs = []
            for hp in range(2):
                t = self.tmpc[hp]
                DVE(lambda e, hp=hp, t=t, bs=bs: e.tensor_tensor(out=t[:, :].rearrange("p (a s) -> p a s", s=256), in0=self.bk(bs[hp]).rearrange("p (a s) -> p a s", s=256),
                                                          in1=bc_mid(mrow[:], 2), op=ALU.add), r=[self.bkey(bs[hp]), C], w=["tmpc%d" % hp])
                scs.append(t)
            yield
            if SUB < 6:
                continue
            if blk == 1 and g == 1 and "g1_sc0" in DEBUG:
                self.dump("g1_sc0", self.tmpc[0][:, :], ["tmpc0"])
                self.dump("g1_sc1", self.tmpc[1][:, :], ["tmpc1"])
            for hp in range(2):
                DVE(lambda e, hp=hp: e.tensor_reduce(out=self.mx[:, hp * 2:hp * 2 + 2], in_=self.tmpc[hp][:, :].rearrange("p (a s) -> p a s", s=256), axis=AX.X, op=ALU.max),
                    r=["tmpc%d" % hp], w=["mx"])
            DVE(lambda e, g=g: e.scalar_tensor_tensor(out=self.nm[:], in0=self.mx[:], scalar=0.125, in1=self.sinks_bc[:, 4 * g:4 * g + 4], op0=ALU.mult, op1=ALU.max),
                r=["mx", "sinks_bc"], w=["nm"])
            DVE(lambda e: e.tensor_scalar_mul(out=self.nm[:], in0=self.nm[:], scalar1=-1.0), r=["nm"], w=["nm"])
            for r_ in range(4):
                sb_ = bs[r_ // 2]
                o0 = (r_ % 2) * 256
                ACT(lambda e, r_=r_, o0=o0, g=g: e.activation(out=self.Eb[:, r_, :], in_=self.tmpc[r_ // 2][:, o0:o0 + 256], func=AF.Exp, bias=self.nm[:, r_:r_ + 1],
                                                                scale=0.125, accum_out=self.dacc[:, r_, 0:1]),
                    r=["tmpc%d" % (r_ // 2), "nm"], w=["Eb", "dacc"])
            DVE(lambda e, g=g: e.tensor_tensor(out=self.esk[:], in0=self.sinks_bc[:, 4 * g:4 * g + 4], in1=self.nm[:], op=ALU.add), r=["sinks_bc", "nm"], w=["esk"])
            ACT(lambda e: e.activation(out=self.esk[:], in_=self.esk[:], func=AF.Exp), r=["esk"], w=["esk"])
            DVE(lambda e, g=g: e.tensor_tensor(out=self.den[:, 4 * g:4 * g + 4], in0=self.dacc[:, :, 0], in1=self.esk[:], op=ALU.add), r=["dacc", "esk"], w=["den"])
            if SUB < 7:
                continue
            if blk == 1 and g == 0:
                self.dump("den_g0", self.den[:], ["den"])
                self.dump("nm_g0", self.nm[:], ["nm"])
                self.dump("mx_g0", self.mx[:], ["mx"])
            be = self.bank()
            for r_ in range(4):
                for kb in range(2):
                    PE(lambda e, r_=r_, kb=kb, be=be: e.transpose(out=self.bkb(be)[:, r_ * 256 + kb * 128:r_ * 256 + (kb + 1) * 128], in_=self.Eb[:, r_, kb * 128:(kb + 1) * 128],
                                                           identity=self.identb[:]), r=["Eb", C], w=[self.bkey(be)])
            DVE(lambda e, be=be: e.tensor_copy(out=self.ETb[:], in_=self.bkb(be).rearrange("p (r s) -> p r s", s=256)), r=[self.bkey(be)], w=["ETb"])
            yield
            ob = self.bank()
            for r_ in range(4):
                oc = r_ * 64
                PE(lambda e, r_=r_, ob=ob, oc=oc, g=g: e.matmul(self.bk(ob)[:, oc:oc + 64], lhsT=self.ETb[:, r_, 0:128], rhs=Vp[:, g * 64:(g + 1) * 64], start=True, stop=False),
                   r=["ETb", Vpk], w=[self.bkey(ob)])
                PE(lambda e, r_=r_, ob=ob, oc=oc, g=g: e.matmul(self.bk(ob)[:, oc:oc + 64], lhsT=self.ETb[:, r_, 128:256], rhs=Vc[:, g * 64:(g + 1) * 64], start=False, stop=True),
                   r=["ETb", Vck], w=[self.bkey(ob)])
            if blk == 1 and g == 1 and "g1_O" in DEBUG:
                self.dump("g1_Eb", self.Eb[:].rearrange("p h d -> p (h d)"), ["Eb"], BF16)
                self.dump("g1_ETb", self.ETb[:].rearrange("p h d -> p (h d)"), ["ETb"], BF16)
                self.dump("g1_den", self.den[:], ["den"])
                DVE(lambda e, ob=ob: e.tensor_copy(out=self.tmpc[0][:, 0:256], in_=self.bk(ob)[:, 0:256]), r=[self.bkey(ob)], w=["tmpc0"])
                self.dump("g1_O", self.tmpc[0][:, 0:256], ["tmpc0"])
            DVE(lambda e, g=g: e.reciprocal(out=self.rden[:, 4 * g:4 * g + 4], in_=self.den[:, 4 * g:4 * g + 4]), r=["den"], w=["rden"])
            DVE(lambda e, g=g, ob=ob: e.tensor_tensor(out=self.osw[:, 4 * g:4 * g + 4, :], in0=self.bk(ob)[:, 0:256].rearrange("p (h d) -> p h d", d=64),
                                                      in1=bc_last(self.rden[:, 4 * g:4 * g + 4], 64), op=ALU.mult),
                r=[self.bkey(ob), "rden"], w=["osw"])
            yield
        if blk == 1:
            self.dump("osw", self.osw[:].rearrange("p h d -> p (h d)"), ["osw"], BF16)
        bt3 = self.bank()
        for c in range(8):
            PE(lambda e, c=c: e.transpose(out=self.bkb(bt3)[:, c * 128:(c + 1) * 128], in_=self.osw[:, 2 * c:2 * c + 2, :].rearrange("p a d -> p (a d)"),
                                          identity=self.identb[:]), r=["osw", C], w=[self.bkey(bt3)])
        ACT(lambda e: e.activation(out=self.ysw[:, :, cols], in_=self.bkb(bt3).rearrange("p (c q) -> p c q", q=128), func=AF.Copy), r=[self.bkey(bt3)], w=["ysw"])

    def sample_tile(self):
        for j in range(NS // 4):
            self.sample_ptile(j)

    def sample_ptile(self, j):
        N = NT
        seqs = [(u * 128, (u + 1) * 128, 1 + 4 * j + u) for u in range(4)]
        self.smp_j = j
        stg = self.yF[:, :, :].rearrange("p c n -> p (c n)")
        self.dma("sp", stg[0:12, 0:3072], self.sconv[4 * j:4 * j + 4].rearrange("s t c -> (s t) c"), w=["yF"], sem="scv")
        for cc in range(24):
            b = self.bank()
            self.pe(lambda e, cc=cc, b=b: e.transpose(out=self.bk(b)[:, 0:12], in_=stg[0:12, cc * 128:(cc + 1) * 128], identity=self.ident_f[0:12, 0:12]),
                    r=["yF", "const"], w=[self.bkey(b)])
            self.dve(lambda e, cc=cc, b=b: e.tensor_copy(out=self.hist4[:, cc, :], in_=self.bk(b)[:, 0:12]), r=[self.bkey(b)], w=["hist4"])
        self.dve(lambda e: e.memset(self.xtok[:], 0.0), w=["xtok"])
        for u in range(4):
            sq = 4 * j + u
            self.dma("sp", self.xtok[0:4, u, :], self.xs[sq * 4:(sq + 1) * 4, :], w=["xtok"], sem="xin")
        self.load_x(None, 4, 128)
        self.prenorm(0, N, seqs)
        self.ffn("wg1", "wu1", "wd1", N)
        self.postnorm(0, N, seqs)
        self.prenorm(1, N, seqs)
        self.mixer_prompt(-1 - j)
        self.postnorm(1, N, seqs)
        self.prenorm(2, N, seqs)
        self.ffn("wg2", "wu2", "wd2", N)
        self.postnorm(2, N, seqs)
        self.store_x(None, 4, 128)
        for u in range(4):
            sq = 4 * j + u
            self.dma("sp", self.ys[sq * 4:(sq + 1) * 4, :], self.xtok[0:4, u, :], r=["xtok"], sem="xout", output=True)


_NC_CACHE = {}


STAGE = 99
TRACE = False
DEBUG = set()
LAST = None
SUB = 99


def _get_nc(NPT, sample):
    key = (NPT, sample)
    if key not in _NC_CACHE:
        kb = KB(NPT=NPT, sample=sample, stage=STAGE)
        nc = kb.build()
        _NC_CACHE[key] = nc
    return _NC_CACHE[key]


def _weights(inputs):
    m = {
        "w_ada": "w_ada", "b_ada": "b_ada", "n1pre": "ffn1_norm_pre", "n1post": "ffn1_norm_post",
        "wg1": "ffn1_w_gate", "wu1": "ffn1_w_up", "wd1": "ffn1_w_down", "n2pre": "mix_norm_pre", "n2post": "mix_norm_post",
        "w_in": "w_in", "conv_w": "conv_w", "a_log": "a_log", "dt_bias": "dt_bias", "dn_norm": "dn_norm", "sinks": "sinks",
        "w_out": "w_out", "n3pre": "ffn2_norm_pre", "n3post": "ffn2_norm_post", "wg2": "ffn2_w_gate", "wu2": "ffn2_w_up",
        "wd2": "ffn2_w_down",
    }
    return {k: np.ascontiguousarray(np.asarray(inputs[v], dtype=np.float32)[0]) for k, v in m.items()}


def kernel(**inputs):
    x_prompt = np.asarray(inputs["x_prompt"], dtype=np.float32)
    B, T, _ = x_prompt.shape
    NPT = T // NT
    sample = "x_sample" in inputs and inputs["x_sample"] is not None and not inputs.get("_no_sample", False)
    nc = _get_nc(NPT, sample)
    Wd = _weights(inputs)
    ncores = 8
    in_maps = []
    for c in range(ncores):
        b = c % B
        m = dict(Wd)
        m["xp"] = np.ascontiguousarray(x_prompt[b])
        m["cp"] = np.ascontiguousarray(np.asarray(inputs["c_prompt"], dtype=np.float32)[b:b + 1])
        if sample:
            sl = slice(c * NS, (c + 1) * NS)
            m["xs"] = np.ascontiguousarray(np.asarray(inputs["x_sample"], dtype=np.float32)[sl].reshape(NS * TS, D))
            m["cs"] = np.ascontiguousarray(np.asarray(inputs["c_sample"], dtype=np.float32)[sl])
            m["ck"] = np.ascontiguousarray(np.asarray(inputs["cache_swa_k"], dtype=np.float32)[0, sl].reshape(NS, 128, 256))
            m["cv"] = np.ascontiguousarray(np.asarray(inputs["cache_swa_v"], dtype=np.float32)[0, sl].reshape(NS, 128, 256))
            m["sconv"] = np.ascontiguousarray(np.asarray(inputs["state_conv"], dtype=np.float32)[0, sl])
            m["sdelta"] = np.ascontiguousarray(np.asarray(inputs["state_delta"], dtype=np.float32)[0, sl])
        in_maps.append(m)
    if TRACE:
        res = run_bass_kernel_spmd(nc, in_maps, core_ids=list(range(ncores)), trace=True)
        print("EXEC_TIME_NS", res.exec_time_ns, flush=True)
    else:
        res = run_bass_kernel_spmd(nc, in_maps, core_ids=list(range(ncores)))
    R = res.results
    global LAST
    LAST = R
    y_p = np.stack([R[b]["yp"] for b in range(B)])
    kp = np.stack([R[b]["kp"].reshape(128, 4, 64) for b in range(B)])[None]
    vp = np.stack([R[b]["vp"].reshape(128, 4, 64) for b in range(B)])[None]
    convp = np.stack([R[b]["convp"] for b in range(B)])[None]
    deltap = np.stack([R[b]["deltap"] for b in range(B)])[None]
    if not sample:
        return (y_p, kp, vp, convp, deltap)
    y_s = np.concatenate([R[c]["ys"].reshape(NS, TS, D) for c in range(ncores)])
    ks = np.concatenate([R[c]["ks"].reshape(NS, 128, 4, 64) for c in range(ncores)])[None]
    vs = np.concatenate([R[c]["vs"].reshape(NS, 128, 4, 64) for c in range(ncores)])[None]
    convs = np.concatenate([R[c]["convs"] for c in range(ncores)])[None]
    deltas = np.concatenate([R[c]["deltas"] for c in range(ncores)])[None]
    return (y_p, y_s, kp, vp, convp, deltap, ks, vs, convs, deltas)
```
